# Optimizing a Trainium2 kernel written in Bass

```python
import jax, jax.numpy as jnp
from jax import lax
import numpy as np

D_MODEL = 2048
BATCH = 1
SEQ = 8192
DEPTH = 1
DEC_BATCH = 4
DEC_SEQ = 2048
PAST_LEN = 128

MLA_HEADS = D_MODEL // 128
Q_LORA = 768
KV_LORA = 512
NOPE_DIM = 128
ROPE_DIM = 64
V_DIM = 128
QK_DIM = NOPE_DIM + ROPE_DIM
ROPE_THETA = 10000.0
Q_BLOCK = 128
D_MLA_V = MLA_HEADS * V_DIM

RWKV_HEAD = 64
D_RWKV = D_MODEL
RWKV_HEADS = D_RWKV // RWKV_HEAD
DECAY_LORA = 96
AAA_LORA = 96
GATE_LORA = 256

D_FF = 4 * D_MODEL
NORM_EPS = 1e-6
GN_EPS = 64e-5

MLA_IN = Q_LORA + KV_LORA + ROPE_DIM
RWKV_IN = 3 * D_RWKV + 2 * DECAY_LORA + 2 * AAA_LORA + GATE_LORA
N_IN = MLA_IN + RWKV_IN + 2 * D_MODEL
D_MIX = D_MLA_V + D_RWKV

kernel_name = 'hybrid_mla_rwkv7_gated_encoder'


def _rmsnorm(x, g):
    xf = x.astype(jnp.float32)
    y = xf * lax.rsqrt(jnp.mean(xf * xf, axis=-1, keepdims=True) + NORM_EPS)
    return (y * g.astype(jnp.float32)).astype(x.dtype)


def _rope_tables(seq):
    inv = 1.0 / (ROPE_THETA ** (jnp.arange(0, ROPE_DIM, 2, dtype=jnp.float32) / ROPE_DIM))
    ang = jnp.arange(seq, dtype=jnp.float32)[:, None] * inv[None, :]
    return jnp.cos(ang), jnp.sin(ang)


def _apply_rope(x, cos, sin):
    xf = x.astype(jnp.float32)
    x1, x2 = jnp.split(xf, 2, axis=-1)
    return jnp.concatenate([x1 * cos - x2 * sin, x1 * sin + x2 * cos], axis=-1).astype(x.dtype)


def _centred_shift(u, mu_prev, mu_next):
    prev = jnp.pad(u[:, :-1], ((0, 0), (1, 0), (0, 0)))
    nxt = jnp.pad(u[:, 1:], ((0, 0), (0, 1), (0, 0)))
    return u + mu_prev * (prev - u) + mu_next * (nxt - u)


def _mla(c_q, c_kv, k_rope, q_norm, w_uq, kv_norm, w_ukv):
    b, s, _ = c_q.shape
    q = (_rmsnorm(c_q, q_norm) @ w_uq).reshape(b, s, MLA_HEADS, QK_DIM)
    kv = (_rmsnorm(c_kv, kv_norm) @ w_ukv).reshape(b, s, MLA_HEADS, NOPE_DIM + V_DIM)
    q_nope, q_rope = q[..., :NOPE_DIM], q[..., NOPE_DIM:]
    k_nope, v = kv[..., :NOPE_DIM], kv[..., NOPE_DIM:]
    cos, sin = _rope_tables(s)
    q_rope = _apply_rope(q_rope, cos[None, :, None, :], sin[None, :, None, :])
    k_rope = _apply_rope(k_rope, cos[None], sin[None])
    scale = QK_DIM ** -0.5

    def attend_block(i):
        qn = lax.dynamic_slice_in_dim(q_nope, i * Q_BLOCK, Q_BLOCK, axis=1)
        qr = lax.dynamic_slice_in_dim(q_rope, i * Q_BLOCK, Q_BLOCK, axis=1)
        sc = (jnp.einsum('bqhd,bkhd->bhqk', qn, k_nope, preferred_element_type=jnp.float32)
              + jnp.einsum('bqhr,bkr->bhqk', qr, k_rope, preferred_element_type=jnp.float32))
        p = jax.nn.softmax(sc * scale, axis=-1).astype(v.dtype)
        return jnp.einsum('bhqk,bkhd->bqhd', p, v)

    out = lax.map(attend_block, jnp.arange(s // Q_BLOCK))
    return jnp.transpose(out, (1, 0, 2, 3, 4)).reshape(b, s, D_MLA_V)


def _wkv_scan(r, w, k, v, aa, bb, reverse):
    b = r.shape[0]

    def step(state, inp):
        r_t, w_t, k_t, v_t, a_t, b_t = inp
        sa = jnp.einsum('bhvk,bhk->bhv', state, a_t)
        state = (state * w_t[:, :, None, :] + sa[..., None] * b_t[:, :, None, :]
                 + v_t[..., None] * k_t[:, :, None, :])
        return state, jnp.einsum('bhvk,bhk->bhv', state, r_t)

    xs = tuple(jnp.moveaxis(t, 1, 0) for t in (r, w, k, v, aa, bb))
    s0 = jnp.zeros((b, RWKV_HEADS, RWKV_HEAD, RWKV_HEAD), jnp.float32)
    _, ys = lax.scan(step, s0, xs, reverse=reverse)
    return jnp.moveaxis(ys, 0, 1)


def _rwkv7(u, w0_f, w2_f, w0_b, w2_b, a0_f, a2_f, a0_b, a2_b, g2, k_k, k_a, r_k, ln_w, ln_b):
    dt = u.dtype
    b, s, _ = u.shape
    f = lambda t: t.astype(jnp.float32)
    u = f(u)
    cuts = np.cumsum([D_RWKV, D_RWKV, D_RWKV, DECAY_LORA, DECAY_LORA, AAA_LORA, AAA_LORA]).tolist()
    r, k, v, xw_f, xw_b, xa_f, xa_b, xg = jnp.split(u, cuts, axis=-1)
    heads = lambda t: t.reshape(b, s, RWKV_HEADS, RWKV_HEAD)

    def decay(xw, w0, w2):
        logw = -jax.nn.softplus(-(f(w0) + jnp.tanh(xw) @ f(w2))) - 0.5
        return heads(jnp.exp(-jnp.exp(logw)))

    a_f = jax.nn.sigmoid(f(a0_f) + xa_f @ f(a2_f))
    a_b = jax.nn.sigmoid(f(a0_b) + xa_b @ f(a2_b))
    g = jax.nn.sigmoid(xg) @ f(g2)
    kk = heads(k * f(k_k))
    kk = kk / jnp.maximum(jnp.linalg.norm(kk, axis=-1, keepdims=True), 1e-12)
    k_f = heads(k * (1.0 + (a_f - 1.0) * f(k_a)))
    k_b = heads(k * (1.0 + (a_b - 1.0) * f(k_a)))
    rh, vh = heads(r), heads(v)
    y = (_wkv_scan(rh, decay(xw_f, w0_f, w2_f), k_f, vh, -kk, kk * heads(a_f), False)
         + _wkv_scan(rh, decay(xw_b, w0_b, w2_b), k_b, vh, -kk, kk * heads(a_b), True))
    mean = jnp.mean(y, axis=-1, keepdims=True)
    var = jnp.mean(jnp.square(y - mean), axis=-1, keepdims=True)
    y = ((y - mean) * lax.rsqrt(var + GN_EPS)).reshape(b, s, D_RWKV) * f(ln_w) + f(ln_b)
    bonus = (jnp.sum(rh * (k_f + k_b) * f(r_k), axis=-1, keepdims=True) * vh).reshape(b, s, D_RWKV)
    return ((y + bonus) * g).astype(dt)


def _block(x, norm_pre_mix, w_in, mu_prev, mu_next, mla_q_norm, mla_w_uq, mla_kv_norm, mla_w_ukv,
           rwkv_w0_f, rwkv_w2_f, rwkv_w0_b, rwkv_w2_b, rwkv_a0_f, rwkv_a2_f, rwkv_a0_b, rwkv_a2_b,
           rwkv_g2, rwkv_k_k, rwkv_k_a, rwkv_r_k, rwkv_ln_w, rwkv_ln_b, w_branch, w_out,
           norm_post_mix, norm_pre_mlp, w_mlp_up, w_mlp_down, norm_post_mlp):
    h = _rmsnorm(x, norm_pre_mix)
    proj = h @ w_in
    c_q = proj[..., :Q_LORA]
    c_kv = proj[..., Q_LORA:Q_LORA + KV_LORA]
    k_rope = proj[..., Q_LORA + KV_LORA:MLA_IN]
    u = proj[..., MLA_IN:MLA_IN + RWKV_IN]
    gates = jax.nn.sigmoid(proj[..., MLA_IN + RWKV_IN:])
    y_a = _mla(c_q, c_kv, k_rope, mla_q_norm, mla_w_uq, mla_kv_norm, mla_w_ukv)
    y_b = _rwkv7(_centred_shift(u, mu_prev, mu_next), rwkv_w0_f, rwkv_w2_f, rwkv_w0_b, rwkv_w2_b,
                 rwkv_a0_f, rwkv_a2_f, rwkv_a0_b, rwkv_a2_b, rwkv_g2, rwkv_k_k, rwkv_k_a,
                 rwkv_r_k, rwkv_ln_w, rwkv_ln_b)
    o_a = y_a @ w_branch[:D_MLA_V]
    o_b = y_b @ w_branch[D_MLA_V:]
    mix = (gates[..., :D_MODEL] * o_a + gates[..., D_MODEL:] * o_b) @ w_out
    x = x + _rmsnorm(mix, norm_post_mix)
    ff = jnp.square(jax.nn.relu(_rmsnorm(x, norm_pre_mlp) @ w_mlp_up)) @ w_mlp_down
    return x + _rmsnorm(ff, norm_post_mlp)


def setup_inputs(seed: int = 0) -> dict:
    key = jax.random.key(seed)
    ks = iter(jax.random.split(key, 40))
    L = DEPTH
    nrm = lambda shape, scale: scale * jax.random.normal(next(ks), shape, jnp.float32)
    gain = lambda n: 1.0 + nrm((L, n), 0.05)
    return {
        'x_prompt': nrm((BATCH, SEQ, D_MODEL), 1.0),
        'x_sample': nrm((DEC_BATCH, DEC_SEQ, D_MODEL), 1.0),
        'norm_pre_mix': gain(D_MODEL),
        'w_in': nrm((L, D_MODEL, N_IN), D_MODEL ** -0.5),
        'mu_prev': jax.random.uniform(next(ks), (L, RWKV_IN), jnp.float32, 0.0, 0.5),
        'mu_next': jax.random.uniform(next(ks), (L, RWKV_IN), jnp.float32, 0.0, 0.5),
        'mla_q_norm': gain(Q_LORA),
        'mla_w_uq': nrm((L, Q_LORA, MLA_HEADS * QK_DIM), Q_LORA ** -0.5),
        'mla_kv_norm': gain(KV_LORA),
        'mla_w_ukv': nrm((L, KV_LORA, MLA_HEADS * (NOPE_DIM + V_DIM)), KV_LORA ** -0.5),
        'rwkv_w0_f': nrm((L, D_RWKV), 0.5),
        'rwkv_w2_f': nrm((L, DECAY_LORA, D_RWKV), 0.1 * DECAY_LORA ** -0.5),
        'rwkv_w0_b': nrm((L, D_RWKV), 0.5),
        'rwkv_w2_b': nrm((L, DECAY_LORA, D_RWKV), 0.1 * DECAY_LORA ** -0.5),
        'rwkv_a0_f': nrm((L, D_RWKV), 0.5),
        'rwkv_a2_f': nrm((L, AAA_LORA, D_RWKV), 0.3 * AAA_LORA ** -0.5),
        'rwkv_a0_b': nrm((L, D_RWKV), 0.5),
        'rwkv_a2_b': nrm((L, AAA_LORA, D_RWKV), 0.3 * AAA_LORA ** -0.5),
        'rwkv_g2': nrm((L, GATE_LORA, D_RWKV), GATE_LORA ** -0.5),
        'rwkv_k_k': 1.0 + nrm((L, D_RWKV), 0.1),
        'rwkv_k_a': 1.0 + nrm((L, D_RWKV), 0.1),
        'rwkv_r_k': nrm((L, RWKV_HEADS, RWKV_HEAD), 0.1),
        'rwkv_ln_w': gain(D_RWKV),
        'rwkv_ln_b': nrm((L, D_RWKV), 0.01),
        'w_branch': nrm((L, D_MIX, D_MODEL), D_MODEL ** -0.5),
        'w_out': nrm((L, D_MODEL, D_MODEL), D_MODEL ** -0.5),
        'norm_post_mix': gain(D_MODEL),
        'norm_pre_mlp': gain(D_MODEL),
        'w_mlp_up': nrm((L, D_MODEL, D_FF), D_MODEL ** -0.5),
        'w_mlp_down': nrm((L, D_FF, D_MODEL), D_FF ** -0.5),
        'norm_post_mlp': gain(D_MODEL),
    }


def reference(x_prompt, x_sample, norm_pre_mix, w_in, mu_prev, mu_next, mla_q_norm, mla_w_uq,
              mla_kv_norm, mla_w_ukv, rwkv_w0_f, rwkv_w2_f, rwkv_w0_b, rwkv_w2_b, rwkv_a0_f,
              rwkv_a2_f, rwkv_a0_b, rwkv_a2_b, rwkv_g2, rwkv_k_k, rwkv_k_a, rwkv_r_k, rwkv_ln_w,
              rwkv_ln_b, w_branch, w_out, norm_post_mix, norm_pre_mlp, w_mlp_up, w_mlp_down,
              norm_post_mlp):
    params = (norm_pre_mix, w_in, mu_prev, mu_next, mla_q_norm, mla_w_uq, mla_kv_norm, mla_w_ukv,
              rwkv_w0_f, rwkv_w2_f, rwkv_w0_b, rwkv_w2_b, rwkv_a0_f, rwkv_a2_f, rwkv_a0_b,
              rwkv_a2_b, rwkv_g2, rwkv_k_k, rwkv_k_a, rwkv_r_k, rwkv_ln_w, rwkv_ln_b, w_branch,
              w_out, norm_post_mix, norm_pre_mlp, w_mlp_up, w_mlp_down, norm_post_mlp)
    y_prompt = x_prompt
    y_sample = x_sample
    for layer in range(DEPTH):
        lp = tuple(p[layer] for p in params)
        y_prompt = _block(y_prompt, *lp)
        y_sample = _block(y_sample, *lp)
    return (y_prompt, y_sample)
```

```python
import numpy as np, contextlib
import concourse.bass as bass
import concourse.mybir as mybir

F32 = mybir.dt.float32
BF = mybir.dt.bfloat16
AF = mybir.ActivationFunctionType
ALU = mybir.AluOpType
AX = mybir.AxisListType


PSUM_EXCL = False


class T:
    __slots__ = ("ap", "w", "r", "ds", "name", "root", "psum")

    def __init__(self, ap, name=""):
        self.ap = ap
        self.root = self
        self.psum = False
        self.w = []
        self.r = {}
        self.ds = None
        self.name = name

    def __getitem__(self, k):
        return self.ap[k]


class Sched:
    ENG = ("pe", "act", "dve", "pool", "sp")
    EPOCH = 30000

    def __init__(self, nc, gstack):
        self.nc = nc
        self.gstack = gstack
        self.h = {"pe": nc.tensor, "act": nc.scalar, "dve": nc.vector, "pool": nc.gpsimd, "sp": nc.sync}
        self.rec = {k: [] for k in self.ENG}
        self.base = {k: 0 for k in self.ENG}
        self.sems = {k: [gstack.enter_context(nc.semaphore(f"s_{k}0"))] for k in self.ENG}
        self.waited = {k: {} for k in self.ENG}
        self.dpool = []
        self.dall = []
        self.nd = 0
        self.cc_sem = gstack.enter_context(nc.semaphore("ccsem"))
        self.ncc = 0
        self.ninstr = 0
        self.tiles = []

    def tile(self, ap, name="", parent=None, psum=False):
        t = T(ap, name)
        t.psum = psum
        if parent is not None:
            t.root = parent.root
            t.psum = parent.root.psum
        self.tiles.append(t)
        return t

    def _dsem(self, t):
        if t.ds is None:
            if self.dpool:
                t.ds = self.dpool.pop()
            else:
                t.ds = [self.gstack.enter_context(self.nc.semaphore(f"d{self.nd}")), 0]
                self.nd += 1
                self.dall.append(t.ds)
        return t.ds

    def release(self, tiles):
        for t in tiles:
            if t.ds is not None:
                self.dpool.append(t.ds)
                t.ds = None

    def _deps(self, e, R, W):
        d = []
        for t in R:
            d.extend(t.w)
        for t in W:
            d.extend(t.w)
            d.extend(t.r.values())
        if e == "pe":
            d = [x for x in d if not (x[0] == "e" and x[1] == "pe")]
        return d

    def _wait(self, e, deps):
        wd = self.waited[e]
        for dep in deps:
            if dep[0] == "e":
                key = dep[1]
                if wd.get(key, -1) < dep[2]:
                    wd[key] = dep[2]
                    self.rec[e].append(["w", dep])
            else:
                key = id(dep[1])
                if wd.get(key, -1) < dep[2]:
                    wd[key] = dep[2]
                    self.rec[e].append(["w", dep])

    def op(self, e, f, R=(), W=()):
        if PSUM_EXCL:
            W = [t.root for t in W] + [t.root for t in R if t.root.psum]
            R = [t.root for t in R if not t.root.psum]
        else:
            W = [t.root for t in W]
            R = [t.root for t in R]
        self._wait(e, self._deps(e, R, W))
        idx = len(self.rec[e])
        self.rec[e].append(["i", f, False])
        dep = ("e", e, idx)
        for t in R:
            t.r[e] = dep
        for t in W:
            t.w = [dep]
            t.r = {}
        return dep

    def dma_in(self, tile, f, q="sp"):
        ds = self._dsem(tile)
        deps = [x for x in self._deps(q, (), (tile,)) if not (x[0] == "d" and x[1] is ds[0])]
        self._wait(q, deps)
        ds[1] += 16
        sem, v = ds[0], ds[1]
        self.rec[q].append(["d", f, sem])
        tile.w = [("d", sem, v)]
        tile.r = {}

    def dma_out(self, tile, f, q="sp"):
        self._wait(q, self._deps(q, (tile,), ()))
        ds = self._dsem(tile)
        ds[1] += 16
        sem, v = ds[0], ds[1]
        self.rec[q].append(["d", f, sem])
        tile.r["dma"] = ("d", sem, v)

    def flush(self, collective=None):
        nc = self.nc
        last = {}
        for e in self.ENG:
            idxs = [i for i, r in enumerate(self.rec[e]) if r[0] == "i"]
            if not idxs:
                self.rec[e].append(["i", (lambda h: h.nop()) if e != "pe" else (lambda h: h.nop()), False])
                idxs = [len(self.rec[e]) - 1]
            last[e] = idxs[-1]
        for e in self.ENG:
            for e2 in self.ENG:
                if e2 != e:
                    self.rec[e].append(["w", ("e", e2, last[e2])])
            for ds in self.dall:
                if ds[1] > 0:
                    self.rec[e].append(["w", ("d", ds[0], ds[1])])
        for e in self.ENG:
            for r in self.rec[e]:
                if r[0] == "w" and r[1][0] == "e":
                    self.rec[r[1][1]][r[1][2]][2] = True
        cntmap = {}
        plan = {}
        for e in self.ENG:
            c = self.base[e]
            ep = len(self.sems[e]) - 1
            pl = []
            for i, r in enumerate(self.rec[e]):
                if r[0] == "i" and r[2]:
                    if c >= self.EPOCH:
                        self.sems[e].append(self.gstack.enter_context(nc.semaphore(f"s_{e}{len(self.sems[e])}")))
                        ep += 1
                        c = 0
                    c += 1
                    cntmap[(e, i)] = (self.sems[e][ep], c)
            self.base[e] = c
        recs = self.rec
        ccs = self.cc_sem

        def emit(e, h):
            for i, r in enumerate(recs[e]):
                if r[0] == "w":
                    dep = r[1]
                    if dep[0] == "e":
                        sem, v = cntmap[(dep[1], dep[2])]
                        h.wait_ge(sem, v)
                    else:
                        h.wait_ge(dep[1], dep[2])
                elif r[0] == "i":
                    ins = r[1](h)
                    if r[2]:
                        sem, v = cntmap[(e, i)]
                        ins.then_inc(sem, 1)
                    self.ninstr += 1
                else:
                    r[1](h).then_inc(r[2], 16)
                    self.ninstr += 1
            if e == "pool" and collective is not None:
                self.ncc += 1
                collective(h).then_inc(ccs)
                h.wait_ge(ccs, self.ncc)

        with nc.Block() as block:
            @block.tensor
            def _(h):
                emit("pe", h)

            @block.scalar
            def _(h):
                emit("act", h)

            @block.vector
            def _(h):
                emit("dve", h)

            @block.gpsimd
            def _(h):
                emit("pool", h)

            @block.sync
            def _(h):
                emit("sp", h)
        self.rec = {k: [] for k in self.ENG}
        self.waited = {k: {} for k in self.ENG}
        for t in self.tiles:
            t.w = []
            t.r = {}
        if collective is not None:
            self.op("pool", lambda h: h.nop())
            self.flush()

from concourse.bass_utils import run_bass_kernel_spmd

NCORES = 8
NT = 16384
TB = 512
NB = NT // TB
D = 2048
KC = D // 128
EPS = 1e-6
SEQS = [(0, 8192), (8192, 2048), (10240, 2048), (12288, 2048), (14336, 2048)]
NSH = 18
U0 = 1344
G0 = 1344 + 6784
SCALE = 192 ** -0.5


class Ph:
    def __init__(s, nc, S):
        s.nc, s.S, s.st = nc, S, contextlib.ExitStack()

    _uid = [0]

    def sb(s, name, shape, dt=F32):
        Ph._uid[0] += 1
        name = f"{name}_{Ph._uid[0]}"
        return s.S.tile(s.st.enter_context(s.nc.sbuf_tensor(name, shape, dt)), name)

    def ps(s, name, shape, dt=F32):
        Ph._uid[0] += 1
        name = f"{name}_{Ph._uid[0]}"
        return s.S.tile(s.st.enter_context(s.nc.psum_tensor(name, shape, dt)), name, psum=True)

    def end(s, collective=None):
        s.S.flush(collective)
        s.S.release(s.S.tiles)
        s.S.tiles = []
        s.st.close()


def allgather(nc, src, dst):
    return lambda h: h.collective_compute("AllGather", ALU.bypass, replica_groups=[list(range(NCORES))],
                                          ins=[src.ap().opt()], outs=[dst.ap().opt()])


def load_w(S, ph, name, wdram, nk, ncols, scal=None, rows=128):
    wb = ph.sb(name, [128, nk, ncols], BF)
    st = [ph.sb(f"{name}_st{i}", [128, ncols], F32) for i in range(2)]
    for kc in range(nk):
        s_ = st[kc % 2]
        S.dma_in(s_, lambda h, s_=s_, kc=kc: h.dma_start(out=s_[:, :], in_=wdram[kc * 128:(kc + 1) * 128, :]))
        if scal is None:
            S.op("pool", lambda h, s_=s_, kc=kc: h.tensor_copy(out=wb[:, kc, :], in_=s_[:, :]), R=[s_], W=[wb])
        else:
            S.op("pool", lambda h, s_=s_, kc=kc: h.tensor_scalar(out=wb[:, kc, :], in0=s_[:, :], scalar1=scal[:, kc:kc + 1],
                                                                scalar2=None, op0=ALU.mult), R=[s_, scal], W=[wb])
    return wb


def rstd_from(S, ph_tiles, ps, n, dim, rs, rstd, npart=128):
    S.op("act", lambda h: h.activation(out=rs[0:npart, 0:n], in_=ps[0:npart, 0:n], func=AF.Sqrt, bias=EPS, scale=1.0 / dim), R=[ps], W=[rs])
    S.op("dve", lambda h: h.reciprocal(out=rstd[0:npart, 0:n], in_=rs[0:npart, 0:n]), R=[rs], W=[rstd])


def phase_A(nc, S, dr):
    NCC = 13
    ph = Ph(nc, S)
    g0 = ph.sb("A_g0", [128, KC])
    S.dma_in(g0, lambda h: h.dma_start(out=g0[:, :], in_=dr["g0"][:, :]))
    wb = load_w(S, ph, "A_wb", dr["w_inA"], KC, NCC * 128, scal=g0)
    ones = ph.sb("A_ones", [128, 128], BF)
    xs = [ph.sb(f"A_xs{i}", [128, KC, TB]) for i in range(2)]
    xb = [ph.sb(f"A_xb{i}", [128, KC, TB], BF) for i in range(2)]
    sq = ph.sb("A_sq", [128, KC, TB], BF)
    rs = ph.sb("A_rs", [128, TB])
    rstd = ph.sb("A_rstd", [128, TB])
    ot = [ph.sb(f"A_ot{i}", [128, TB], BF) for i in range(4)]
    gt = [ph.sb(f"A_gt{i}", [128, TB]) for i in range(2)]
    pss = ph.ps("A_pss", [128, TB])
    pso = [ph.ps(f"A_pso{i}", [128, TB]) for i in range(4)]
    S.op("pool", lambda h: h.memset(ones[:, :], 1.0), W=[ones])
    xT = dr["xT"].ap().rearrange("(kc p) t -> p kc t", p=128)

    def load_x(tb):
        t = xs[tb % 2]
        S.dma_in(t, lambda h: h.dma_start(out=t[:, :, :], in_=xT[:, :, tb * TB:(tb + 1) * TB]))
    load_x(0)
    oi = 0
    for tb in range(NB):
        x, b = xs[tb % 2], xb[tb % 2]
        if tb + 1 < NB:
            load_x(tb + 1)
        S.op("pool", lambda h, x=x, b=b: h.tensor_copy(out=b[:, :, :], in_=x[:, :, :]), R=[x], W=[b])
        S.op("act", lambda h, x=x: h.activation(out=sq[:, :, :], in_=x[:, :, :], func=AF.Square), R=[x], W=[sq])
        for kc in range(KC):
            S.op("pe", lambda h, kc=kc: h.matmul(pss[:, :], ones[:, :], sq[:, kc, :], start=(kc == 0), stop=(kc == KC - 1)),
                 R=[ones, sq], W=[pss])
        rstd_from(S, None, pss, TB, D, rs, rstd)
        for cc in range(NCC):
            p = pso[cc % 4]
            for kc in range(KC):
                S.op("pe", lambda h, p=p, kc=kc, cc=cc, b=b: h.matmul(p[:, :], wb[:, kc, cc * 128:(cc + 1) * 128], b[:, kc, :],
                                                                  start=(kc == 0), stop=(kc == KC - 1)), R=[wb, b], W=[p])
            o = ot[oi % 4]
            oi += 1
            tsl = slice(tb * TB, (tb + 1) * TB)
            if cc < 9:
                S.op("dve", lambda h, p=p, o=o: h.tensor_tensor(out=o[:, :], in0=p[:, :], in1=rstd[:, :], op=ALU.mult), R=[p, rstd], W=[o])
                dst = dr["sh_b"][cc * 128:(cc + 1) * 128, tsl] if cc < 3 else dr["rkv_s"][(cc - 3) * 128:(cc - 2) * 128, tsl]
            else:
                g = gt[cc % 2]
                S.op("dve", lambda h, p=p, g=g: h.tensor_tensor(out=g[:, :], in0=p[:, :], in1=rstd[:, :], op=ALU.mult), R=[p, rstd], W=[g])
                S.op("act", lambda h, g=g, o=o: h.activation(out=o[:, :], in_=g[:, :], func=AF.Sigmoid), R=[g], W=[o])
                dst = dr["gate_s"][(cc - 9) * 128:(cc - 8) * 128, tsl]
            S.dma_out(o, lambda h, o=o, dst=dst: h.dma_start(out=dst, in_=o[:, :]))
    ph.end(allgather(nc, dr["sh_b"], dr["sh_g"]))


def sh_rows(i):
    return (i % 8) * 384 + (i // 8) * 128


def phase_M(nc, S, dr):
    ph = Ph(nc, S)
    qg = ph.sb("M_qg", [128, 6])
    kg = ph.sb("M_kg", [128, 4])
    S.dma_in(qg, lambda h: h.dma_start(out=qg[:, :], in_=dr["qng"][:, :]))
    S.dma_in(kg, lambda h: h.dma_start(out=kg[:, :], in_=dr["kvg"][:, :]))
    wq = load_w(S, ph, "M_wq", dr["w_uq"], 6, 512, scal=qg)
    wkv = load_w(S, ph, "M_wkv", dr["w_ukv"], 4, 512, scal=kg)
    ones = ph.sb("M_ones", [128, 128], BF)
    ident = ph.sb("M_id", [128, 128], F32)
    S.op("pool", lambda h: h.memset(ones[:, :], 1.0), W=[ones])
    S.dma_in(ident, lambda h: h.dma_start(out=ident[:, :], in_=dr["ident"][:, :]))
    LMAX = 8192
    qn = ph.sb("M_qn", [128, LMAX], BF)
    qr = ph.sb("M_qr", [64, LMAX], BF)
    kn = ph.sb("M_kn", [128, LMAX], BF)
    kr = ph.sb("M_kr", [64, LMAX], BF)
    va = ph.sb("M_va", [128, LMAX // 128, 132], BF)
    S.op("pool", lambda h: h.memset(va[:, :, 128:129], 1.0), W=[va])
    cq = ph.sb("M_cq", [128, 6, TB], BF)
    ckv = ph.sb("M_ckv", [128, 4, TB], BF)
    ckvs = ph.sb("M_ckvs", [128, 4, TB], BF)
    kro = ph.sb("M_kro", [64, 2, TB], BF)
    sqq = ph.sb("M_sqq", [128, 6, TB], BF)
    sqk = ph.sb("M_sqk", [128, 4, TB], BF)
    cs = ph.sb("M_cs", [64, 2, TB])
    rs = ph.sb("M_rs", [128, TB])
    rq = ph.sb("M_rq", [128, TB])
    rk = ph.sb("M_rk", [128, TB])
    rsc = ph.sb("M_rsc", [128, 4])
    rkc = ph.sb("M_rkc", [128, 4])
    t1 = ph.sb("M_t1", [64, TB])
    t2 = ph.sb("M_t2", [64, TB])
    pt = [ph.sb(f"M_pt{i}", [128, TB], BF) for i in range(2)]
    on = ph.sb("M_on", [128, 128], F32)
    rinv = ph.sb("M_rinv", [128, 1])
    yt = [ph.sb(f"M_yt{i}", [128, TB], BF) for i in range(2)]
    psA = [ph.ps(f"M_psA{i}", [128, TB]) for i in range(2)]
    psO = [ph.ps(f"M_psO{i}", [128, 512]) for i in range(4)]
    psB = ph.ps("M_psB", [128, TB])
    psT = ph.ps("M_psT", [128, 512], F32)
    shg = dr["sh_g"]
    yi = 0
    for hl in range(2):
        for (s0, L) in SEQS:
            for tb in range(L // TB):
                tsl = slice(s0 + tb * TB, s0 + (tb + 1) * TB)
                lsl = slice(tb * TB, (tb + 1) * TB)
                for i in range(6):
                    S.dma_in(cq, lambda h, i=i, tsl=tsl: h.dma_start(out=cq[:, i, :], in_=shg[sh_rows(i):sh_rows(i) + 128, tsl]))
                for i in range(4):
                    S.dma_in(ckv, lambda h, i=i, tsl=tsl: h.dma_start(out=ckv[:, i, :], in_=shg[sh_rows(6 + i):sh_rows(6 + i) + 128, tsl]))
                S.dma_in(kro, lambda h, tsl=tsl: h.dma_start(out=kro[:, 0, :], in_=shg[sh_rows(10):sh_rows(10) + 64, tsl]))
                S.dma_in(kro, lambda h, tsl=tsl: h.dma_start(out=kro[:, 1, :], in_=shg[sh_rows(17):sh_rows(17) + 64, tsl]))
                S.dma_in(cs, lambda h, lsl=lsl: h.dma_start(out=cs[:, 0, :], in_=dr["cos2"][:, lsl]))
                S.dma_in(cs, lambda h, lsl=lsl: h.dma_start(out=cs[:, 1, :], in_=dr["sin2"][:, lsl]))
                S.op("act", lambda h: h.activation(out=sqq[:, :, :], in_=cq[:, :, :], func=AF.Square), R=[cq], W=[sqq])
                S.op("act", lambda h: h.activation(out=sqk[:, :, :], in_=ckv[:, :, :], func=AF.Square), R=[ckv], W=[sqk])
                p = psA[0]
                for i in range(6):
                    S.op("pe", lambda h, i=i, p=p: h.matmul(p[:, :], ones[:, :], sqq[:, i, :], start=(i == 0), stop=(i == 5)), R=[ones, sqq], W=[p])
                rstd_from(S, None, p, TB, 768, rs, rq)
                p = psA[1]
                for i in range(4):
                    S.op("pe", lambda h, i=i, p=p: h.matmul(p[:, :], ones[:, :], sqk[:, i, :], start=(i == 0), stop=(i == 3)), R=[ones, sqk], W=[p])
                rstd_from(S, None, p, TB, 512, rs, rk)
                for i in range(4):
                    S.op("pool" if i % 2 else "dve", lambda h, i=i: h.tensor_tensor(out=ckvs[:, i, :], in0=ckv[:, i, :], in1=rk[:, :], op=ALU.mult),
                         R=[ckv, rk], W=[ckvs])
                c0 = hl * 256
                p = psA[0]
                for i in range(6):
                    S.op("pe", lambda h, i=i, p=p, c0=c0: h.matmul(p[:, :], wq[:, i, c0:c0 + 128], cq[:, i, :], start=(i == 0), stop=(i == 5)), R=[wq, cq], W=[p])
                S.op("dve", lambda h, p=p, lsl=lsl: h.tensor_tensor(out=qn[:, lsl], in0=p[:, :], in1=rq[:, :], op=ALU.mult), R=[p, rq], W=[qn])
                p = psA[1]
                for i in range(6 if 'q' not in PHASES else 0):
                    S.op("pe", lambda h, i=i, p=p, c0=c0: h.matmul(p[0:64, :], wq[:, i, c0 + 128:c0 + 192], cq[:, i, :], start=(i == 0), stop=(i == 5)), R=[wq, cq], W=[p])
                S.op("dve", lambda h, p=p: h.tensor_tensor(out=t1[:, :], in0=p[0:64, :], in1=cs[:, 0, :], op=ALU.mult), R=[p, cs], W=[t1])
                p = psA[0]
                for i in range(6):
                    S.op("pe", lambda h, i=i, p=p, c0=c0: h.matmul(p[0:64, :], wq[:, i, c0 + 192:c0 + 256], cq[:, i, :], start=(i == 0), stop=(i == 5)), R=[wq, cq], W=[p])
                S.op("dve", lambda h, p=p: h.tensor_tensor(out=t2[:, :], in0=p[0:64, :], in1=cs[:, 1, :], op=ALU.mult), R=[p, cs], W=[t2])
                S.op("pool", lambda h: h.tensor_tensor(out=t1[:, :], in0=t1[:, :], in1=t2[:, :], op=ALU.add), R=[t2], W=[t1])
                S.op("dve", lambda h, lsl=lsl: h.tensor_tensor(out=qr[:, lsl], in0=t1[:, :], in1=rq[0:64, :], op=ALU.mult), R=[t1, rq], W=[qr])
                k0 = hl * 256
                p = psA[1]
                for i in range(4):
                    S.op("pe", lambda h, i=i, p=p, k0=k0: h.matmul(p[:, :], wkv[:, i, k0:k0 + 128], ckvs[:, i, :], start=(i == 0), stop=(i == 3)), R=[wkv, ckvs], W=[p])
                S.op("dve", lambda h, p=p, lsl=lsl: h.tensor_copy(out=kn[:, lsl], in_=p[:, :]), R=[p], W=[kn])
                S.op("dve", lambda h: h.tensor_tensor(out=t1[:, :], in0=kro[:, 0, :], in1=cs[:, 0, :], op=ALU.mult), R=[kro, cs], W=[t1])
                S.op("pool", lambda h: h.tensor_tensor(out=t2[:, :], in0=kro[:, 1, :], in1=cs[:, 1, :], op=ALU.mult), R=[kro, cs], W=[t2])
                S.op("dve", lambda h, lsl=lsl: h.tensor_tensor(out=kr[:, lsl], in0=t1[:, :], in1=t2[:, :], op=ALU.add), R=[t1, t2], W=[kr])
                for j in range(4 if 'v' not in PHASES else 0):
                    p = psO[j]
                    for i in range(4):
                        S.op("pe", lambda h, i=i, j=j, p=p, k0=k0: h.matmul(p[:, 0:128], ckvs[:, i, j * 128:(j + 1) * 128], wkv[:, i, k0 + 128:k0 + 256],
                                                                  start=(i == 0), stop=(i == 3)), R=[wkv, ckvs], W=[p])
                    S.op("dve", lambda h, j=j, p=p, tb=tb: h.tensor_copy(out=va[:, tb * 4 + j, 0:128], in_=p[:, 0:128]), R=[p], W=[va])
            nkb = L // 128
            for qb in range(L // TB):
                qsl = slice(qb * TB, (qb + 1) * TB)
                for kb in range(nkb):
                    ksl = slice(kb * 128, (kb + 1) * 128)
                    p = psA[kb % 2]
                    S.op("pe", lambda h, p=p, ksl=ksl, qsl=qsl: h.matmul(p[:, :], kn[:, ksl], qn[:, qsl], start=True, stop=False), R=[kn, qn], W=[p])
                    S.op("pe", lambda h, p=p, ksl=ksl, qsl=qsl: h.matmul(p[:, :], kr[:, ksl], qr[:, qsl], start=False, stop=True), R=[kr, qr], W=[p])
                    e = pt[kb % 2]
                    S.op("act", lambda h, p=p, e=e: h.activation(out=e[:, :], in_=p[:, :], func=AF.Exp, scale=SCALE), R=[p], W=[e])
                    for j in range(4):
                        S.op("pe", lambda h, j=j, e=e, kb=kb, nkb=nkb: h.matmul(psO[j][:, 0:129], e[:, j * 128:(j + 1) * 128], va[:, kb, 0:129],
                                                                    start=(kb == 0), stop=(kb == nkb - 1)), R=[e, va], W=[psO[j]])
                y = yt[yi % 2]
                yi += 1
                for j in range(4):
                    S.op("dve", lambda h, j=j: h.reciprocal(out=rinv[:, :], in_=psO[j][:, 128:129]), R=[psO[j]], W=[rinv])
                    S.op("dve", lambda h, j=j: h.tensor_scalar(out=on[:, :], in0=psO[j][:, 0:128], scalar1=rinv[:, 0:1], scalar2=None, op0=ALU.mult),
                         R=[psO[j], rinv], W=[on])
                    S.op("pe", lambda h, j=j: h.transpose(psT[:, j * 128:(j + 1) * 128], on[:, :], ident[:, :]), R=[on, ident], W=[psT])
                S.op("act", lambda h, y=y: h.activation(out=y[:, :], in_=psT[:, 0:512], func=AF.Copy), R=[psT], W=[y])
                dst = dr["y_b"][hl * 128:(hl + 1) * 128, s0 + qb * TB:s0 + (qb + 1) * TB]
                S.dma_out(y, lambda h, y=y, dst=dst: h.dma_start(out=dst, in_=y[:, :]))
    ph.end()


def lin_phase(nc, S, ph, name, src_ap, nk, wb, ncc, epi, tbs=TB, pre=None, ks=None):
    xb = [ph.sb(f"{name}_xb{i}", [128, nk, tbs], BF) for i in range(2)]
    pso = [ph.ps(f"{name}_ps{i}", [128, tbs]) for i in range(3)]
    nb = NT // tbs

    def load(tb):
        t = xb[tb % 2]
        tsl = slice(tb * tbs, (tb + 1) * tbs)
        for f in src_ap(t, tsl):
            S.dma_in(t, f)
    load(0)
    pi = 0
    for tb in range(nb):
        tsl = slice(tb * tbs, (tb + 1) * tbs)
        if tb + 1 < nb:
            load(tb + 1)
        b = xb[tb % 2]
        if pre is not None:
            pre(tb, tsl)
        for cc in range(ncc):
            p = pso[pi % 3]
            pi += 1
            kl = list(range(nk)) if ks is None else ks(cc)
            wc = cc if ks is None else cc % 2
            for n, kc in enumerate(kl):
                S.op("pe", lambda h, p=p, kc=kc, wc=wc, b=b, n=n, kl=kl: h.matmul(p[:, :], wb[:, kc, wc * 128:(wc + 1) * 128], b[:, kc, :],
                                                                              start=(n == 0), stop=(n == len(kl) - 1)), R=[wb, b], W=[p])
            epi(tb, cc, p, tsl)


def ss_partial(S, ph, name):
    ones = ph.sb(f"{name}_ones", [128, 128], BF)
    S.op("pool", lambda h: h.memset(ones[:, :], 1.0), W=[ones])
    sq = [ph.sb(f"{name}_sq{i}", [128, TB], BF) for i in range(2)]
    row = ph.sb(f"{name}_row", [1, TB])
    pss = ph.ps(f"{name}_pss", [128, TB])

    def f(cc, ncc, src, n, tsl, dst):
        s_ = sq[cc % 2]
        S.op("act", lambda h: h.activation(out=s_[:, 0:n], in_=src[:, 0:n], func=AF.Square), R=[src], W=[s_])
        S.op("pe", lambda h: h.matmul(pss[:, 0:n], ones[:, :], s_[:, 0:n], start=(cc == 0), stop=(cc == ncc - 1)), R=[ones, s_], W=[pss])
        if cc == ncc - 1:
            S.op("act", lambda h: h.activation(out=row[0:1, 0:n], in_=pss[0:1, 0:n], func=AF.Copy), R=[pss], W=[row])
            S.dma_out(row, lambda h: h.dma_start(out=dst[0:1, tsl], in_=row[0:1, 0:n]))
    return f


def ss_total(S, ph, name, dim):
    onesf = ph.sb(f"{name}_onesf", [8, 128])
    S.op("pool", lambda h: h.memset(onesf[:, :], 1.0), W=[onesf])
    ssl = ph.sb(f"{name}_ssl", [8, TB])
    rs = ph.sb(f"{name}_rs", [128, TB])
    rstd = ph.sb(f"{name}_rstd", [128, TB])
    pst = ph.ps(f"{name}_pst", [128, TB])

    def f(ssg, tsl, n):
        S.dma_in(ssl, lambda h: h.dma_start(out=ssl[:, 0:n], in_=ssg[:, tsl]))
        S.op("pe", lambda h: h.matmul(pst[:, 0:n], onesf[:, :], ssl[:, 0:n], start=True, stop=True), R=[onesf, ssl], W=[pst])
        rstd_from(S, None, pst, n, dim, rs, rstd)
        return rstd
    return f


def one_dma(ap_fn):
    return lambda t, tsl: [lambda h: h.dma_start(out=t[:, :, :], in_=ap_fn(tsl))]


def phase_C(nc, S, dr):
    ph = Ph(nc, S)
    wb = load_w(S, ph, "C_wb", dr["w_br"], 32, 256)
    gt = [ph.sb(f"C_gt{i}", [128, 4, TB], BF) for i in range(2)]
    tmp = [ph.sb(f"C_tmp{i}", [128, TB]) for i in range(2)]
    t2 = ph.sb("C_t2", [128, TB])
    ot = [ph.sb(f"C_ot{i}", [128, TB], BF) for i in range(2)]
    yg = dr["y_g"].ap().rearrange("(k p) t -> p k t", p=128)
    gs = dr["gate_s"].ap().rearrange("(k p) t -> p k t", p=128)
    st = {}

    def pre(tb, tsl):
        g = gt[tb % 2]
        S.dma_in(g, lambda h: h.dma_start(out=g[:, :, :], in_=gs[:, :, tsl]))
        st["g"] = g

    def ks(cc):
        br = cc // 2
        return [r * 4 + br * 2 + j for r in range(8) for j in range(2)]

    def epi(tb, cc, p, tsl):
        g = st["g"]
        j = cc % 2
        if cc < 2:
            S.op("dve", lambda h: h.tensor_tensor(out=tmp[j][:, :], in0=p[:, :], in1=g[:, j, :], op=ALU.mult), R=[p, g], W=[tmp[j]])
        else:
            o = ot[j]
            S.op("dve", lambda h: h.tensor_tensor(out=t2[:, :], in0=p[:, :], in1=g[:, 2 + j, :], op=ALU.mult), R=[p, g], W=[t2])
            S.op("pool", lambda h: h.tensor_tensor(out=o[:, :], in0=t2[:, :], in1=tmp[j][:, :], op=ALU.add), R=[t2, tmp[j]], W=[o])
            dst = dr["mix_b"][j * 128:(j + 1) * 128, tsl]
            S.dma_out(o, lambda h: h.dma_start(out=dst, in_=o[:, :]))
    lin_phase(nc, S, ph, "C", one_dma(lambda tsl: yg[:, :, tsl]), 32, wb, 4, epi, pre=pre, ks=ks)
    ph.end(allgather(nc, dr["mix_b"], dr["mix_g"]))


def phase_D(nc, S, dr):
    ph = Ph(nc, S)
    wb = load_w(S, ph, "D_wb", dr["w_o"], 16, 256)
    ssp = ss_partial(S, ph, "D")
    mt = [ph.sb(f"D_mt{i}", [128, TB]) for i in range(2)]
    mg = dr["mix_g"].ap().rearrange("(k p) t -> p k t", p=128)

    def epi(tb, cc, p, tsl):
        m = mt[cc % 2]
        S.op("dve", lambda h: h.tensor_copy(out=m[:, :], in_=p[:, :]), R=[p], W=[m])
        dst = dr["mix_s"][cc * 128:(cc + 1) * 128, tsl]
        S.dma_out(m, lambda h: h.dma_start(out=dst, in_=m[:, :]))
        ssp(cc, 2, m, TB, tsl, dr["ss1_b"])
    lin_phase(nc, S, ph, "D", one_dma(lambda tsl: mg[:, :, tsl]), 16, wb, 2, epi)
    ph.end(allgather(nc, dr["ss1_b"], dr["ss1_g"]))


def phase_E(nc, S, dr):
    ph = Ph(nc, S)
    gn = ph.sb("E_gn", [128, 6])
    S.dma_in(gn, lambda h: h.dma_start(out=gn[:, :], in_=dr["gn"][:, :]))
    tot = ss_total(S, ph, "E", D)
    ssp = ss_partial(S, ph, "E2")
    mt = [ph.sb(f"E_mt{i}", [128, 2, TB]) for i in range(2)]
    xt = [ph.sb(f"E_xt{i}", [128, 2, TB]) for i in range(2)]
    x1 = [ph.sb(f"E_x1{i}", [128, TB]) for i in range(2)]
    tt = ph.sb("E_tt", [128, TB])
    hb = [ph.sb(f"E_hb{i}", [128, TB], BF) for i in range(2)]
    ms = dr["mix_s"].ap().rearrange("(k p) t -> p k t", p=128)
    xs = dr["xTs"].ap().rearrange("(k p) t -> p k t", p=128)
    for tb in range(NB):
        tsl = slice(tb * TB, (tb + 1) * TB)
        m, x = mt[tb % 2], xt[tb % 2]
        S.dma_in(m, lambda h, m=m, tsl=tsl: h.dma_start(out=m[:, :, :], in_=ms[:, :, tsl]))
        S.dma_in(x, lambda h, x=x, tsl=tsl: h.dma_start(out=x[:, :, :], in_=xs[:, :, tsl]))
        rstd = tot(dr["ss1_g"], tsl, TB)
        for cc in range(2):
            o = x1[cc]
            S.op("dve", lambda h, m=m, cc=cc: h.tensor_tensor(out=tt[:, :], in0=m[:, cc, :], in1=rstd[:, :], op=ALU.mult), R=[m, rstd], W=[tt])
            S.op("dve", lambda h, o=o, x=x, cc=cc: h.scalar_tensor_tensor(out=o[:, :], in0=tt[:, :], scalar=gn[:, cc:cc + 1], in1=x[:, cc, :],
                                                                       op0=ALU.mult, op1=ALU.add), R=[tt, gn, x], W=[o])
            dst = dr["x1_s"][cc * 128:(cc + 1) * 128, tsl]
            S.dma_out(o, lambda h, o=o, dst=dst: h.dma_start(out=dst, in_=o[:, :]))
            hh = hb[cc]
            S.op("pool", lambda h, hh=hh, o=o, cc=cc: h.tensor_scalar(out=hh[:, :], in0=o[:, :], scalar1=gn[:, 2 + cc:3 + cc], scalar2=None, op0=ALU.mult),
                 R=[o, gn], W=[hh])
            dst2 = dr["h2_b"][cc * 128:(cc + 1) * 128, tsl]
            S.dma_out(hh, lambda h, hh=hh, dst2=dst2: h.dma_start(out=dst2, in_=hh[:, :]))
            ssp(cc, 2, o, TB, tsl, dr["ss2_b"])
    ph.end(allgather(nc, dr["h2_b"], dr["h2_g"]))
    ph = Ph(nc, S)
    nop = ph.sb("E_nop", [128, 8])
    S.op("pool", lambda h: h.memset(nop[:, :], 0.0), W=[nop])
    ph.end(allgather(nc, dr["ss2_b"], dr["ss2_g"]))


def phase_G(nc, S, dr):
    ph = Ph(nc, S)
    wb = load_w(S, ph, "G_wb", dr["w_up"], 16, 1024)
    tot = ss_total(S, ph, "G", D)
    tt = [ph.sb(f"G_tt{i}", [128, TB]) for i in range(2)]
    ot = [ph.sb(f"G_ot{i}", [128, TB], BF) for i in range(3)]
    hg = dr["h2_g"].ap().rearrange("(k p) t -> p k t", p=128)
    st = {"n": 0}

    def pre(tb, tsl):
        st["r"] = tot(dr["ss2_g"], tsl, TB)

    def epi(tb, cc, p, tsl):
        rstd = st["r"]
        t = tt[cc % 2]
        o = ot[st["n"] % 3]
        st["n"] += 1
        S.op("dve", lambda h: h.scalar_tensor_tensor(out=t[:, :], in0=p[:, :], scalar=0.0, in1=rstd[:, :], op0=ALU.max, op1=ALU.mult),
             R=[p, rstd], W=[t])
        S.op("act", lambda h: h.activation(out=o[:, :], in_=t[:, :], func=AF.Square), R=[t], W=[o])
        dst = dr["hid_b"][cc * 128:(cc + 1) * 128, tsl]
        S.dma_out(o, lambda h: h.dma_start(out=dst, in_=o[:, :]))
    lin_phase(nc, S, ph, "G", one_dma(lambda tsl: hg[:, :, tsl]), 16, wb, 8, epi, pre=pre)
    ph.end(allgather(nc, dr["hid_b"], dr["hid_g"]))


def phase_H(nc, S, dr):
    ph = Ph(nc, S)
    wb = load_w(S, ph, "H_wb", dr["w_dn"], 64, 256)
    ssp = ss_partial(S, ph, "H")
    mt = [ph.sb(f"H_mt{i}", [128, 256]) for i in range(2)]
    hg = dr["hid_g"].ap().rearrange("(k p) t -> p k t", p=128)

    def epi(tb, cc, p, tsl):
        m = mt[cc % 2]
        S.op("dve", lambda h: h.tensor_copy(out=m[:, :], in_=p[:, :]), R=[p], W=[m])
        dst = dr["ff_s"][cc * 128:(cc + 1) * 128, tsl]
        S.dma_out(m, lambda h: h.dma_start(out=dst, in_=m[:, :]))
        ssp(cc, 2, m, 256, tsl, dr["ss3_b"])
    lin_phase(nc, S, ph, "H", one_dma(lambda tsl: hg[:, :, tsl]), 64, wb, 2, epi, tbs=256)
    ph.end(allgather(nc, dr["ss3_b"], dr["ss3_g"]))


def phase_I(nc, S, dr):
    ph = Ph(nc, S)
    gn = ph.sb("I_gn", [128, 6])
    S.dma_in(gn, lambda h: h.dma_start(out=gn[:, :], in_=dr["gn"][:, :]))
    tot = ss_total(S, ph, "I", D)
    ft = [ph.sb(f"I_ft{i}", [128, 2, TB]) for i in range(2)]
    xt = [ph.sb(f"I_xt{i}", [128, 2, TB]) for i in range(2)]
    yo = [ph.sb(f"I_yo{i}", [128, TB]) for i in range(2)]
    tt = ph.sb("I_tt", [128, TB])
    fs = dr["ff_s"].ap().rearrange("(k p) t -> p k t", p=128)
    xs = dr["x1_s"].ap().rearrange("(k p) t -> p k t", p=128)
    for tb in range(NB):
        tsl = slice(tb * TB, (tb + 1) * TB)
        f, x = ft[tb % 2], xt[tb % 2]
        S.dma_in(f, lambda h, f=f, tsl=tsl: h.dma_start(out=f[:, :, :], in_=fs[:, :, tsl]))
        S.dma_in(x, lambda h, x=x, tsl=tsl: h.dma_start(out=x[:, :, :], in_=xs[:, :, tsl]))
        rstd = tot(dr["ss3_g"], tsl, TB)
        for cc in range(2):
            o = yo[cc]
            S.op("dve", lambda h, f=f, cc=cc: h.tensor_tensor(out=tt[:, :], in0=f[:, cc, :], in1=rstd[:, :], op=ALU.mult), R=[f, rstd], W=[tt])
            S.op("dve", lambda h, o=o, x=x, cc=cc: h.scalar_tensor_tensor(out=o[:, :], in0=tt[:, :], scalar=gn[:, 4 + cc:5 + cc], in1=x[:, cc, :],
                                                                       op0=ALU.mult, op1=ALU.add), R=[tt, gn, x], W=[o])
            dst = dr["yT"][cc * 128:(cc + 1) * 128, tsl]
            S.dma_out(o, lambda h, o=o, dst=dst: h.dma_start(out=dst, in_=o[:, :]))
    ph.end()


RW_INPUTS = {"rw_vec": [128, 9 * 256], "w_lora": [128, 4 * 256], "w_g2": [256, 256], "mu_rkv": [128, 12], "mu_sh": [128, 12],
             "rw_M": [128, 6 * 128], "identf": [128, 128]}
CDEC = 0.6065306597126334
RSTAGE = 9
RNBLK = 999
GN_EPS = 64e-5
LORA_CH = (11, 12, 13, 14, 15, 16)


def rw_host(inp, c):
    f = lambda k: np.asarray(inp[k], dtype=np.float32)[0]
    sl = slice(256 * c, 256 * (c + 1))
    vec = np.stack([f("rwkv_w0_f")[sl], f("rwkv_w0_b")[sl], f("rwkv_a0_f")[sl], f("rwkv_a0_b")[sl], f("rwkv_k_k")[sl], f("rwkv_k_a")[sl],
                    f("rwkv_r_k").reshape(-1)[sl], f("rwkv_ln_w")[sl], f("rwkv_ln_b")[sl]], 0).reshape(1, -1)
    m = {"rw_vec": np.ascontiguousarray(np.repeat(vec, 128, 0))}
    wl = np.zeros((128, 4, 256), np.float32)
    for j, k in enumerate(("rwkv_w2_f", "rwkv_w2_b", "rwkv_a2_f", "rwkv_a2_b")):
        wl[:96, j] = f(k)[:, sl]
    m["w_lora"] = wl.reshape(128, -1)
    m["w_g2"] = np.ascontiguousarray(f("rwkv_g2")[:, sl])
    mp, mn = f("mu_prev"), f("mu_next")
    mu = np.zeros((128, 6, 2), np.float32)
    for j, base in enumerate((0, 0, 2048, 2048, 4096, 4096)):
        idx = base + 256 * c + 128 * (j % 2) + np.arange(128)
        mu[:, j, 0], mu[:, j, 1] = mp[idx], mn[idx]
    m["mu_rkv"] = mu.reshape(128, -1)
    mu = np.zeros((128, 6, 2), np.float32)
    for j in range(4):
        idx = 6144 + 96 * j + np.arange(96)
        mu[:96, j, 0], mu[:96, j, 1] = mp[idx], mn[idx]
    for j in range(2):
        idx = 6528 + 128 * j + np.arange(128)
        mu[:, 4 + j, 0], mu[:, 4 + j, 1] = mp[idx], mn[idx]
    m["mu_sh"] = mu.reshape(128, -1)
    i = np.arange(128)
    same = (i[:, None] // 64) == (i[None, :] // 64)
    M = np.zeros((128, 6, 128), np.float32)
    lt = same & (i[:, None] < i[None, :])
    gt = same & (i[:, None] > i[None, :])
    eq = np.eye(128, dtype=bool)
    M[:, 0], M[:, 1], M[:, 2] = lt, lt | eq, lt.T
    M[:, 3], M[:, 4], M[:, 5] = gt, gt | eq, gt.T
    m["rw_M"] = M.reshape(128, -1)
    m["identf"] = np.eye(128, dtype=np.float32)
    return m


def phase_R(nc, S, dr):
    for d in (0, 1):
        rwkv_pass(nc, S, dr, d)


def rwkv_pass(nc, S, dr, d):
    ph = Ph(nc, S)
    sb, ps = ph.sb, ph.ps
    op = S.op
    vec = sb("R_vec", [128, 9, 256])
    S.dma_in(vec, lambda h: h.dma_start(out=vec[:, :, :], in_=dr["rw_vec"].ap().rearrange("p (a b) -> p a b", b=256)))
    W0F, W0B, A0F, A0B, KK_, KA_, RK_, LNW, LNB = range(9)
    wl = load_w(S, ph, "R_wl", dr["w_lora"], 1, 1024)
    wg2 = load_w(S, ph, "R_wg2", dr["w_g2"], 2, 256)
    mur = sb("R_mur", [128, 6, 2]); mus = sb("R_mus", [128, 6, 2])
    S.dma_in(mur, lambda h: h.dma_start(out=mur[:, :, :], in_=dr["mu_rkv"].ap().rearrange("p (a b) -> p a b", b=2)))
    S.dma_in(mus, lambda h: h.dma_start(out=mus[:, :, :], in_=dr["mu_sh"].ap().rearrange("p (a b) -> p a b", b=2)))
    c0r = sb("R_c0r", [128, 6]); c0s = sb("R_c0s", [128, 6])
    for (c0t, mut) in ((c0r, mur), (c0s, mus)):
        op("dve", lambda h, c0t=c0t, mut=mut: h.tensor_tensor(out=c0t[:, :], in0=mut[:, :, 0], in1=mut[:, :, 1], op=ALU.add), R=[mut], W=[c0t])
        op("dve", lambda h, c0t=c0t: h.tensor_scalar(out=c0t[:, :], in0=c0t[:, :], scalar1=-1.0, scalar2=1.0, op0=ALU.mult, op1=ALU.add), W=[c0t])
    Mf = sb("R_M", [128, 768])
    S.dma_in(Mf, lambda h: h.dma_start(out=Mf[:, :], in_=dr["rw_M"][:, :]))
    MS, MI, MST = slice(384 * d, 384 * d + 128), slice(384 * d + 128, 384 * d + 256), slice(384 * d + 256, 384 * d + 384)
    MSI = slice(384 * d, 384 * d + 256)
    TL = 63 if d == 0 else 0
    identf = sb("R_idf", [128, 128])
    S.dma_in(identf, lambda h: h.dma_start(out=identf[:, :], in_=dr["identf"][:, :]))
    identb = sb("R_idb", [128, 128], BF)
    op("pool", lambda h: h.tensor_copy(out=identb[:, :], in_=identf[:, :]), R=[identf], W=[identb])
    onesf = sb("R_onesf", [128, 2])
    op("pool", lambda h: h.memset(onesf[:, :], 1.0), W=[onesf])
    uf = [sb(f"R_uf{i}", [128, 6, 514], BF) for i in range(2)]
    lf = [sb(f"R_lf{i}", [128, 6, 514], BF) for i in range(2)]
    usr = sb("R_usr", [128, 6, 512], BF)
    lor = sb("R_lor", [128, 6, 512], BF)
    tA = [sb(f"R_tA{i}", [128, 512]) for i in range(2)]
    rkvT = sb("R_rkvT", [128, 768], BF)
    F = lambda n: sb(n, [128, 256])
    kkr, sqk, kk = F("R_kkr"), F("R_sqk"), F("R_kk")
    ssum = sb("R_ssum", [128, 4]); nrm = sb("R_nrm", [128, 4]); rn = sb("R_rn", [128, 4])
    zz, sg, al, alo = F("R_zz"), F("R_sg"), F("R_al"), F("R_alo")
    Ep, Em, Ea, Ed = F("R_Ep"), F("R_Em"), F("R_Ea"), F("R_Ed")
    t1, kd, bd = F("R_t1"), F("R_kd"), F("R_bd")
    gC = sb("R_gC", [128, 4])
    tm = sb("R_tm", [128, 6, 256], BF)
    cm = sb("R_cm", [128, 2, 512], BF)
    AB = [sb(f"R_AB{i}", [128, 256], BF) for i in range(2)]
    AK = [sb(f"R_AK{i}", [128, 256], BF) for i in range(2)]
    NTt = [sb(f"R_NT{i}", [128, 128], BF) for i in range(2)]
    Xs = [[sb(f"R_X{b}{i}", [128, 128], BF) for i in range(5)] for b in range(2)]
    XTs = [[sb(f"R_XT{b}{i}", [128, 128], BF) for i in range(4)] for b in range(2)]
    Zt = [sb(f"R_Z{i}", [128, 128], BF) for i in range(3)]
    Pm = [sb(f"R_Pm{i}", [128, 64], BF) for i in range(2)]
    Gt = [sb(f"R_G{i}", [128, 64], BF) for i in range(2)]
    st = [sb(f"R_st{h}", [128, 64], BF) for h in range(4)]
    Yt = sb("R_Yt", [128, 256])
    if d == 1:
        yfl, ysum, yn, yo = F("R_yfl"), F("R_ysum"), F("R_yn"), F("R_yo")
        stats = sb("R_stats", [128, 4, 6]); mv = sb("R_mv", [128, 4, 2])
        sd = sb("R_sd", [128, 4]); rsd = sb("R_rsd", [128, 4]); bs = sb("R_bs", [128, 4])
        ybt = sb("R_ybt", [128, 2, 128], BF)
    bT = ps("R_pT", [128, 512])
    bT2 = ps("R_pT2", [128, 512])
    bZ = ps("R_pZ", [128, 512])
    bC = ps("R_pC", [128, 512])
    bD = ps("R_pD", [128, 512])
    bE = ps("R_pE", [128, 512])
    bF_ = ps("R_pF", [128, 512])
    bG = ps("R_pG", [128, 512])
    sub = lambda par, a, b, nm: S.tile(par[:, a:b], nm, parent=par)
    pF = [sub(bF_, 0, 128, "pF0"), sub(bT, 0, 128, "pF1"), sub(bT2, 0, 128, "pF2"), sub(bC, 0, 128, "pF3")]
    pP = [sub(bG, 0, 64, "pP0"), sub(bE, 0, 64, "pP1")]
    pGm = [sub(bG, 64, 128, "pG0"), sub(bE, 64, 128, "pG1")]
    pS = [sub(bZ, 0, 64, "pS0"), sub(bD, 256, 320, "pS1")]
    pYs = [sub(bZ, 64, 128, "pY0"), sub(bD, 320, 384, "pY1")]
    pz = sub(bF_, 128, 256, "pz")
    ev = {"n": 0}

    act_banks = {id(bT), id(bT2), id(bC), id(bD)}

    def evac(out_t, out_ap, in_t, in_ap):
        if id(in_t.root) in act_banks:
            op("act", lambda h: h.activation(out=out_ap, in_=in_ap, func=AF.Copy), R=[in_t], W=[out_t])
        else:
            op("dve", lambda h: h.tensor_copy(out=out_ap, in_=in_ap), R=[in_t], W=[out_t])

    def mm(out_t, out_ap, l_t, l_ap, r_t, r_ap, start=True, stop=True):
        op("pe", lambda h: h.matmul(out_ap, l_ap, r_ap, start=start, stop=stop), R=[l_t, r_t], W=[out_t])

    def tt(e, out_t, out_ap, a_t, a_ap, b_t, b_ap, o):
        op(e, lambda h: h.tensor_tensor(out=out_ap, in0=a_ap, in1=b_ap, op=o), R=[a_t, b_t], W=[out_t])

    def stt(out_t, out_ap, a_t, a_ap, scalar, b_t, b_ap, o0, o1, extra=()):
        op("dve", lambda h: h.scalar_tensor_tensor(out=out_ap, in0=a_ap, scalar=scalar, in1=b_ap, op0=o0, op1=o1), R=[a_t, b_t, *extra], W=[out_t])

    def act(out_t, out_ap, in_t, in_ap, func, scale=1.0, bias=0.0):
        op("act", lambda h: h.activation(out=out_ap, in_=in_ap, func=func, scale=scale, bias=bias), R=[in_t], W=[out_t])

    rk = dr["rkv_s"].ap().rearrange("(k p) t -> p k t", p=128)
    shg = dr["sh_g"]
    seq_starts = {s0 for s0, L in SEQS}
    seq_ends = {s0 + L for s0, L in SEQS}

    def load_block(tb):
        t0 = tb * TB
        u, l = uf[tb % 2], lf[tb % 2]
        lo = 0 if t0 in seq_starts else -1
        hi = 0 if (t0 + TB) in seq_ends else 1
        csl = slice(1 + lo, 513 + hi)
        tsl = slice(t0 + lo, t0 + TB + hi)
        if lo == 0:
            op("pool", lambda h: h.memset(u[:, :, 0:1], 0.0), W=[u])
            op("pool", lambda h: h.memset(l[:, :, 0:1], 0.0), W=[l])
        if hi == 0:
            op("pool", lambda h: h.memset(u[:, :, 513:514], 0.0), W=[u])
            op("pool", lambda h: h.memset(l[:, :, 513:514], 0.0), W=[l])
        S.dma_in(u, lambda h: h.dma_start(out=u[:, :, csl], in_=rk[:, :, tsl]))
        for j, ch in enumerate(LORA_CH):
            r0 = sh_rows(ch)
            S.dma_in(l, lambda h, j=j, r0=r0: h.dma_start(out=l[:, j, csl], in_=shg[r0:r0 + 128, tsl]))

    def shift(src, j, c0t, mut, dst_t, dst_ap, func=None):
        a = tA[j % 2]
        op("pool", lambda h: h.tensor_scalar(out=a[:, :], in0=src[:, j, 1:513], scalar1=c0t[:, j:j + 1], scalar2=None, op0=ALU.mult), R=[src, c0t], W=[a])
        stt(a, a[:, :], src, src[:, j, 0:512], mut[:, j, 0:1], a, a[:, :], ALU.mult, ALU.add, extra=(mut,))
        if func is None:
            stt(dst_t, dst_ap, src, src[:, j, 2:514], mut[:, j, 1:2], a, a[:, :], ALU.mult, ALU.add, extra=(mut,))
        else:
            stt(a, a[:, :], src, src[:, j, 2:514], mut[:, j, 1:2], a, a[:, :], ALU.mult, ALU.add, extra=(mut,))
            act(dst_t, dst_ap, a, a[:, :], func)

    blocks = list(range(NB)) if d == 0 else list(range(NB - 1, -1, -1))
    load_block(blocks[0])
    ui = 0
    for bi, tb in enumerate(blocks[:RNBLK]):
        t0 = tb * TB
        if bi + 1 < NB:
            load_block(blocks[bi + 1])
        u, l = uf[tb % 2], lf[tb % 2]
        for j in range(6):
            shift(u, j, c0r, mur, usr, usr[:, j, :])
        for j in range(6):
            shift(l, j, c0s, mus, lor, lor[:, j, :], func=(AF.Tanh if j < 2 else (None if j < 4 else AF.Sigmoid)))
        if (d == 0 and t0 in seq_starts) or (d == 1 and (t0 + TB) in seq_ends):
            for h_ in range(4):
                op("pool", lambda h, h_=h_: h.memset(st[h_][:, :], 0.0), W=[st[h_]])
        tiles = list(range(4)) if d == 0 else [3, 2, 1, 0]
        if RSTAGE < 2:
            tiles = []
        for tj in tiles:
            ksl = slice(tj * 128, (tj + 1) * 128)
            g0_ = t0 + tj * 128
            for j in range(6):
                bt_ = bT if j < 4 else bT2
                mm(bt_, bt_[:, (j % 4) * 128:(j % 4 + 1) * 128], usr, usr[:, j, ksl], identb, identb[:, :])
            evac(rkvT, rkvT[:, 0:512], bT, bT[:, :])
            evac(rkvT, rkvT[:, 512:768], bT2, bT2[:, 0:256])
            r_ap, k_ap, v_ap = rkvT[:, 0:256], rkvT[:, 256:512], rkvT[:, 512:768]
            tt("dve", kkr, kkr[:, :], rkvT, k_ap, vec, vec[:, KK_, :], ALU.mult)
            act(sqk, sqk[:, :], kkr, kkr[:, :], AF.Square)
            op("dve", lambda h: h.tensor_reduce(out=ssum[:, :], in_=sqk[:, :].rearrange("p (a b) -> p a b", b=64), axis=AX.X, op=ALU.add), R=[sqk], W=[ssum])
            act(nrm, nrm[:, :], ssum, ssum[:, :], AF.Sqrt)
            op("dve", lambda h: h.tensor_scalar(out=nrm[:, :], in0=nrm[:, :], scalar1=1e-12, scalar2=None, op0=ALU.max), W=[nrm])
            op("dve", lambda h: h.reciprocal(out=rn[:, :], in_=nrm[:, :]), R=[nrm], W=[rn])
            for h_ in range(4):
                op("dve", lambda h, h_=h_: h.tensor_scalar(out=kk[:, h_ * 64:(h_ + 1) * 64], in0=kkr[:, h_ * 64:(h_ + 1) * 64], scalar1=rn[:, h_:h_ + 1],
                                                         scalar2=None, op0=ALU.mult), R=[kkr, rn], W=[kk])
            if RSTAGE < 3:
                continue
            mm(bZ, bZ[:, 0:256], lor, lor[:, d, ksl], wl, wl[:, 0, d * 256:(d + 1) * 256])
            mm(bZ, bZ[:, 256:512], lor, lor[:, 2 + d, ksl], wl, wl[:, 0, (2 + d) * 256:(3 + d) * 256])
            tt("dve", zz, zz[:, :], bZ, bZ[:, 0:256], vec, vec[:, W0F + d, :], ALU.add)
            act(sg, sg[:, :], zz, zz[:, :], AF.Sigmoid)
            tt("dve", zz, zz[:, :], bZ, bZ[:, 256:512], vec, vec[:, A0F + d, :], ALU.add)
            act(al, al[:, :], zz, zz[:, :], AF.Sigmoid)
            if d == 1:
                mm(bZ, bZ[:, 0:256], lor, lor[:, 2, ksl], wl, wl[:, 0, 512:768])
                tt("dve", zz, zz[:, :], bZ, bZ[:, 0:256], vec, vec[:, A0F, :], ALU.add)
                act(alo, alo[:, :], zz, zz[:, :], AF.Sigmoid)
            mm(bC, bC[:, 0:256], Mf, Mf[:, MI], sg, sg[:, :])
            mm(bC, bC[:, 256:512], Mf, Mf[:, MS], sg, sg[:, :])
            mm(bD, bD[:, 0:256], Mf, Mf[:, MST], sg, sg[:, :])
            act(Ep, Ep[:, :], bC, bC[:, 0:256], AF.Exp, scale=-CDEC)
            for c in range(2):
                pc = 64 * c
                for h_ in range(4):
                    mm(bD, bD[pc:pc + 64, 256 + 64 * h_:256 + 64 * (h_ + 1)], Ep, Ep[pc:pc + 64, h_ * 64:(h_ + 1) * 64], identf, identf[pc:pc + 64, pc:pc + 64])
            act(Em, Em[:, :], bC, bC[:, 0:256], AF.Exp, scale=CDEC)
            act(Ea, Ea[:, :], bC, bC[:, 256:512], AF.Exp, scale=-CDEC)
            act(Ed, Ed[:, :], bD, bD[:, 0:256], AF.Exp, scale=-CDEC)
            op("act", lambda h: h.activation(out=gC[:, :], in_=bD[:, 256:512].rearrange("p (a b) -> p a b", b=64)[:, :, TL], func=AF.Copy), R=[bD], W=[gC])
            stt(t1, t1[:, :], al, al[:, :], -1.0, vec, vec[:, KA_, :], ALU.add, ALU.mult)
            stt(kd, kd[:, :], t1, t1[:, :], 1.0, rkvT, k_ap, ALU.add, ALU.mult)
            tt("pool", bd, bd[:, :], kk, kk[:, :], al, al[:, :], ALU.mult)
            tt("pool", tm, tm[:, 0, :], rkvT, r_ap, Ep, Ep[:, :], ALU.mult)
            tt("dve", tm, tm[:, 1, :], kd, kd[:, :], Em, Em[:, :], ALU.mult)
            tt("pool", tm, tm[:, 2, :], bd, bd[:, :], Em, Em[:, :], ALU.mult)
            stt(tm, tm[:, 3, :], kk, kk[:, :], -1.0, Ea, Ea[:, :], ALU.mult, ALU.mult)
            tt("dve", tm, tm[:, 4, :], kd, kd[:, :], Ed, Ed[:, :], ALU.mult)
            tt("pool", tm, tm[:, 5, :], bd, bd[:, :], Ed, Ed[:, :], ALU.mult)
            for hp in range(2):
                bt_ = bT if hp == 0 else bT2
                for s_, xi in enumerate((3, 0, 2, 1)):
                    mm(bt_, bt_[:, s_ * 128:(s_ + 1) * 128], tm, tm[:, xi, hp * 128:(hp + 1) * 128], identb, identb[:, :])
            evac(cm, cm[:, 0, :], bT, bT[:, :])
            evac(cm, cm[:, 1, :], bT2, bT2[:, :])
            for h_ in range(4 if RSTAGE >= 4 else 0):
                hp, hb = h_ // 2, 64 * (h_ % 2)
                hs = slice(h_ * 64, (h_ + 1) * 64)
                ab, ak, ntt = AB[ui % 2], AK[ui % 2], NTt[ui % 2]
                xs, xts = Xs[ui % 2], XTs[ui % 2]
                ui += 1
                arT = cm[hb:hb + 64, hp, 0:256]
                bTa = cm[hb:hb + 64, hp, 256:384]
                kTa = cm[hb:hb + 64, hp, 384:512]
                aTa = cm[hb:hb + 64, hp, 0:128]
                mm(bE, bE[:, 0:256], cm, bTa, cm, arT)
                tt("dve", ab, ab[:, :], bE, bE[:, 0:256], Mf, Mf[:, MSI], ALU.mult)
                mm(bE, bE[:, 256:512], cm, kTa, cm, arT)
                tt("dve", ak, ak[:, :], bE, bE[:, 256:512], Mf, Mf[:, MSI], ALU.mult)
                mm(pF[0], pF[0][:, :], cm, aTa, cm, bTa)
                tt("dve", ntt, ntt[:, :], pF[0], pF[0][:, :], Mf, Mf[:, MST], ALU.mult)
                X, XT = (ab, ab[:, 0:128]), (ntt, ntt[:, :])
                Xl = [X]
                for i in range(5):
                    pa, pb = pF[(2 * i + 1) % 4], pF[(2 * i + 2) % 4]
                    mm(pa, pa[:, :], XT[0], XT[1], X[0], X[1])
                    if i < 4:
                        mm(pb, pb[:, :], X[0], X[1], XT[0], XT[1])
                    evac(xs[i], xs[i][:, :], pa, pa[:, :])
                    if i < 4:
                        evac(xts[i], xts[i][:, :], pb, pb[:, :])
                        XT = (xts[i], xts[i][:, :])
                    X = (xs[i], xs[i][:, :])
                    Xl.append(X)
                z = Zt[0]
                op("pool", lambda h, z=z, hs=hs: h.tensor_copy(out=z[:, 0:64], in_=tm[:, 3, hs]), R=[tm], W=[z])
                vh = rkvT[:, 512 + h_ * 64:512 + (h_ + 1) * 64]
                mm(pz, pz[:, 0:64], ak, ak[:, 0:128], rkvT, vh)
                evac(z, z[:, 64:128], pz, pz[:, 0:64])
                zi = 0
                for i in range(6):
                    pa = pF[(i + 3) % 4]
                    zn = Zt[(zi + 1) % 3]
                    mm(pa, pa[:, :], identb, identb[:, :], z, z[:, :], start=True, stop=False)
                    mm(pa, pa[:, :], Xl[i][0], Xl[i][1], z, z[:, :], start=False, stop=True)
                    evac(zn, zn[:, :], pa, pa[:, :])
                    z = zn
                    zi += 1
                for c in (((0, 1) if d == 0 else (1, 0)) if RSTAGE >= 5 else ()):
                    pc, po = 64 * c, 64 * (1 - c)
                    cs_ = slice(pc, pc + 64)
                    os_ = slice(po, po + 64)
                    Bh = tm[cs_, 5, hs]
                    Kh = tm[cs_, 4, hs]
                    Rt = tm[cs_, 0, hs]
                    pm, gt_ = Pm[c], Gt[c]
                    mm(pP[c], pP[c][cs_, :], z, z[cs_, 0:64], tm, Bh)
                    stt(pm, pm[cs_, :], identf, identf[cs_, pc:pc + 64], gC[cs_, h_:h_ + 1], pP[c], pP[c][cs_, :], ALU.mult, ALU.add, extra=(gC,))
                    mm(pGm[c], pGm[c][cs_, :], z, z[cs_, 0:64], ab, ab[cs_, 128 + pc:128 + pc + 64], start=True, stop=False)
                    mm(pGm[c], pGm[c][cs_, :], tm, Rt, identb, identb[cs_, pc:pc + 64], start=False, stop=True)
                    evac(gt_, gt_[cs_, :], pGm[c], pGm[c][cs_, :])
                    pY = pYs[c]
                    mm(pY, pY[cs_, :], gt_, gt_[cs_, :], st[h_], st[h_][cs_, :], start=True, stop=False)
                    mm(pY, pY[cs_, :], ab, ab[cs_, 128 + pc:128 + pc + 64], z, z[cs_, 64:128], start=False, stop=False)
                    mm(pY, pY[cs_, :], ak, ak[cs_, 128 + pc:128 + pc + 64], rkvT, rkvT[cs_, 512 + h_ * 64:512 + (h_ + 1) * 64], start=False, stop=True)
                    evac(Yt, Yt[cs_, hs], pY, pY[cs_, :])
                    mm(pS[c], pS[c][os_, :], tm, Bh, z, z[cs_, 64:128], start=True, stop=False)
                    mm(pS[c], pS[c][os_, :], tm, Kh, rkvT, rkvT[cs_, 512 + h_ * 64:512 + (h_ + 1) * 64], start=False, stop=False)
                    mm(pS[c], pS[c][os_, :], pm, pm[cs_, :], st[h_], st[h_][cs_, :], start=False, stop=True)
                    evac(st[h_], st[h_][os_, :], pS[c], pS[c][os_, :])
            tok = slice(g0_, g0_ + 128)
            if RSTAGE < 6:
                continue
            if d == 0:
                S.dma_out(Yt, lambda h, tok=tok: h.dma_start(out=dr["yf_s"][tok, :], in_=Yt[:, :]))
            else:
                S.dma_in(yfl, lambda h, tok=tok: h.dma_start(out=yfl[:, :], in_=dr["yf_s"][tok, :]))
                tt("dve", ysum, ysum[:, :], Yt, Yt[:, :], yfl, yfl[:, :], ALU.add)
                for h_ in range(4):
                    hs = slice(h_ * 64, (h_ + 1) * 64)
                    op("dve", lambda h, h_=h_, hs=hs: h.bn_stats(out=stats[:, h_, :], in_=ysum[:, hs]), R=[ysum], W=[stats])
                    op("dve", lambda h, h_=h_: h.bn_aggr(out=mv[:, h_, :], in_=stats[:, h_, :]), R=[stats], W=[mv])
                act(sd, sd[:, :], mv, mv[:, :, 1], AF.Sqrt, bias=GN_EPS)
                op("dve", lambda h: h.reciprocal(out=rsd[:, :], in_=sd[:, :]), R=[sd], W=[rsd])
                for h_ in range(4):
                    hs = slice(h_ * 64, (h_ + 1) * 64)
                    op("dve", lambda h, h_=h_, hs=hs: h.tensor_scalar(out=yn[:, hs], in0=ysum[:, hs], scalar1=mv[:, h_, 0:1], scalar2=rsd[:, h_:h_ + 1],
                                                                    op0=ALU.subtract, op1=ALU.mult), R=[ysum, mv, rsd], W=[yn])
                tt("pool", yn, yn[:, :], yn, yn[:, :], vec, vec[:, LNW, :], ALU.mult)
                tt("pool", yn, yn[:, :], yn, yn[:, :], vec, vec[:, LNB, :], ALU.add)
                tt("dve", t1, t1[:, :], al, al[:, :], alo, alo[:, :], ALU.add)
                stt(t1, t1[:, :], t1, t1[:, :], -2.0, vec, vec[:, KA_, :], ALU.add, ALU.mult)
                stt(kd, kd[:, :], t1, t1[:, :], 2.0, rkvT, k_ap, ALU.add, ALU.mult)
                tt("dve", kd, kd[:, :], kd, kd[:, :], rkvT, r_ap, ALU.mult)
                tt("dve", kd, kd[:, :], kd, kd[:, :], vec, vec[:, RK_, :], ALU.mult)
                op("dve", lambda h: h.tensor_reduce(out=bs[:, :], in_=kd[:, :].rearrange("p (a b) -> p a b", b=64), axis=AX.X, op=ALU.add), R=[kd], W=[bs])
                for h_ in range(4):
                    hs = slice(h_ * 64, (h_ + 1) * 64)
                    stt(yn, yn[:, hs], rkvT, rkvT[:, 512 + h_ * 64:512 + (h_ + 1) * 64], bs[:, h_:h_ + 1], yn, yn[:, hs], ALU.mult, ALU.add, extra=(bs,))
                mm(bZ, bZ[:, 0:256], lor, lor[:, 4, ksl], wg2, wg2[:, 0, :], start=True, stop=False)
                mm(bZ, bZ[:, 0:256], lor, lor[:, 5, ksl], wg2, wg2[:, 1, :], start=False, stop=True)
                tt("dve", yo, yo[:, :], yn, yn[:, :], bZ, bZ[:, 0:256], ALU.mult)
                for cc in range(2):
                    pq = pF[cc]
                    op("pe", lambda h, cc=cc, pq=pq: h.transpose(pq[:, :], yo[:, cc * 128:(cc + 1) * 128], identf[:, :]), R=[yo, identf], W=[pq])
                    evac(ybt, ybt[:, cc, :], pq, pq[:, :])
                S.dma_out(ybt, lambda h, tok=tok: h.dma_start(out=dr["y_b"].ap()[256:512, tok].rearrange("(c p) t -> p c t", p=128), in_=ybt[:, :, :]))
    ph.end()


DBG = []
PHASES = "MRCDEGHI"


def build_nc(with_rwkv=True):
    nc = bass.Bass("TRN2", target_bir_lowering=False)
    dr = {}

    def ein(name, shape, dt=F32):
        dr[name] = nc.dram_tensor(name, shape, dt, kind="ExternalInput")

    def scr(name, shape, dt):
        dr[name] = nc.dram_tensor(name, shape, dt)
    ein("xT", [D, NT]); ein("xTs", [256, NT]); ein("w_inA", [D, 13 * 128]); ein("g0", [128, KC])
    ein("w_uq", [768, 512]); ein("w_ukv", [512, 512]); ein("qng", [128, 6]); ein("kvg", [128, 4])
    ein("ident", [128, 128], F32); ein("cos2", [64, 8192]); ein("sin2", [64, 8192])
    ein("w_br", [4096, 256]); ein("w_o", [D, 256]); ein("w_up", [D, 1024]); ein("w_dn", [8192, 256]); ein("gn", [128, 6])
    for k, shp in RW_INPUTS.items():
        ein(k, shp)
    dr["yT"] = nc.dram_tensor("yT", [256, NT], F32, kind="ExternalOutput")
    scr("sh_b", [384, NT], BF); scr("sh_g", [8 * 384, NT], BF)
    scr("rkv_s", [768, NT], BF); scr("gate_s", [512, NT], BF)
    scr("y_b", [512, NT], BF); scr("y_g", [8 * 512, NT], BF)
    scr("mix_b", [256, NT], BF); scr("mix_g", [D, NT], BF)
    scr("mix_s", [256, NT], F32); scr("x1_s", [256, NT], F32); scr("ff_s", [256, NT], F32)
    scr("h2_b", [256, NT], BF); scr("h2_g", [D, NT], BF)
    scr("hid_b", [1024, NT], BF); scr("hid_g", [8192, NT], BF)
    scr("yf_s", [NT, 256], F32)
    for i in (1, 2, 3):
        scr(f"ss{i}_b", [1, NT], F32); scr(f"ss{i}_g", [8, NT], F32)
    with contextlib.ExitStack() as gs:
        S = Sched(nc, gs)
        phase_A(nc, S, dr)
        if "M" in PHASES:
            phase_M(nc, S, dr)
        if "R" in PHASES:
            phase_R(nc, S, dr)
        ph = Ph(nc, S)
        nop = ph.sb("X_nop", [128, 8])
        S.op("pool", lambda h: h.memset(nop[:, :], 0.0), W=[nop])
        ph.end(allgather(nc, dr["y_b"], dr["y_g"]))
        for nm, fn in (("C", phase_C), ("D", phase_D), ("E", phase_E), ("G", phase_G), ("H", phase_H), ("I", phase_I)):
            if nm in PHASES:
                fn(nc, S, dr)
        if DBG:
            ph = Ph(nc, S)
            dt_ = ph.sb("dbg_t", [128, 8])
            for nm in DBG:
                src = dr[nm]
                dst = nc.dram_tensor("dbg_" + nm, list(src.shape), src.dtype, kind="ExternalOutput")
                for r0 in range(0, src.shape[0], 128):
                    r1 = min(r0 + 128, src.shape[0])
                    for c0 in range(0, src.shape[1], 4096):
                        c1 = min(c0 + 4096, src.shape[1])
                        S.dma_out(dt_, lambda h, src=src, dst=dst, r0=r0, r1=r1, c0=c0, c1=c1: h.dma_start(out=dst[r0:r1, c0:c1], in_=src[r0:r1, c0:c1]))
            ph.end()
    return nc


def _cols(a, n=128):
    return np.ascontiguousarray(a.reshape(-1, 128).T.astype(np.float32))


def kernel(**inp):
    f = lambda k: np.asarray(inp[k], dtype=np.float32)
    x = np.concatenate([f("x_prompt").reshape(-1, D), f("x_sample").reshape(-1, D)], 0)
    xT = np.ascontiguousarray(x.T)
    w_in = f("w_in")[0]
    pad = lambda cols, n=128: list(cols) + [-1] * (n - len(cols))
    sh = [list(range(i * 128, (i + 1) * 128)) for i in range(10)]
    sh.append(pad(range(1280, 1344)))
    sh += [pad(range(U0 + 6144 + 96 * j, U0 + 6144 + 96 * (j + 1))) for j in range(4)]
    sh += [list(range(U0 + 6528, U0 + 6656)), list(range(U0 + 6656, U0 + 6784))]
    sh.append(pad(list(range(1312, 1344)) + list(range(1280, 1312))))
    w_pad = np.concatenate([w_in, np.zeros((D, 1), np.float32)], 1)
    inv = 1.0 / (10000.0 ** (np.arange(0, 64, 2, dtype=np.float32) / 64))
    ang = np.arange(8192, dtype=np.float32)[:, None] * inv[None, :]
    cos, sin = np.cos(ang).T.astype(np.float32), np.sin(ang).T.astype(np.float32)
    cos2 = np.ascontiguousarray(np.concatenate([cos, cos], 0))
    sin2 = np.ascontiguousarray(np.concatenate([-sin, sin], 0))
    import ml_dtypes
    ident = np.eye(128, dtype=np.float32)
    w_uq, w_ukv = f("mla_w_uq")[0], f("mla_w_ukv")[0]
    w_br, w_o, w_up, w_dn = f("w_branch")[0], f("w_out")[0], f("w_mlp_up")[0], f("w_mlp_down")[0]
    in_maps = []
    for c in range(NCORES):
        m = {"xT": xT, "xTs": np.ascontiguousarray(xT[256 * c:256 * (c + 1)])}
        cols = []
        for slot in range(3):
            i = slot * 8 + c
            cols += sh[i] if i < NSH else [-1] * 128
        for base in (0, 2048, 4096):
            cols += list(range(U0 + base + 256 * c, U0 + base + 256 * (c + 1)))
        cols += list(range(G0 + 256 * c, G0 + 256 * (c + 1))) + list(range(G0 + 2048 + 256 * c, G0 + 2048 + 256 * (c + 1)))
        m["w_inA"] = np.ascontiguousarray(w_pad[:, cols])
        m["g0"] = _cols(f("norm_pre_mix")[0])
        qc = []
        for hd in (2 * c, 2 * c + 1):
            b = hd * 192
            qc += list(range(b, b + 192)) + list(range(b + 160, b + 192)) + list(range(b + 128, b + 160))
        m["w_uq"] = np.ascontiguousarray(w_uq[:, qc])
        m["w_ukv"] = np.ascontiguousarray(w_ukv[:, 512 * c:512 * (c + 1)])
        m["qng"] = _cols(f("mla_q_norm")[0]); m["kvg"] = _cols(f("mla_kv_norm")[0])
        m["ident"] = ident; m["cos2"] = cos2; m["sin2"] = sin2
        rows = []
        for r in range(8):
            rows += list(range(256 * r, 256 * (r + 1))) + list(range(2048 + 256 * r, 2048 + 256 * (r + 1)))
        m["w_br"] = np.ascontiguousarray(w_br[rows][:, 256 * c:256 * (c + 1)])
        m["w_o"] = np.ascontiguousarray(w_o[:, 256 * c:256 * (c + 1)])
        m["w_up"] = np.ascontiguousarray(w_up[:, 1024 * c:1024 * (c + 1)])
        m["w_dn"] = np.ascontiguousarray(w_dn[:, 256 * c:256 * (c + 1)])
        sl = slice(256 * c, 256 * (c + 1))
        m["gn"] = np.ascontiguousarray(np.concatenate([_cols(f("norm_post_mix")[0][sl]), _cols(f("norm_pre_mlp")[0][sl]),
                                                       _cols(f("norm_post_mlp")[0][sl])], 1))
        m.update(rw_host(inp, c))
        in_maps.append(m)
    nc = build_nc()
    res = run_bass_kernel_spmd(nc, in_maps, core_ids=list(range(NCORES)))
    global LAST
    LAST = res
    yT = np.concatenate([res.results[c]["yT"] for c in range(NCORES)], 0)
    y = np.ascontiguousarray(yT.T).astype(np.float32)
    return (y[:8192].reshape(1, 8192, D), y[8192:].reshape(4, 2048, D))
```

```python
import numpy as np, contextlib
import concourse.bass as bass
import concourse.mybir as mybir

F32 = mybir.dt.float32
BF = mybir.dt.bfloat16
AF = mybir.ActivationFunctionType
ALU = mybir.AluOpType
AX = mybir.AxisListType


PSUM_EXCL = False
SELF_WAITS = True


class T:
    __slots__ = ("ap", "w", "r", "ds", "name", "root", "psum")

    def __init__(self, ap, name=""):
        self.ap = ap
        self.root = self
        self.psum = False
        self.w = []
        self.r = {}
        self.ds = None
        self.name = name

    def __getitem__(self, k):
        return self.ap[k]


class Sched:
    ENG = ("pe", "act", "dve", "pool", "sp")
    EPOCH = 30000

    def __init__(self, nc, gstack):
        self.nc = nc
        self.gstack = gstack
        self.h = {"pe": nc.tensor, "act": nc.scalar, "dve": nc.vector, "pool": nc.gpsimd, "sp": nc.sync}
        self.rec = {k: [] for k in self.ENG}
        self.base = {k: 0 for k in self.ENG}
        self.sems = {k: [gstack.enter_context(nc.semaphore(f"s_{k}0"))] for k in self.ENG}
        self.waited = {k: {} for k in self.ENG}
        self.dpool = []
        self.dall = []
        self.nd = 0
        self.cc_sem = gstack.enter_context(nc.semaphore("ccsem"))
        self.ncc = 0
        self.ninstr = 0
        self.tiles = []

    def tile(self, ap, name="", parent=None, psum=False):
        t = T(ap, name)
        t.psum = psum
        if parent is not None:
            t.root = parent.root
            t.psum = parent.root.psum
        self.tiles.append(t)
        return t

    def _dsem(self, t):
        if t.ds is None:
            if self.dpool:
                t.ds = self.dpool.pop()
            else:
                t.ds = [self.gstack.enter_context(self.nc.semaphore(f"d{self.nd}")), 0]
                self.nd += 1
                self.dall.append(t.ds)
        return t.ds

    def release(self, tiles):
        for t in tiles:
            if t.ds is not None:
                self.dpool.append(t.ds)
                t.ds = None

    def _deps(self, e, R, W):
        d = []
        for t in R:
            d.extend(t.w)
        for t in W:
            d.extend(t.w)
            d.extend(t.r.values())
        if e == "pe" or not SELF_WAITS:
            d = [x for x in d if not (x[0] == "e" and x[1] == e)]
        return d

    def _wait(self, e, deps):
        wd = self.waited[e]
        for dep in deps:
            if dep[0] == "e":
                key = dep[1]
                if wd.get(key, -1) < dep[2]:
                    wd[key] = dep[2]
                    self.rec[e].append(["w", dep])
            else:
                key = id(dep[1])
                if wd.get(key, -1) < dep[2]:
                    wd[key] = dep[2]
                    self.rec[e].append(["w", dep])

    def op(self, e, f, R=(), W=()):
        if PSUM_EXCL:
            W = [t.root for t in W] + [t.root for t in R if t.root.psum]
            R = [t.root for t in R if not t.root.psum]
        else:
            W = [t.root for t in W]
            R = [t.root for t in R]
        self._wait(e, self._deps(e, R, W))
        idx = len(self.rec[e])
        self.rec[e].append(["i", f, False])
        dep = ("e", e, idx)
        for t in R:
            t.r[e] = dep
        for t in W:
            t.w = [dep]
            t.r = {}
        return dep

    def dma_in(self, tile, f, q="sp"):
        ds = self._dsem(tile)
        deps = [x for x in self._deps(q, (), (tile,)) if not (x[0] == "d" and x[1] is ds[0])]
        self._wait(q, deps)
        ds[1] += 16
        sem, v = ds[0], ds[1]
        self.rec[q].append(["d", f, sem])
        tile.w = [("d", sem, v)]
        tile.r = {}

    def dma_out(self, tile, f, q="sp"):
        self._wait(q, self._deps(q, (tile,), ()))
        ds = self._dsem(tile)
        ds[1] += 16
        sem, v = ds[0], ds[1]
        self.rec[q].append(["d", f, sem])
        tile.r["dma"] = ("d", sem, v)

    def flush(self, collective=None):
        nc = self.nc
        last = {}
        for e in self.ENG:
            idxs = [i for i, r in enumerate(self.rec[e]) if r[0] == "i"]
            if not idxs:
                self.rec[e].append(["i", (lambda h: h.nop()) if e != "pe" else (lambda h: h.nop()), False])
                idxs = [len(self.rec[e]) - 1]
            last[e] = idxs[-1]
        for e in self.ENG:
            for e2 in self.ENG:
                if e2 != e:
                    self.rec[e].append(["w", ("e", e2, last[e2])])
            for ds in self.dall:
                if ds[1] > 0:
                    self.rec[e].append(["w", ("d", ds[0], ds[1])])
        for e in self.ENG:
            for r in self.rec[e]:
                if r[0] == "w" and r[1][0] == "e":
                    self.rec[r[1][1]][r[1][2]][2] = True
        cntmap = {}
        plan = {}
        for e in self.ENG:
            c = self.base[e]
            ep = len(self.sems[e]) - 1
            pl = []
            for i, r in enumerate(self.rec[e]):
                if r[0] == "i" and r[2]:
                    if c >= self.EPOCH:
                        self.sems[e].append(self.gstack.enter_context(nc.semaphore(f"s_{e}{len(self.sems[e])}")))
                        ep += 1
                        c = 0
                    c += 1
                    cntmap[(e, i)] = (self.sems[e][ep], c)
            self.base[e] = c
        recs = self.rec
        ccs = self.cc_sem

        def emit(e, h):
            for i, r in enumerate(recs[e]):
                if r[0] == "w":
                    dep = r[1]
                    if dep[0] == "e":
                        sem, v = cntmap[(dep[1], dep[2])]
                        h.wait_ge(sem, v)
                    else:
                        h.wait_ge(dep[1], dep[2])
                elif r[0] == "i":
                    ins = r[1](h)
                    if r[2]:
                        sem, v = cntmap[(e, i)]
                        ins.then_inc(sem, 1)
                    self.ninstr += 1
                else:
                    r[1](h).then_inc(r[2], 16)
                    self.ninstr += 1
            if e == "pool" and collective is not None:
                self.ncc += 1
                collective(h).then_inc(ccs)
                h.wait_ge(ccs, self.ncc)

        with nc.Block() as block:
            @block.tensor
            def _(h):
                emit("pe", h)

            @block.scalar
            def _(h):
                emit("act", h)

            @block.vector
            def _(h):
                emit("dve", h)

            @block.gpsimd
            def _(h):
                emit("pool", h)

            @block.sync
            def _(h):
                emit("sp", h)
        self.rec = {k: [] for k in self.ENG}
        self.waited = {k: {} for k in self.ENG}
        for t in self.tiles:
            t.w = []
            t.r = {}
        if collective is not None:
            self.op("pool", lambda h: h.nop())
            self.flush()

from concourse.bass_utils import run_bass_kernel_spmd

NCORES = 8
NT = 16384
TB = 512
NB = NT // TB
D = 2048
KC = D // 128
EPS = 1e-6
SEQS = [(0, 8192), (8192, 2048), (10240, 2048), (12288, 2048), (14336, 2048)]
NSH = 18
U0 = 1344
G0 = 1344 + 6784
SCALE = 192 ** -0.5


class Ph:
    def __init__(s, nc, S):
        s.nc, s.S, s.st = nc, S, contextlib.ExitStack()

    _uid = [0]

    def sb(s, name, shape, dt=F32):
        Ph._uid[0] += 1
        name = f"{name}_{Ph._uid[0]}"
        return s.S.tile(s.st.enter_context(s.nc.sbuf_tensor(name, shape, dt)), name)

    def ps(s, name, shape, dt=F32):
        Ph._uid[0] += 1
        name = f"{name}_{Ph._uid[0]}"
        return s.S.tile(s.st.enter_context(s.nc.psum_tensor(name, shape, dt)), name, psum=True)

    def end(s, collective=None):
        s.S.flush(collective)
        s.S.release(s.S.tiles)
        s.S.tiles = []
        s.st.close()


def allgather(nc, src, dst):
    return lambda h: h.collective_compute("AllGather", ALU.bypass, replica_groups=[list(range(NCORES))],
                                          ins=[src.ap().opt()], outs=[dst.ap().opt()])


def load_w(S, ph, name, wdram, nk, ncols, scal=None, rows=128):
    wb = ph.sb(name, [128, nk, ncols], BF)
    st = [ph.sb(f"{name}_st{i}", [128, ncols], F32) for i in range(2)]
    for kc in range(nk):
        s_ = st[kc % 2]
        S.dma_in(s_, lambda h, s_=s_, kc=kc: h.dma_start(out=s_[:, :], in_=wdram[kc * 128:(kc + 1) * 128, :]))
        if scal is None:
            S.op("pool", lambda h, s_=s_, kc=kc: h.tensor_copy(out=wb[:, kc, :], in_=s_[:, :]), R=[s_], W=[wb])
        else:
            S.op("pool", lambda h, s_=s_, kc=kc: h.tensor_scalar(out=wb[:, kc, :], in0=s_[:, :], scalar1=scal[:, kc:kc + 1],
                                                                scalar2=None, op0=ALU.mult), R=[s_, scal], W=[wb])
    return wb


def rstd_from(S, ph_tiles, ps, n, dim, rs, rstd, npart=128):
    S.op("act", lambda h: h.activation(out=rs[0:npart, 0:n], in_=ps[0:npart, 0:n], func=AF.Sqrt, bias=EPS, scale=1.0 / dim), R=[ps], W=[rs])
    S.op("dve", lambda h: h.reciprocal(out=rstd[0:npart, 0:n], in_=rs[0:npart, 0:n]), R=[rs], W=[rstd])


def phase_A(nc, S, dr):
    NCC = 13
    ph = Ph(nc, S)
    g0 = ph.sb("A_g0", [128, KC])
    S.dma_in(g0, lambda h: h.dma_start(out=g0[:, :], in_=dr["g0"][:, :]))
    wb = load_w(S, ph, "A_wb", dr["w_inA"], KC, NCC * 128, scal=g0)
    ones = ph.sb("A_ones", [128, 128], BF)
    xs = [ph.sb(f"A_xs{i}", [128, KC, TB]) for i in range(2)]
    xb = [ph.sb(f"A_xb{i}", [128, KC, TB], BF) for i in range(2)]
    sq = ph.sb("A_sq", [128, KC, TB], BF)
    rs = ph.sb("A_rs", [128, TB])
    rstd = ph.sb("A_rstd", [128, TB])
    ot = [ph.sb(f"A_ot{i}", [128, TB], BF) for i in range(4)]
    gt = [ph.sb(f"A_gt{i}", [128, TB]) for i in range(2)]
    pss = ph.ps("A_pss", [128, TB])
    pso = [ph.ps(f"A_pso{i}", [128, TB]) for i in range(4)]
    S.op("pool", lambda h: h.memset(ones[:, :], 1.0), W=[ones])
    xT = dr["xT"].ap().rearrange("(kc p) t -> p kc t", p=128)

    def load_x(tb):
        t = xs[tb % 2]
        S.dma_in(t, lambda h: h.dma_start(out=t[:, :, :], in_=xT[:, :, tb * TB:(tb + 1) * TB]))
    load_x(0)
    oi = 0
    for tb in range(NB):
        x, b = xs[tb % 2], xb[tb % 2]
        if tb + 1 < NB:
            load_x(tb + 1)
        S.op("pool", lambda h, x=x, b=b: h.tensor_copy(out=b[:, :, :], in_=x[:, :, :]), R=[x], W=[b])
        S.op("act", lambda h, x=x: h.activation(out=sq[:, :, :], in_=x[:, :, :], func=AF.Square), R=[x], W=[sq])
        for kc in range(KC):
            S.op("pe", lambda h, kc=kc: h.matmul(pss[:, :], ones[:, :], sq[:, kc, :], start=(kc == 0), stop=(kc == KC - 1)),
                 R=[ones, sq], W=[pss])
        rstd_from(S, None, pss, TB, D, rs, rstd)
        for cc in range(NCC):
            p = pso[cc % 4]
            for kc in range(KC):
                S.op("pe", lambda h, p=p, kc=kc, cc=cc, b=b: h.matmul(p[:, :], wb[:, kc, cc * 128:(cc + 1) * 128], b[:, kc, :],
                                                                  start=(kc == 0), stop=(kc == KC - 1)), R=[wb, b], W=[p])
            o = ot[oi % 4]
            oi += 1
            tsl = slice(tb * TB, (tb + 1) * TB)
            if cc < 9:
                S.op("dve", lambda h, p=p, o=o: h.tensor_tensor(out=o[:, :], in0=p[:, :], in1=rstd[:, :], op=ALU.mult), R=[p, rstd], W=[o])
                dst = dr["sh_b"][cc * 128:(cc + 1) * 128, tsl] if cc < 3 else dr["rkv_s"][(cc - 3) * 128:(cc - 2) * 128, tsl]
            else:
                g = gt[cc % 2]
                S.op("dve", lambda h, p=p, g=g: h.tensor_tensor(out=g[:, :], in0=p[:, :], in1=rstd[:, :], op=ALU.mult), R=[p, rstd], W=[g])
                S.op("act", lambda h, g=g, o=o: h.activation(out=o[:, :], in_=g[:, :], func=AF.Sigmoid), R=[g], W=[o])
                dst = dr["gate_s"][(cc - 9) * 128:(cc - 8) * 128, tsl]
            S.dma_out(o, lambda h, o=o, dst=dst: h.dma_start(out=dst, in_=o[:, :]))
    ph.end(allgather(nc, dr["sh_b"], dr["sh_g"]))


def sh_rows(i):
    return (i % 8) * 384 + (i // 8) * 128


def phase_M(nc, S, dr):
    ph = Ph(nc, S)
    qg = ph.sb("M_qg", [128, 6])
    kg = ph.sb("M_kg", [128, 4])
    S.dma_in(qg, lambda h: h.dma_start(out=qg[:, :], in_=dr["qng"][:, :]))
    S.dma_in(kg, lambda h: h.dma_start(out=kg[:, :], in_=dr["kvg"][:, :]))
    wq = load_w(S, ph, "M_wq", dr["w_uq"], 6, 512, scal=qg)
    wkv = load_w(S, ph, "M_wkv", dr["w_ukv"], 4, 512, scal=kg)
    ones = ph.sb("M_ones", [128, 128], BF)
    ident = ph.sb("M_id", [128, 128], F32)
    S.op("pool", lambda h: h.memset(ones[:, :], 1.0), W=[ones])
    S.dma_in(ident, lambda h: h.dma_start(out=ident[:, :], in_=dr["ident"][:, :]))
    LMAX = 8192
    qn = ph.sb("M_qn", [128, LMAX], BF)
    qr = ph.sb("M_qr", [64, LMAX], BF)
    kn = ph.sb("M_kn", [128, LMAX], BF)
    kr = ph.sb("M_kr", [64, LMAX], BF)
    va = ph.sb("M_va", [128, LMAX // 128, 132], BF)
    S.op("pool", lambda h: h.memset(va[:, :, 128:129], 1.0), W=[va])
    cq = ph.sb("M_cq", [128, 6, TB], BF)
    ckv = ph.sb("M_ckv", [128, 4, TB], BF)
    ckvs = ph.sb("M_ckvs", [128, 4, TB], BF)
    kro = ph.sb("M_kro", [64, 2, TB], BF)
    sqq = ph.sb("M_sqq", [128, 6, TB], BF)
    sqk = ph.sb("M_sqk", [128, 4, TB], BF)
    cs = ph.sb("M_cs", [64, 2, TB])
    rs = ph.sb("M_rs", [128, TB])
    rq = ph.sb("M_rq", [128, TB])
    rk = ph.sb("M_rk", [128, TB])
    rsc = ph.sb("M_rsc", [128, 4])
    rkc = ph.sb("M_rkc", [128, 4])
    t1 = ph.sb("M_t1", [64, TB])
    t2 = ph.sb("M_t2", [64, TB])
    pt = [ph.sb(f"M_pt{i}", [128, TB], BF) for i in range(2)]
    on = ph.sb("M_on", [128, 128], F32)
    rinv = ph.sb("M_rinv", [128, 1])
    yt = [ph.sb(f"M_yt{i}", [128, TB], BF) for i in range(2)]
    psA = [ph.ps(f"M_psA{i}", [128, TB]) for i in range(2)]
    psO = [ph.ps(f"M_psO{i}", [128, 512]) for i in range(4)]
    psB = ph.ps("M_psB", [128, TB])
    psT = ph.ps("M_psT", [128, 512], F32)
    shg = dr["sh_g"]
    yi = 0
    for hl in range(2):
        for (s0, L) in SEQS:
            for tb in range(L // TB):
                tsl = slice(s0 + tb * TB, s0 + (tb + 1) * TB)
                lsl = slice(tb * TB, (tb + 1) * TB)
                for i in range(6):
                    S.dma_in(cq, lambda h, i=i, tsl=tsl: h.dma_start(out=cq[:, i, :], in_=shg[sh_rows(i):sh_rows(i) + 128, tsl]))
                for i in range(4):
                    S.dma_in(ckv, lambda h, i=i, tsl=tsl: h.dma_start(out=ckv[:, i, :], in_=shg[sh_rows(6 + i):sh_rows(6 + i) + 128, tsl]))
                S.dma_in(kro, lambda h, tsl=tsl: h.dma_start(out=kro[:, 0, :], in_=shg[sh_rows(10):sh_rows(10) + 64, tsl]))
                S.dma_in(kro, lambda h, tsl=tsl: h.dma_start(out=kro[:, 1, :], in_=shg[sh_rows(17):sh_rows(17) + 64, tsl]))
                S.dma_in(cs, lambda h, lsl=lsl: h.dma_start(out=cs[:, 0, :], in_=dr["cos2"][:, lsl]))
                S.dma_in(cs, lambda h, lsl=lsl: h.dma_start(out=cs[:, 1, :], in_=dr["sin2"][:, lsl]))
                S.op("act", lambda h: h.activation(out=sqq[:, :, :], in_=cq[:, :, :], func=AF.Square), R=[cq], W=[sqq])
                S.op("act", lambda h: h.activation(out=sqk[:, :, :], in_=ckv[:, :, :], func=AF.Square), R=[ckv], W=[sqk])
                p = psA[0]
                for i in range(6):
                    S.op("pe", lambda h, i=i, p=p: h.matmul(p[:, :], ones[:, :], sqq[:, i, :], start=(i == 0), stop=(i == 5)), R=[ones, sqq], W=[p])
                rstd_from(S, None, p, TB, 768, rs, rq)
                p = psA[1]
                for i in range(4):
                    S.op("pe", lambda h, i=i, p=p: h.matmul(p[:, :], ones[:, :], sqk[:, i, :], start=(i == 0), stop=(i == 3)), R=[ones, sqk], W=[p])
                rstd_from(S, None, p, TB, 512, rs, rk)
                for i in range(4):
                    S.op("pool" if i % 2 else "dve", lambda h, i=i: h.tensor_tensor(out=ckvs[:, i, :], in0=ckv[:, i, :], in1=rk[:, :], op=ALU.mult),
                         R=[ckv, rk], W=[ckvs])
                c0 = hl * 256
                p = psA[0]
                for i in range(6):
                    S.op("pe", lambda h, i=i, p=p, c0=c0: h.matmul(p[:, :], wq[:, i, c0:c0 + 128], cq[:, i, :], start=(i == 0), stop=(i == 5)), R=[wq, cq], W=[p])
                S.op("dve", lambda h, p=p, lsl=lsl: h.tensor_tensor(out=qn[:, lsl], in0=p[:, :], in1=rq[:, :], op=ALU.mult), R=[p, rq], W=[qn])
                p = psA[1]
                for i in range(6 if 'q' not in PHASES else 0):
                    S.op("pe", lambda h, i=i, p=p, c0=c0: h.matmul(p[0:64, :], wq[:, i, c0 + 128:c0 + 192], cq[:, i, :], start=(i == 0), stop=(i == 5)), R=[wq, cq], W=[p])
                S.op("dve", lambda h, p=p: h.tensor_tensor(out=t1[:, :], in0=p[0:64, :], in1=cs[:, 0, :], op=ALU.mult), R=[p, cs], W=[t1])
                p = psA[0]
                for i in range(6):
                    S.op("pe", lambda h, i=i, p=p, c0=c0: h.matmul(p[0:64, :], wq[:, i, c0 + 192:c0 + 256], cq[:, i, :], start=(i == 0), stop=(i == 5)), R=[wq, cq], W=[p])
                S.op("dve", lambda h, p=p: h.tensor_tensor(out=t2[:, :], in0=p[0:64, :], in1=cs[:, 1, :], op=ALU.mult), R=[p, cs], W=[t2])
                S.op("pool", lambda h: h.tensor_tensor(out=t1[:, :], in0=t1[:, :], in1=t2[:, :], op=ALU.add), R=[t2], W=[t1])
                S.op("dve", lambda h, lsl=lsl: h.tensor_tensor(out=qr[:, lsl], in0=t1[:, :], in1=rq[0:64, :], op=ALU.mult), R=[t1, rq], W=[qr])
                k0 = hl * 256
                p = psA[1]
                for i in range(4):
                    S.op("pe", lambda h, i=i, p=p, k0=k0: h.matmul(p[:, :], wkv[:, i, k0:k0 + 128], ckvs[:, i, :], start=(i == 0), stop=(i == 3)), R=[wkv, ckvs], W=[p])
                S.op("dve", lambda h, p=p, lsl=lsl: h.tensor_copy(out=kn[:, lsl], in_=p[:, :]), R=[p], W=[kn])
                S.op("dve", lambda h: h.tensor_tensor(out=t1[:, :], in0=kro[:, 0, :], in1=cs[:, 0, :], op=ALU.mult), R=[kro, cs], W=[t1])
                S.op("pool", lambda h: h.tensor_tensor(out=t2[:, :], in0=kro[:, 1, :], in1=cs[:, 1, :], op=ALU.mult), R=[kro, cs], W=[t2])
                S.op("dve", lambda h, lsl=lsl: h.tensor_tensor(out=kr[:, lsl], in0=t1[:, :], in1=t2[:, :], op=ALU.add), R=[t1, t2], W=[kr])
                for j in range(4 if 'v' not in PHASES else 0):
                    p = psO[j]
                    for i in range(4):
                        S.op("pe", lambda h, i=i, j=j, p=p, k0=k0: h.matmul(p[:, 0:128], ckvs[:, i, j * 128:(j + 1) * 128], wkv[:, i, k0 + 128:k0 + 256],
                                                                  start=(i == 0), stop=(i == 3)), R=[wkv, ckvs], W=[p])
                    S.op("dve", lambda h, j=j, p=p, tb=tb: h.tensor_copy(out=va[:, tb * 4 + j, 0:128], in_=p[:, 0:128]), R=[p], W=[va])
            nkb = L // 128
            for qb in range(L // TB):
                qsl = slice(qb * TB, (qb + 1) * TB)
                for kb in range(nkb):
                    ksl = slice(kb * 128, (kb + 1) * 128)
                    p = psA[kb % 2]
                    S.op("pe", lambda h, p=p, ksl=ksl, qsl=qsl: h.matmul(p[:, :], kn[:, ksl], qn[:, qsl], start=True, stop=False), R=[kn, qn], W=[p])
                    S.op("pe", lambda h, p=p, ksl=ksl, qsl=qsl: h.matmul(p[:, :], kr[:, ksl], qr[:, qsl], start=False, stop=True), R=[kr, qr], W=[p])
                    e = pt[kb % 2]
                    S.op("act", lambda h, p=p, e=e: h.activation(out=e[:, :], in_=p[:, :], func=AF.Exp, scale=SCALE), R=[p], W=[e])
                    for j in range(4):
                        S.op("pe", lambda h, j=j, e=e, kb=kb, nkb=nkb: h.matmul(psO[j][:, 0:129], e[:, j * 128:(j + 1) * 128], va[:, kb, 0:129],
                                                                    start=(kb == 0), stop=(kb == nkb - 1)), R=[e, va], W=[psO[j]])
                y = yt[yi % 2]
                yi += 1
                for j in range(4):
                    S.op("dve", lambda h, j=j: h.reciprocal(out=rinv[:, :], in_=psO[j][:, 128:129]), R=[psO[j]], W=[rinv])
                    S.op("dve", lambda h, j=j: h.tensor_scalar(out=on[:, :], in0=psO[j][:, 0:128], scalar1=rinv[:, 0:1], scalar2=None, op0=ALU.mult),
                         R=[psO[j], rinv], W=[on])
                    S.op("pe", lambda h, j=j: h.transpose(psT[:, j * 128:(j + 1) * 128], on[:, :], ident[:, :]), R=[on, ident], W=[psT])
                S.op("act", lambda h, y=y: h.activation(out=y[:, :], in_=psT[:, 0:512], func=AF.Copy), R=[psT], W=[y])
                dst = dr["y_b"][hl * 128:(hl + 1) * 128, s0 + qb * TB:s0 + (qb + 1) * TB]
                S.dma_out(y, lambda h, y=y, dst=dst: h.dma_start(out=dst, in_=y[:, :]))
    ph.end()


def lin_phase(nc, S, ph, name, src_ap, nk, wb, ncc, epi, tbs=TB, pre=None, ks=None):
    xb = [ph.sb(f"{name}_xb{i}", [128, nk, tbs], BF) for i in range(2)]
    pso = [ph.ps(f"{name}_ps{i}", [128, tbs]) for i in range(3)]
    nb = NT // tbs

    def load(tb):
        t = xb[tb % 2]
        tsl = slice(tb * tbs, (tb + 1) * tbs)
        for f in src_ap(t, tsl):
            S.dma_in(t, f)
    load(0)
    pi = 0
    for tb in range(nb):
        tsl = slice(tb * tbs, (tb + 1) * tbs)
        if tb + 1 < nb:
            load(tb + 1)
        b = xb[tb % 2]
        if pre is not None:
            pre(tb, tsl)
        for cc in range(ncc):
            p = pso[pi % 3]
            pi += 1
            kl = list(range(nk)) if ks is None else ks(cc)
            wc = cc if ks is None else cc % 2
            for n, kc in enumerate(kl):
                S.op("pe", lambda h, p=p, kc=kc, wc=wc, b=b, n=n, kl=kl: h.matmul(p[:, :], wb[:, kc, wc * 128:(wc + 1) * 128], b[:, kc, :],
                                                                              start=(n == 0), stop=(n == len(kl) - 1)), R=[wb, b], W=[p])
            epi(tb, cc, p, tsl)


def ss_partial(S, ph, name):
    ones = ph.sb(f"{name}_ones", [128, 128], BF)
    S.op("pool", lambda h: h.memset(ones[:, :], 1.0), W=[ones])
    sq = [ph.sb(f"{name}_sq{i}", [128, TB], BF) for i in range(2)]
    row = ph.sb(f"{name}_row", [1, TB])
    pss = ph.ps(f"{name}_pss", [128, TB])

    def f(cc, ncc, src, n, tsl, dst):
        s_ = sq[cc % 2]
        S.op("act", lambda h: h.activation(out=s_[:, 0:n], in_=src[:, 0:n], func=AF.Square), R=[src], W=[s_])
        S.op("pe", lambda h: h.matmul(pss[:, 0:n], ones[:, :], s_[:, 0:n], start=(cc == 0), stop=(cc == ncc - 1)), R=[ones, s_], W=[pss])
        if cc == ncc - 1:
            S.op("act", lambda h: h.activation(out=row[0:1, 0:n], in_=pss[0:1, 0:n], func=AF.Copy), R=[pss], W=[row])
            S.dma_out(row, lambda h: h.dma_start(out=dst[0:1, tsl], in_=row[0:1, 0:n]))
    return f


def ss_total(S, ph, name, dim):
    onesf = ph.sb(f"{name}_onesf", [8, 128])
    S.op("pool", lambda h: h.memset(onesf[:, :], 1.0), W=[onesf])
    ssl = ph.sb(f"{name}_ssl", [8, TB])
    rs = ph.sb(f"{name}_rs", [128, TB])
    rstd = ph.sb(f"{name}_rstd", [128, TB])
    pst = ph.ps(f"{name}_pst", [128, TB])

    def f(ssg, tsl, n):
        S.dma_in(ssl, lambda h: h.dma_start(out=ssl[:, 0:n], in_=ssg[:, tsl]))
        S.op("pe", lambda h: h.matmul(pst[:, 0:n], onesf[:, :], ssl[:, 0:n], start=True, stop=True), R=[onesf, ssl], W=[pst])
        rstd_from(S, None, pst, n, dim, rs, rstd)
        return rstd
    return f


def one_dma(ap_fn):
    return lambda t, tsl: [lambda h: h.dma_start(out=t[:, :, :], in_=ap_fn(tsl))]


def phase_C(nc, S, dr):
    ph = Ph(nc, S)
    wb = load_w(S, ph, "C_wb", dr["w_br"], 32, 256)
    gt = [ph.sb(f"C_gt{i}", [128, 4, TB], BF) for i in range(2)]
    tmp = [ph.sb(f"C_tmp{i}", [128, TB]) for i in range(2)]
    t2 = ph.sb("C_t2", [128, TB])
    ot = [ph.sb(f"C_ot{i}", [128, TB], BF) for i in range(2)]
    yg = dr["y_g"].ap().rearrange("(k p) t -> p k t", p=128)
    gs = dr["gate_s"].ap().rearrange("(k p) t -> p k t", p=128)
    st = {}

    def pre(tb, tsl):
        g = gt[tb % 2]
        S.dma_in(g, lambda h: h.dma_start(out=g[:, :, :], in_=gs[:, :, tsl]))
        st["g"] = g

    def ks(cc):
        br = cc // 2
        return [r * 4 + br * 2 + j for r in range(8) for j in range(2)]

    def epi(tb, cc, p, tsl):
        g = st["g"]
        j = cc % 2
        if cc < 2:
            S.op("dve", lambda h: h.tensor_tensor(out=tmp[j][:, :], in0=p[:, :], in1=g[:, j, :], op=ALU.mult), R=[p, g], W=[tmp[j]])
        else:
            o = ot[j]
            S.op("dve", lambda h: h.tensor_tensor(out=t2[:, :], in0=p[:, :], in1=g[:, 2 + j, :], op=ALU.mult), R=[p, g], W=[t2])
            S.op("pool", lambda h: h.tensor_tensor(out=o[:, :], in0=t2[:, :], in1=tmp[j][:, :], op=ALU.add), R=[t2, tmp[j]], W=[o])
            dst = dr["mix_b"][j * 128:(j + 1) * 128, tsl]
            S.dma_out(o, lambda h: h.dma_start(out=dst, in_=o[:, :]))
    lin_phase(nc, S, ph, "C", one_dma(lambda tsl: yg[:, :, tsl]), 32, wb, 4, epi, pre=pre, ks=ks)
    ph.end(allgather(nc, dr["mix_b"], dr["mix_g"]))


def phase_D(nc, S, dr):
    ph = Ph(nc, S)
    wb = load_w(S, ph, "D_wb", dr["w_o"], 16, 256)
    ssp = ss_partial(S, ph, "D")
    mt = [ph.sb(f"D_mt{i}", [128, TB]) for i in range(2)]
    mg = dr["mix_g"].ap().rearrange("(k p) t -> p k t", p=128)

    def epi(tb, cc, p, tsl):
        m = mt[cc % 2]
        S.op("dve", lambda h: h.tensor_copy(out=m[:, :], in_=p[:, :]), R=[p], W=[m])
        dst = dr["mix_s"][cc * 128:(cc + 1) * 128, tsl]
        S.dma_out(m, lambda h: h.dma_start(out=dst, in_=m[:, :]))
        ssp(cc, 2, m, TB, tsl, dr["ss1_b"])
    lin_phase(nc, S, ph, "D", one_dma(lambda tsl: mg[:, :, tsl]), 16, wb, 2, epi)
    ph.end(allgather(nc, dr["ss1_b"], dr["ss1_g"]))


def phase_E(nc, S, dr):
    ph = Ph(nc, S)
    gn = ph.sb("E_gn", [128, 6])
    S.dma_in(gn, lambda h: h.dma_start(out=gn[:, :], in_=dr["gn"][:, :]))
    tot = ss_total(S, ph, "E", D)
    ssp = ss_partial(S, ph, "E2")
    mt = [ph.sb(f"E_mt{i}", [128, 2, TB]) for i in range(2)]
    xt = [ph.sb(f"E_xt{i}", [128, 2, TB]) for i in range(2)]
    x1 = [ph.sb(f"E_x1{i}", [128, TB]) for i in range(2)]
    tt = ph.sb("E_tt", [128, TB])
    hb = [ph.sb(f"E_hb{i}", [128, TB], BF) for i in range(2)]
    ms = dr["mix_s"].ap().rearrange("(k p) t -> p k t", p=128)
    xs = dr["xTs"].ap().rearrange("(k p) t -> p k t", p=128)
    for tb in range(NB):
        tsl = slice(tb * TB, (tb + 1) * TB)
        m, x = mt[tb % 2], xt[tb % 2]
        S.dma_in(m, lambda h, m=m, tsl=tsl: h.dma_start(out=m[:, :, :], in_=ms[:, :, tsl]))
        S.dma_in(x, lambda h, x=x, tsl=tsl: h.dma_start(out=x[:, :, :], in_=xs[:, :, tsl]))
        rstd = tot(dr["ss1_g"], tsl, TB)
        for cc in range(2):
            o = x1[cc]
            S.op("dve", lambda h, m=m, cc=cc: h.tensor_tensor(out=tt[:, :], in0=m[:, cc, :], in1=rstd[:, :], op=ALU.mult), R=[m, rstd], W=[tt])
            S.op("dve", lambda h, o=o, x=x, cc=cc: h.scalar_tensor_tensor(out=o[:, :], in0=tt[:, :], scalar=gn[:, cc:cc + 1], in1=x[:, cc, :],
                                                                       op0=ALU.mult, op1=ALU.add), R=[tt, gn, x], W=[o])
            dst = dr["x1_s"][cc * 128:(cc + 1) * 128, tsl]
            S.dma_out(o, lambda h, o=o, dst=dst: h.dma_start(out=dst, in_=o[:, :]))
            hh = hb[cc]
            S.op("pool", lambda h, hh=hh, o=o, cc=cc: h.tensor_scalar(out=hh[:, :], in0=o[:, :], scalar1=gn[:, 2 + cc:3 + cc], scalar2=None, op0=ALU.mult),
                 R=[o, gn], W=[hh])
            dst2 = dr["h2_b"][cc * 128:(cc + 1) * 128, tsl]
            S.dma_out(hh, lambda h, hh=hh, dst2=dst2: h.dma_start(out=dst2, in_=hh[:, :]))
            ssp(cc, 2, o, TB, tsl, dr["ss2_b"])
    ph.end(allgather(nc, dr["h2_b"], dr["h2_g"]))
    ph = Ph(nc, S)
    nop = ph.sb("E_nop", [128, 8])
    S.op("pool", lambda h: h.memset(nop[:, :], 0.0), W=[nop])
    ph.end(allgather(nc, dr["ss2_b"], dr["ss2_g"]))


def phase_G(nc, S, dr):
    ph = Ph(nc, S)
    wb = load_w(S, ph, "G_wb", dr["w_up"], 16, 1024)
    tot = ss_total(S, ph, "G", D)
    tt = [ph.sb(f"G_tt{i}", [128, TB]) for i in range(2)]
    ot = [ph.sb(f"G_ot{i}", [128, TB], BF) for i in range(3)]
    hg = dr["h2_g"].ap().rearrange("(k p) t -> p k t", p=128)
    st = {"n": 0}

    def pre(tb, tsl):
        st["r"] = tot(dr["ss2_g"], tsl, TB)

    def epi(tb, cc, p, tsl):
        rstd = st["r"]
        t = tt[cc % 2]
        o = ot[st["n"] % 3]
        st["n"] += 1
        S.op("dve", lambda h: h.scalar_tensor_tensor(out=t[:, :], in0=p[:, :], scalar=0.0, in1=rstd[:, :], op0=ALU.max, op1=ALU.mult),
             R=[p, rstd], W=[t])
        S.op("act", lambda h: h.activation(out=o[:, :], in_=t[:, :], func=AF.Square), R=[t], W=[o])
        dst = dr["hid_b"][cc * 128:(cc + 1) * 128, tsl]
        S.dma_out(o, lambda h: h.dma_start(out=dst, in_=o[:, :]))
    lin_phase(nc, S, ph, "G", one_dma(lambda tsl: hg[:, :, tsl]), 16, wb, 8, epi, pre=pre)
    ph.end(allgather(nc, dr["hid_b"], dr["hid_g"]))


def phase_H(nc, S, dr):
    ph = Ph(nc, S)
    wb = load_w(S, ph, "H_wb", dr["w_dn"], 64, 256)
    ssp = ss_partial(S, ph, "H")
    mt = [ph.sb(f"H_mt{i}", [128, 256]) for i in range(2)]
    hg = dr["hid_g"].ap().rearrange("(k p) t -> p k t", p=128)

    def epi(tb, cc, p, tsl):
        m = mt[cc % 2]
        S.op("dve", lambda h: h.tensor_copy(out=m[:, :], in_=p[:, :]), R=[p], W=[m])
        dst = dr["ff_s"][cc * 128:(cc + 1) * 128, tsl]
        S.dma_out(m, lambda h: h.dma_start(out=dst, in_=m[:, :]))
        ssp(cc, 2, m, 256, tsl, dr["ss3_b"])
    lin_phase(nc, S, ph, "H", one_dma(lambda tsl: hg[:, :, tsl]), 64, wb, 2, epi, tbs=256)
    ph.end(allgather(nc, dr["ss3_b"], dr["ss3_g"]))


def phase_I(nc, S, dr):
    ph = Ph(nc, S)
    gn = ph.sb("I_gn", [128, 6])
    S.dma_in(gn, lambda h: h.dma_start(out=gn[:, :], in_=dr["gn"][:, :]))
    tot = ss_total(S, ph, "I", D)
    ft = [ph.sb(f"I_ft{i}", [128, 2, TB]) for i in range(2)]
    xt = [ph.sb(f"I_xt{i}", [128, 2, TB]) for i in range(2)]
    yo = [ph.sb(f"I_yo{i}", [128, TB]) for i in range(2)]
    tt = ph.sb("I_tt", [128, TB])
    fs = dr["ff_s"].ap().rearrange("(k p) t -> p k t", p=128)
    xs = dr["x1_s"].ap().rearrange("(k p) t -> p k t", p=128)
    for tb in range(NB):
        tsl = slice(tb * TB, (tb + 1) * TB)
        f, x = ft[tb % 2], xt[tb % 2]
        S.dma_in(f, lambda h, f=f, tsl=tsl: h.dma_start(out=f[:, :, :], in_=fs[:, :, tsl]))
        S.dma_in(x, lambda h, x=x, tsl=tsl: h.dma_start(out=x[:, :, :], in_=xs[:, :, tsl]))
        rstd = tot(dr["ss3_g"], tsl, TB)
        for cc in range(2):
            o = yo[cc]
            S.op("dve", lambda h, f=f, cc=cc: h.tensor_tensor(out=tt[:, :], in0=f[:, cc, :], in1=rstd[:, :], op=ALU.mult), R=[f, rstd], W=[tt])
            S.op("dve", lambda h, o=o, x=x, cc=cc: h.scalar_tensor_tensor(out=o[:, :], in0=tt[:, :], scalar=gn[:, 4 + cc:5 + cc], in1=x[:, cc, :],
                                                                       op0=ALU.mult, op1=ALU.add), R=[tt, gn, x], W=[o])
            dst = dr["yT"][cc * 128:(cc + 1) * 128, tsl]
            S.dma_out(o, lambda h, o=o, dst=dst: h.dma_start(out=dst, in_=o[:, :]))
    ph.end()


RW_INPUTS = {"rw_vec": [128, 9 * 256], "w_lora": [128, 4 * 256], "w_g2": [256, 256], "mu_rkv": [128, 12], "mu_sh": [128, 12],
             "rw_M": [128, 6 * 128], "identf": [128, 128]}
CDEC = 0.6065306597126334
RSTAGE = 9
RNBLK = 999
GN_EPS = 64e-5
LORA_CH = (11, 12, 13, 14, 15, 16)


def rw_host(inp, c):
    f = lambda k: np.asarray(inp[k], dtype=np.float32)[0]
    sl = slice(256 * c, 256 * (c + 1))
    vec = np.stack([f("rwkv_w0_f")[sl], f("rwkv_w0_b")[sl], f("rwkv_a0_f")[sl], f("rwkv_a0_b")[sl], f("rwkv_k_k")[sl], f("rwkv_k_a")[sl],
                    f("rwkv_r_k").reshape(-1)[sl], f("rwkv_ln_w")[sl], f("rwkv_ln_b")[sl]], 0).reshape(1, -1)
    m = {"rw_vec": np.ascontiguousarray(np.repeat(vec, 128, 0))}
    wl = np.zeros((128, 4, 256), np.float32)
    for j, k in enumerate(("rwkv_w2_f", "rwkv_w2_b", "rwkv_a2_f", "rwkv_a2_b")):
        wl[:96, j] = f(k)[:, sl]
    m["w_lora"] = wl.reshape(128, -1)
    m["w_g2"] = np.ascontiguousarray(f("rwkv_g2")[:, sl])
    mp, mn = f("mu_prev"), f("mu_next")
    mu = np.zeros((128, 6, 2), np.float32)
    for j, base in enumerate((0, 0, 2048, 2048, 4096, 4096)):
        idx = base + 256 * c + 128 * (j % 2) + np.arange(128)
        mu[:, j, 0], mu[:, j, 1] = mp[idx], mn[idx]
    m["mu_rkv"] = mu.reshape(128, -1)
    mu = np.zeros((128, 6, 2), np.float32)
    for j in range(4):
        idx = 6144 + 96 * j + np.arange(96)
        mu[:96, j, 0], mu[:96, j, 1] = mp[idx], mn[idx]
    for j in range(2):
        idx = 6528 + 128 * j + np.arange(128)
        mu[:, 4 + j, 0], mu[:, 4 + j, 1] = mp[idx], mn[idx]
    m["mu_sh"] = mu.reshape(128, -1)
    i = np.arange(128)
    same = (i[:, None] // 64) == (i[None, :] // 64)
    M = np.zeros((128, 6, 128), np.float32)
    lt = same & (i[:, None] < i[None, :])
    gt = same & (i[:, None] > i[None, :])
    eq = np.eye(128, dtype=bool)
    M[:, 0], M[:, 1], M[:, 2] = lt, lt | eq, lt.T
    M[:, 3], M[:, 4], M[:, 5] = gt, gt | eq, gt.T
    m["rw_M"] = M.reshape(128, -1)
    m["identf"] = np.eye(128, dtype=np.float32)
    return m


def phase_R(nc, S, dr):
    for d in (0, 1):
        rwkv_pass(nc, S, dr, d)


def rwkv_pass(nc, S, dr, d):
    ph = Ph(nc, S)
    sb, ps = ph.sb, ph.ps
    op = S.op
    vec = sb("R_vec", [128, 9, 256])
    S.dma_in(vec, lambda h: h.dma_start(out=vec[:, :, :], in_=dr["rw_vec"].ap().rearrange("p (a b) -> p a b", b=256)))
    W0F, W0B, A0F, A0B, KK_, KA_, RK_, LNW, LNB = range(9)
    wl = load_w(S, ph, "R_wl", dr["w_lora"], 1, 1024)
    wg2 = load_w(S, ph, "R_wg2", dr["w_g2"], 2, 256)
    mur = sb("R_mur", [128, 6, 2]); mus = sb("R_mus", [128, 6, 2])
    S.dma_in(mur, lambda h: h.dma_start(out=mur[:, :, :], in_=dr["mu_rkv"].ap().rearrange("p (a b) -> p a b", b=2)))
    S.dma_in(mus, lambda h: h.dma_start(out=mus[:, :, :], in_=dr["mu_sh"].ap().rearrange("p (a b) -> p a b", b=2)))
    c0r = sb("R_c0r", [128, 6]); c0s = sb("R_c0s", [128, 6])
    for (c0t, mut) in ((c0r, mur), (c0s, mus)):
        op("dve", lambda h, c0t=c0t, mut=mut: h.tensor_tensor(out=c0t[:, :], in0=mut[:, :, 0], in1=mut[:, :, 1], op=ALU.add), R=[mut], W=[c0t])
        op("dve", lambda h, c0t=c0t: h.tensor_scalar(out=c0t[:, :], in0=c0t[:, :], scalar1=-1.0, scalar2=1.0, op0=ALU.mult, op1=ALU.add), W=[c0t])
    Mf = sb("R_M", [128, 768])
    S.dma_in(Mf, lambda h: h.dma_start(out=Mf[:, :], in_=dr["rw_M"][:, :]))
    MS, MI, MST = slice(384 * d, 384 * d + 128), slice(384 * d + 128, 384 * d + 256), slice(384 * d + 256, 384 * d + 384)
    MSI = slice(384 * d, 384 * d + 256)
    TL = 63 if d == 0 else 0
    identf = sb("R_idf", [128, 128])
    S.dma_in(identf, lambda h: h.dma_start(out=identf[:, :], in_=dr["identf"][:, :]))
    identb = sb("R_idb", [128, 128], BF)
    op("pool", lambda h: h.tensor_copy(out=identb[:, :], in_=identf[:, :]), R=[identf], W=[identb])
    onesf = sb("R_onesf", [128, 2])
    op("pool", lambda h: h.memset(onesf[:, :], 1.0), W=[onesf])
    uf = [sb(f"R_uf{i}", [128, 6, 514], BF) for i in range(2)]
    lf = [sb(f"R_lf{i}", [128, 6, 514], BF) for i in range(2)]
    usr = sb("R_usr", [128, 6, 512], BF)
    lor = sb("R_lor", [128, 6, 512], BF)
    tA = [sb(f"R_tA{i}", [128, 512]) for i in range(2)]
    rkvT = sb("R_rkvT", [128, 768], BF)
    F = lambda n: sb(n, [128, 256])
    kkr, sqk, kk = F("R_kkr"), F("R_sqk"), F("R_kk")
    ssum = sb("R_ssum", [128, 4]); nrm = sb("R_nrm", [128, 4]); rn = sb("R_rn", [128, 4])
    zz, sg, al, alo = F("R_zz"), F("R_sg"), F("R_al"), F("R_alo")
    Ep, Em, Ea, Ed = F("R_Ep"), F("R_Em"), F("R_Ea"), F("R_Ed")
    t1, kd, bd = F("R_t1"), F("R_kd"), F("R_bd")
    gC = sb("R_gC", [128, 4])
    tm = sb("R_tm", [128, 6, 256], BF)
    cm = sb("R_cm", [128, 2, 512], BF)
    AB = [sb(f"R_AB{i}", [128, 256], BF) for i in range(4)]
    AK = [sb(f"R_AK{i}", [128, 256], BF) for i in range(4)]
    NTt = [sb(f"R_NT{i}", [128, 128], BF) for i in range(4)]
    Xs = [[sb(f"R_X{b}{i}", [128, 128], BF) for i in range(5)] for b in range(4)]
    XTs = [[sb(f"R_XT{b}{i}", [128, 128], BF) for i in range(4)] for b in range(4)]
    Zt = [[sb(f"R_Z{b}{i}", [128, 128], BF) for i in range(2)] for b in range(4)]
    Pm = [[sb(f"R_Pm{b}{i}", [128, 64], BF) for i in range(2)] for b in range(4)]
    Gt = [[sb(f"R_G{b}{i}", [128, 64], BF) for i in range(2)] for b in range(4)]
    st = [sb(f"R_st{h}", [128, 64], BF) for h in range(4)]
    Yt = sb("R_Yt", [128, 256])
    if d == 1:
        yfl, ysum, yn, yo = F("R_yfl"), F("R_ysum"), F("R_yn"), F("R_yo")
        stats = sb("R_stats", [128, 4, 6]); mv = sb("R_mv", [128, 4, 2])
        sd = sb("R_sd", [128, 4]); rsd = sb("R_rsd", [128, 4]); bs = sb("R_bs", [128, 4])
        ybt = sb("R_ybt", [128, 2, 128], BF)
    bT = ps("R_pT", [128, 512])
    bT2 = ps("R_pT2", [128, 512])
    bZ = ps("R_pZ", [128, 512])
    bC = ps("R_pC", [128, 512])
    bD = ps("R_pD", [128, 512])
    bE = ps("R_pE", [128, 512])
    bF_ = ps("R_pF", [128, 512])
    bG = ps("R_pG", [128, 512])
    sub = lambda par, a, b, nm: S.tile(par[:, a:b], nm, parent=par)
    pF = [sub(bF_, 0, 128, "pF0"), sub(bT, 0, 128, "pF1"), sub(bT2, 0, 128, "pF2"), sub(bC, 0, 128, "pF3")]
    pP = [sub(bG, 0, 64, "pP0"), sub(bE, 0, 64, "pP1")]
    pGm = [sub(bG, 64, 128, "pG0"), sub(bE, 64, 128, "pG1")]
    pS = [sub(bZ, 0, 64, "pS0"), sub(bD, 256, 320, "pS1")]
    pYs = [sub(bZ, 64, 128, "pY0"), sub(bD, 320, 384, "pY1")]
    pz = sub(bF_, 128, 256, "pz")
    ev = {"n": 0}

    act_banks = {id(bT), id(bC), id(bD)}
    hbank = [bF_, bT, bT2, bC]
    _subs = {}

    def sub_cache(par, a_, b_):
        k_ = (id(par), a_, b_)
        if k_ not in _subs:
            _subs[k_] = S.tile(par[:, a_:b_], f"sub{len(_subs)}", parent=par)
        return _subs[k_]

    def evac(out_t, out_ap, in_t, in_ap):
        if id(in_t.root) in act_banks:
            op("act", lambda h: h.activation(out=out_ap, in_=in_ap, func=AF.Copy), R=[in_t], W=[out_t])
        else:
            op("dve", lambda h: h.tensor_copy(out=out_ap, in_=in_ap), R=[in_t], W=[out_t])

    def mm(out_t, out_ap, l_t, l_ap, r_t, r_ap, start=True, stop=True):
        op("pe", lambda h: h.matmul(out_ap, l_ap, r_ap, start=start, stop=stop), R=[l_t, r_t], W=[out_t])

    def tt(e, out_t, out_ap, a_t, a_ap, b_t, b_ap, o):
        op(e, lambda h: h.tensor_tensor(out=out_ap, in0=a_ap, in1=b_ap, op=o), R=[a_t, b_t], W=[out_t])

    def stt(out_t, out_ap, a_t, a_ap, scalar, b_t, b_ap, o0, o1, extra=()):
        op("dve", lambda h: h.scalar_tensor_tensor(out=out_ap, in0=a_ap, scalar=scalar, in1=b_ap, op0=o0, op1=o1), R=[a_t, b_t, *extra], W=[out_t])

    def act(out_t, out_ap, in_t, in_ap, func, scale=1.0, bias=0.0):
        op("act", lambda h: h.activation(out=out_ap, in_=in_ap, func=func, scale=scale, bias=bias), R=[in_t], W=[out_t])

    rk = dr["rkv_s"].ap().rearrange("(k p) t -> p k t", p=128)
    shg = dr["sh_g"]
    seq_starts = {s0 for s0, L in SEQS}
    seq_ends = {s0 + L for s0, L in SEQS}

    def load_block(tb):
        t0 = tb * TB
        u, l = uf[tb % 2], lf[tb % 2]
        lo = 0 if t0 in seq_starts else -1
        hi = 0 if (t0 + TB) in seq_ends else 1
        csl = slice(1 + lo, 513 + hi)
        tsl = slice(t0 + lo, t0 + TB + hi)
        if lo == 0:
            op("pool", lambda h: h.memset(u[:, :, 0:1], 0.0), W=[u])
            op("pool", lambda h: h.memset(l[:, :, 0:1], 0.0), W=[l])
        if hi == 0:
            op("pool", lambda h: h.memset(u[:, :, 513:514], 0.0), W=[u])
            op("pool", lambda h: h.memset(l[:, :, 513:514], 0.0), W=[l])
        S.dma_in(u, lambda h: h.dma_start(out=u[:, :, csl], in_=rk[:, :, tsl]))
        for j, ch in enumerate(LORA_CH):
            r0 = sh_rows(ch)
            S.dma_in(l, lambda h, j=j, r0=r0: h.dma_start(out=l[:, j, csl], in_=shg[r0:r0 + 128, tsl]))

    def shift(src, j, c0t, mut, dst_t, dst_ap, func=None):
        a = tA[j % 2]
        op("pool", lambda h: h.tensor_scalar(out=a[:, :], in0=src[:, j, 1:513], scalar1=c0t[:, j:j + 1], scalar2=None, op0=ALU.mult), R=[src, c0t], W=[a])
        stt(a, a[:, :], src, src[:, j, 0:512], mut[:, j, 0:1], a, a[:, :], ALU.mult, ALU.add, extra=(mut,))
        if func is None:
            stt(dst_t, dst_ap, src, src[:, j, 2:514], mut[:, j, 1:2], a, a[:, :], ALU.mult, ALU.add, extra=(mut,))
        else:
            stt(a, a[:, :], src, src[:, j, 2:514], mut[:, j, 1:2], a, a[:, :], ALU.mult, ALU.add, extra=(mut,))
            act(dst_t, dst_ap, a, a[:, :], func)

    blocks = list(range(NB)) if d == 0 else list(range(NB - 1, -1, -1))
    load_block(blocks[0])
    ui = 0
    for bi, tb in enumerate(blocks[:RNBLK]):
        t0 = tb * TB
        if bi + 1 < NB:
            load_block(blocks[bi + 1])
        u, l = uf[tb % 2], lf[tb % 2]
        for j in range(6):
            shift(u, j, c0r, mur, usr, usr[:, j, :])
        for j in range(6):
            shift(l, j, c0s, mus, lor, lor[:, j, :], func=(AF.Tanh if j < 2 else (None if j < 4 else AF.Sigmoid)))
        if (d == 0 and t0 in seq_starts) or (d == 1 and (t0 + TB) in seq_ends):
            for h_ in range(4):
                op("pool", lambda h, h_=h_: h.memset(st[h_][:, :], 0.0), W=[st[h_]])
        tiles = list(range(4)) if d == 0 else [3, 2, 1, 0]
        if RSTAGE < 2:
            tiles = []
        for tj in tiles:
            ksl = slice(tj * 128, (tj + 1) * 128)
            g0_ = t0 + tj * 128
            for j in range(6):
                bt_ = bT if j < 4 else bT2
                mm(bt_, bt_[:, (j % 4) * 128:(j % 4 + 1) * 128], usr, usr[:, j, ksl], identb, identb[:, :])
            evac(rkvT, rkvT[:, 0:512], bT, bT[:, :])
            evac(rkvT, rkvT[:, 512:768], bT2, bT2[:, 0:256])
            r_ap, k_ap, v_ap = rkvT[:, 0:256], rkvT[:, 256:512], rkvT[:, 512:768]
            tt("dve", kkr, kkr[:, :], rkvT, k_ap, vec, vec[:, KK_, :], ALU.mult)
            act(sqk, sqk[:, :], kkr, kkr[:, :], AF.Square)
            op("dve", lambda h: h.tensor_reduce(out=ssum[:, :], in_=sqk[:, :].rearrange("p (a b) -> p a b", b=64), axis=AX.X, op=ALU.add), R=[sqk], W=[ssum])
            act(nrm, nrm[:, :], ssum, ssum[:, :], AF.Sqrt)
            op("dve", lambda h: h.tensor_scalar(out=nrm[:, :], in0=nrm[:, :], scalar1=1e-12, scalar2=None, op0=ALU.max), W=[nrm])
            op("dve", lambda h: h.reciprocal(out=rn[:, :], in_=nrm[:, :]), R=[nrm], W=[rn])
            for h_ in range(4):
                op("dve", lambda h, h_=h_: h.tensor_scalar(out=kk[:, h_ * 64:(h_ + 1) * 64], in0=kkr[:, h_ * 64:(h_ + 1) * 64], scalar1=rn[:, h_:h_ + 1],
                                                         scalar2=None, op0=ALU.mult), R=[kkr, rn], W=[kk])
            if RSTAGE < 3:
                continue
            mm(bZ, bZ[:, 0:256], lor, lor[:, d, ksl], wl, wl[:, 0, d * 256:(d + 1) * 256])
            mm(bZ, bZ[:, 256:512], lor, lor[:, 2 + d, ksl], wl, wl[:, 0, (2 + d) * 256:(3 + d) * 256])
            tt("dve", zz, zz[:, :], bZ, bZ[:, 0:256], vec, vec[:, W0F + d, :], ALU.add)
            act(sg, sg[:, :], zz, zz[:, :], AF.Sigmoid)
            tt("dve", zz, zz[:, :], bZ, bZ[:, 256:512], vec, vec[:, A0F + d, :], ALU.add)
            act(al, al[:, :], zz, zz[:, :], AF.Sigmoid)
            if d == 1:
                mm(bZ, bZ[:, 0:256], lor, lor[:, 2, ksl], wl, wl[:, 0, 512:768])
                tt("dve", zz, zz[:, :], bZ, bZ[:, 0:256], vec, vec[:, A0F, :], ALU.add)
                act(alo, alo[:, :], zz, zz[:, :], AF.Sigmoid)
            mm(bC, bC[:, 0:256], Mf, Mf[:, MI], sg, sg[:, :])
            mm(bC, bC[:, 256:512], Mf, Mf[:, MS], sg, sg[:, :])
            mm(bD, bD[:, 0:256], Mf, Mf[:, MST], sg, sg[:, :])
            act(Ep, Ep[:, :], bC, bC[:, 0:256], AF.Exp, scale=-CDEC)
            for c in range(2):
                pc = 64 * c
                for h_ in range(4):
                    mm(bD, bD[pc:pc + 64, 256 + 64 * h_:256 + 64 * (h_ + 1)], Ep, Ep[pc:pc + 64, h_ * 64:(h_ + 1) * 64], identf, identf[pc:pc + 64, pc:pc + 64])
            act(Em, Em[:, :], bC, bC[:, 0:256], AF.Exp, scale=CDEC)
            act(Ea, Ea[:, :], bC, bC[:, 256:512], AF.Exp, scale=-CDEC)
            act(Ed, Ed[:, :], bD, bD[:, 0:256], AF.Exp, scale=-CDEC)
            op("act", lambda h: h.activation(out=gC[:, :], in_=bD[:, 256:512].rearrange("p (a b) -> p a b", b=64)[:, :, TL], func=AF.Copy), R=[bD], W=[gC])
            stt(t1, t1[:, :], al, al[:, :], -1.0, vec, vec[:, KA_, :], ALU.add, ALU.mult)
            stt(kd, kd[:, :], t1, t1[:, :], 1.0, rkvT, k_ap, ALU.add, ALU.mult)
            tt("pool", bd, bd[:, :], kk, kk[:, :], al, al[:, :], ALU.mult)
            tt("pool", tm, tm[:, 0, :], rkvT, r_ap, Ep, Ep[:, :], ALU.mult)
            tt("dve", tm, tm[:, 1, :], kd, kd[:, :], Em, Em[:, :], ALU.mult)
            tt("pool", tm, tm[:, 2, :], bd, bd[:, :], Em, Em[:, :], ALU.mult)
            stt(tm, tm[:, 3, :], kk, kk[:, :], -1.0, Ea, Ea[:, :], ALU.mult, ALU.mult)
            tt("dve", tm, tm[:, 4, :], kd, kd[:, :], Ed, Ed[:, :], ALU.mult)
            tt("pool", tm, tm[:, 5, :], bd, bd[:, :], Ed, Ed[:, :], ALU.mult)
            for hp in range(2):
                bt_ = bT if hp == 0 else bT2
                for s_, xi in enumerate((3, 0, 2, 1)):
                    mm(bt_, bt_[:, s_ * 128:(s_ + 1) * 128], tm, tm[:, xi, hp * 128:(hp + 1) * 128], identb, identb[:, :])
            evac(cm, cm[:, 0, :], bT, bT[:, :])
            evac(cm, cm[:, 1, :], bT2, bT2[:, :])
            def head_gen(h_):
                hp, hb = h_ // 2, 64 * (h_ % 2)
                hs = slice(h_ * 64, (h_ + 1) * 64)
                ab, ak, ntt = AB[h_], AK[h_], NTt[h_]
                xs, xts = Xs[h_], XTs[h_]
                bank = hbank[h_]
                sA = sub_cache(bank, 0, 128)
                sB = sub_cache(bank, 128, 256)
                pzh = sub_cache(bank, 256, 384)
                arT = cm[hb:hb + 64, hp, 0:256]
                bTa = cm[hb:hb + 64, hp, 256:384]
                kTa = cm[hb:hb + 64, hp, 384:512]
                aTa = cm[hb:hb + 64, hp, 0:128]
                mm(bE, bE[:, 0:256], cm, bTa, cm, arT)
                tt("dve", ab, ab[:, :], bE, bE[:, 0:256], Mf, Mf[:, MSI], ALU.mult)
                mm(bE, bE[:, 256:512], cm, kTa, cm, arT)
                tt("dve", ak, ak[:, :], bE, bE[:, 256:512], Mf, Mf[:, MSI], ALU.mult)
                yield
                mm(bG, bG[:, 256:384], cm, aTa, cm, bTa)
                tt("dve", ntt, ntt[:, :], bG, bG[:, 256:384], Mf, Mf[:, MST], ALU.mult)
                yield
                X, XT = (ab, ab[:, 0:128]), (ntt, ntt[:, :])
                Xl = [X]
                for i in range(5):
                    mm(sA, sA[:, :], XT[0], XT[1], X[0], X[1])
                    if i < 4:
                        mm(sB, sB[:, :], X[0], X[1], XT[0], XT[1])
                    evac(xs[i], xs[i][:, :], sA, sA[:, :])
                    if i < 4:
                        evac(xts[i], xts[i][:, :], sB, sB[:, :])
                        XT = (xts[i], xts[i][:, :])
                    X = (xs[i], xs[i][:, :])
                    Xl.append(X)
                    yield
                zt = Zt[h_]
                z = zt[0]
                op("pool", lambda h, z=z, hs=hs: h.tensor_copy(out=z[:, 0:64], in_=tm[:, 3, hs]), R=[tm], W=[z])
                vh = rkvT[:, 512 + h_ * 64:512 + (h_ + 1) * 64]
                mm(pzh, pzh[:, 0:64], ak, ak[:, 0:128], rkvT, vh)
                evac(z, z[:, 64:128], pzh, pzh[:, 0:64])
                yield
                for i in range(6):
                    zn = zt[(i + 1) % 2]
                    mm(sA, sA[:, :], identb, identb[:, :], z, z[:, :], start=True, stop=False)
                    mm(sA, sA[:, :], Xl[i][0], Xl[i][1], z, z[:, :], start=False, stop=True)
                    evac(zn, zn[:, :], sA, sA[:, :])
                    z = zn
                    yield
                for c in (((0, 1) if d == 0 else (1, 0)) if RSTAGE >= 5 else ()):
                    pc, po = 64 * c, 64 * (1 - c)
                    cs_ = slice(pc, pc + 64)
                    os_ = slice(po, po + 64)
                    Bh = tm[cs_, 5, hs]
                    Kh = tm[cs_, 4, hs]
                    Rt = tm[cs_, 0, hs]
                    pm, gt_ = Pm[h_][c], Gt[h_][c]
                    mm(pP[c], pP[c][cs_, :], z, z[cs_, 0:64], tm, Bh)
                    stt(pm, pm[cs_, :], identf, identf[cs_, pc:pc + 64], gC[cs_, h_:h_ + 1], pP[c], pP[c][cs_, :], ALU.mult, ALU.add, extra=(gC,))
                    mm(pGm[c], pGm[c][cs_, :], z, z[cs_, 0:64], ab, ab[cs_, 128 + pc:128 + pc + 64], start=True, stop=False)
                    mm(pGm[c], pGm[c][cs_, :], tm, Rt, identb, identb[cs_, pc:pc + 64], start=False, stop=True)
                    evac(gt_, gt_[cs_, :], pGm[c], pGm[c][cs_, :])
                    yield
                    pY = pYs[c]
                    mm(pY, pY[cs_, :], gt_, gt_[cs_, :], st[h_], st[h_][cs_, :], start=True, stop=False)
                    mm(pY, pY[cs_, :], ab, ab[cs_, 128 + pc:128 + pc + 64], z, z[cs_, 64:128], start=False, stop=False)
                    mm(pY, pY[cs_, :], ak, ak[cs_, 128 + pc:128 + pc + 64], rkvT, rkvT[cs_, 512 + h_ * 64:512 + (h_ + 1) * 64], start=False, stop=True)
                    evac(Yt, Yt[cs_, hs], pY, pY[cs_, :])
                    mm(pS[c], pS[c][os_, :], tm, Bh, z, z[cs_, 64:128], start=True, stop=False)
                    mm(pS[c], pS[c][os_, :], tm, Kh, rkvT, rkvT[cs_, 512 + h_ * 64:512 + (h_ + 1) * 64], start=False, stop=False)
                    mm(pS[c], pS[c][os_, :], pm, pm[cs_, :], st[h_], st[h_][cs_, :], start=False, stop=True)
                    evac(st[h_], st[h_][os_, :], pS[c], pS[c][os_, :])
                    yield

            gens = [head_gen(h_) for h_ in range(4 if RSTAGE >= 4 else 0)]
            while gens:
                for g_ in gens[:]:
                    try:
                        next(g_)
                    except StopIteration:
                        gens.remove(g_)
            tok = slice(g0_, g0_ + 128)
            if RSTAGE < 6:
                continue
            if d == 0:
                S.dma_out(Yt, lambda h, tok=tok: h.dma_start(out=dr["yf_s"][tok, :], in_=Yt[:, :]))
            else:
                S.dma_in(yfl, lambda h, tok=tok: h.dma_start(out=yfl[:, :], in_=dr["yf_s"][tok, :]))
                tt("dve", ysum, ysum[:, :], Yt, Yt[:, :], yfl, yfl[:, :], ALU.add)
                for h_ in range(4):
                    hs = slice(h_ * 64, (h_ + 1) * 64)
                    op("dve", lambda h, h_=h_, hs=hs: h.bn_stats(out=stats[:, h_, :], in_=ysum[:, hs]), R=[ysum], W=[stats])
                    op("dve", lambda h, h_=h_: h.bn_aggr(out=mv[:, h_, :], in_=stats[:, h_, :]), R=[stats], W=[mv])
                act(sd, sd[:, :], mv, mv[:, :, 1], AF.Sqrt, bias=GN_EPS)
                op("dve", lambda h: h.reciprocal(out=rsd[:, :], in_=sd[:, :]), R=[sd], W=[rsd])
                for h_ in range(4):
                    hs = slice(h_ * 64, (h_ + 1) * 64)
                    op("dve", lambda h, h_=h_, hs=hs: h.tensor_scalar(out=yn[:, hs], in0=ysum[:, hs], scalar1=mv[:, h_, 0:1], scalar2=rsd[:, h_:h_ + 1],
                                                                    op0=ALU.subtract, op1=ALU.mult), R=[ysum, mv, rsd], W=[yn])
                tt("pool", yn, yn[:, :], yn, yn[:, :], vec, vec[:, LNW, :], ALU.mult)
                tt("pool", yn, yn[:, :], yn, yn[:, :], vec, vec[:, LNB, :], ALU.add)
                tt("dve", t1, t1[:, :], al, al[:, :], alo, alo[:, :], ALU.add)
                stt(t1, t1[:, :], t1, t1[:, :], -2.0, vec, vec[:, KA_, :], ALU.add, ALU.mult)
                stt(kd, kd[:, :], t1, t1[:, :], 2.0, rkvT, k_ap, ALU.add, ALU.mult)
                tt("dve", kd, kd[:, :], kd, kd[:, :], rkvT, r_ap, ALU.mult)
                tt("dve", kd, kd[:, :], kd, kd[:, :], vec, vec[:, RK_, :], ALU.mult)
                op("dve", lambda h: h.tensor_reduce(out=bs[:, :], in_=kd[:, :].rearrange("p (a b) -> p a b", b=64), axis=AX.X, op=ALU.add), R=[kd], W=[bs])
                for h_ in range(4):
                    hs = slice(h_ * 64, (h_ + 1) * 64)
                    stt(yn, yn[:, hs], rkvT, rkvT[:, 512 + h_ * 64:512 + (h_ + 1) * 64], bs[:, h_:h_ + 1], yn, yn[:, hs], ALU.mult, ALU.add, extra=(bs,))
                mm(bZ, bZ[:, 0:256], lor, lor[:, 4, ksl], wg2, wg2[:, 0, :], start=True, stop=False)
                mm(bZ, bZ[:, 0:256], lor, lor[:, 5, ksl], wg2, wg2[:, 1, :], start=False, stop=True)
                tt("dve", yo, yo[:, :], yn, yn[:, :], bZ, bZ[:, 0:256], ALU.mult)
                for cc in range(2):
                    pq = pF[cc]
                    op("pe", lambda h, cc=cc, pq=pq: h.transpose(pq[:, :], yo[:, cc * 128:(cc + 1) * 128], identf[:, :]), R=[yo, identf], W=[pq])
                    evac(ybt, ybt[:, cc, :], pq, pq[:, :])
                S.dma_out(ybt, lambda h, tok=tok: h.dma_start(out=dr["y_b"].ap()[256:512, tok].rearrange("(c p) t -> p c t", p=128), in_=ybt[:, :, :]))
    ph.end()


DBG = []
PHASES = "MRCDEGHI"


def build_nc(with_rwkv=True):
    nc = bass.Bass("TRN2", target_bir_lowering=False)
    dr = {}

    def ein(name, shape, dt=F32):
        dr[name] = nc.dram_tensor(name, shape, dt, kind="ExternalInput")

    def scr(name, shape, dt):
        dr[name] = nc.dram_tensor(name, shape, dt)
    ein("xT", [D, NT]); ein("xTs", [256, NT]); ein("w_inA", [D, 13 * 128]); ein("g0", [128, KC])
    ein("w_uq", [768, 512]); ein("w_ukv", [512, 512]); ein("qng", [128, 6]); ein("kvg", [128, 4])
    ein("ident", [128, 128], F32); ein("cos2", [64, 8192]); ein("sin2", [64, 8192])
    ein("w_br", [4096, 256]); ein("w_o", [D, 256]); ein("w_up", [D, 1024]); ein("w_dn", [8192, 256]); ein("gn", [128, 6])
    for k, shp in RW_INPUTS.items():
        ein(k, shp)
    dr["yT"] = nc.dram_tensor("yT", [256, NT], F32, kind="ExternalOutput")
    scr("sh_b", [384, NT], BF); scr("sh_g", [8 * 384, NT], BF)
    scr("rkv_s", [768, NT], BF); scr("gate_s", [512, NT], BF)
    scr("y_b", [512, NT], BF); scr("y_g", [8 * 512, NT], BF)
    scr("mix_b", [256, NT], BF); scr("mix_g", [D, NT], BF)
    scr("mix_s", [256, NT], F32); scr("x1_s", [256, NT], F32); scr("ff_s", [256, NT], F32)
    scr("h2_b", [256, NT], BF); scr("h2_g", [D, NT], BF)
    scr("hid_b", [1024, NT], BF); scr("hid_g", [8192, NT], BF)
    scr("yf_s", [NT, 256], F32)
    for i in (1, 2, 3):
        scr(f"ss{i}_b", [1, NT], F32); scr(f"ss{i}_g", [8, NT], F32)
    with contextlib.ExitStack() as gs:
        S = Sched(nc, gs)
        phase_A(nc, S, dr)
        if "M" in PHASES:
            phase_M(nc, S, dr)
        if "R" in PHASES:
            phase_R(nc, S, dr)
        ph = Ph(nc, S)
        nop = ph.sb("X_nop", [128, 8])
        S.op("pool", lambda h: h.memset(nop[:, :], 0.0), W=[nop])
        ph.end(allgather(nc, dr["y_b"], dr["y_g"]))
        for nm, fn in (("C", phase_C), ("D", phase_D), ("E", phase_E), ("G", phase_G), ("H", phase_H), ("I", phase_I)):
            if nm in PHASES:
                fn(nc, S, dr)
        if DBG:
            ph = Ph(nc, S)
            dt_ = ph.sb("dbg_t", [128, 8])
            for nm in DBG:
                src = dr[nm]
                dst = nc.dram_tensor("dbg_" + nm, list(src.shape), src.dtype, kind="ExternalOutput")
                for r0 in range(0, src.shape[0], 128):
                    r1 = min(r0 + 128, src.shape[0])
                    for c0 in range(0, src.shape[1], 4096):
                        c1 = min(c0 + 4096, src.shape[1])
                        S.dma_out(dt_, lambda h, src=src, dst=dst, r0=r0, r1=r1, c0=c0, c1=c1: h.dma_start(out=dst[r0:r1, c0:c1], in_=src[r0:r1, c0:c1]))
            ph.end()
    return nc


def _cols(a, n=128):
    return np.ascontiguousarray(a.reshape(-1, 128).T.astype(np.float32))


def kernel(**inp):
    f = lambda k: np.asarray(inp[k], dtype=np.float32)
    x = np.concatenate([f("x_prompt").reshape(-1, D), f("x_sample").reshape(-1, D)], 0)
    xT = np.ascontiguousarray(x.T)
    w_in = f("w_in")[0]
    pad = lambda cols, n=128: list(cols) + [-1] * (n - len(cols))
    sh = [list(range(i * 128, (i + 1) * 128)) for i in range(10)]
    sh.append(pad(range(1280, 1344)))
    sh += [pad(range(U0 + 6144 + 96 * j, U0 + 6144 + 96 * (j + 1))) for j in range(4)]
    sh += [list(range(U0 + 6528, U0 + 6656)), list(range(U0 + 6656, U0 + 6784))]
    sh.append(pad(list(range(1312, 1344)) + list(range(1280, 1312))))
    w_pad = np.concatenate([w_in, np.zeros((D, 1), np.float32)], 1)
    inv = 1.0 / (10000.0 ** (np.arange(0, 64, 2, dtype=np.float32) / 64))
    ang = np.arange(8192, dtype=np.float32)[:, None] * inv[None, :]
    cos, sin = np.cos(ang).T.astype(np.float32), np.sin(ang).T.astype(np.float32)
    cos2 = np.ascontiguousarray(np.concatenate([cos, cos], 0))
    sin2 = np.ascontiguousarray(np.concatenate([-sin, sin], 0))
    import ml_dtypes
    ident = np.eye(128, dtype=np.float32)
    w_uq, w_ukv = f("mla_w_uq")[0], f("mla_w_ukv")[0]
    w_br, w_o, w_up, w_dn = f("w_branch")[0], f("w_out")[0], f("w_mlp_up")[0], f("w_mlp_down")[0]
    in_maps = []
    for c in range(NCORES):
        m = {"xT": xT, "xTs": np.ascontiguousarray(xT[256 * c:256 * (c + 1)])}
        cols = []
        for slot in range(3):
            i = slot * 8 + c
            cols += sh[i] if i < NSH else [-1] * 128
        for base in (0, 2048, 4096):
            cols += list(range(U0 + base + 256 * c, U0 + base + 256 * (c + 1)))
        cols += list(range(G0 + 256 * c, G0 + 256 * (c + 1))) + list(range(G0 + 2048 + 256 * c, G0 + 2048 + 256 * (c + 1)))
        m["w_inA"] = np.ascontiguousarray(w_pad[:, cols])
        m["g0"] = _cols(f("norm_pre_mix")[0])
        qc = []
        for hd in (2 * c, 2 * c + 1):
            b = hd * 192
            qc += list(range(b, b + 192)) + list(range(b + 160, b + 192)) + list(range(b + 128, b + 160))
        m["w_uq"] = np.ascontiguousarray(w_uq[:, qc])
        m["w_ukv"] = np.ascontiguousarray(w_ukv[:, 512 * c:512 * (c + 1)])
        m["qng"] = _cols(f("mla_q_norm")[0]); m["kvg"] = _cols(f("mla_kv_norm")[0])
        m["ident"] = ident; m["cos2"] = cos2; m["sin2"] = sin2
        rows = []
        for r in range(8):
            rows += list(range(256 * r, 256 * (r + 1))) + list(range(2048 + 256 * r, 2048 + 256 * (r + 1)))
        m["w_br"] = np.ascontiguousarray(w_br[rows][:, 256 * c:256 * (c + 1)])
        m["w_o"] = np.ascontiguousarray(w_o[:, 256 * c:256 * (c + 1)])
        m["w_up"] = np.ascontiguousarray(w_up[:, 1024 * c:1024 * (c + 1)])
        m["w_dn"] = np.ascontiguousarray(w_dn[:, 256 * c:256 * (c + 1)])
        sl = slice(256 * c, 256 * (c + 1))
        m["gn"] = np.ascontiguousarray(np.concatenate([_cols(f("norm_post_mix")[0][sl]), _cols(f("norm_pre_mlp")[0][sl]),
                                                       _cols(f("norm_post_mlp")[0][sl])], 1))
        m.update(rw_host(inp, c))
        in_maps.append(m)
    nc = build_nc()
    res = run_bass_kernel_spmd(nc, in_maps, core_ids=list(range(NCORES)))
    global LAST
    LAST = res
    yT = np.concatenate([res.results[c]["yT"] for c in range(NCORES)], 0)
    y = np.ascontiguousarray(yT.T).astype(np.float32)
    return (y[:8192].reshape(1, 8192, D), y[8192:].reshape(4, 2048, D))
```

```python
import numpy as np, contextlib
import concourse.bass as bass
import concourse.mybir as mybir

F32 = mybir.dt.float32
BF = mybir.dt.bfloat16
AF = mybir.ActivationFunctionType
ALU = mybir.AluOpType
AX = mybir.AxisListType


PSUM_EXCL = False
SELF_WAITS = True


class T:
    __slots__ = ("ap", "w", "r", "ds", "name", "root", "psum")

    def __init__(self, ap, name=""):
        self.ap = ap
        self.root = self
        self.psum = False
        self.w = []
        self.r = {}
        self.ds = None
        self.name = name

    def __getitem__(self, k):
        return self.ap[k]


class Sched:
    ENG = ("pe", "act", "dve", "pool", "sp")
    EPOCH = 30000

    def __init__(self, nc, gstack):
        self.nc = nc
        self.gstack = gstack
        self.h = {"pe": nc.tensor, "act": nc.scalar, "dve": nc.vector, "pool": nc.gpsimd, "sp": nc.sync}
        self.rec = {k: [] for k in self.ENG}
        self.base = {k: 0 for k in self.ENG}
        self.sems = {k: [gstack.enter_context(nc.semaphore(f"s_{k}0"))] for k in self.ENG}
        self.waited = {k: {} for k in self.ENG}
        self.dpool = []
        self.dall = []
        self.nd = 0
        self.cc_sem = gstack.enter_context(nc.semaphore("ccsem"))
        self.ncc = 0
        self.ninstr = 0
        self.tiles = []

    def tile(self, ap, name="", parent=None, psum=False):
        t = T(ap, name)
        t.psum = psum
        if parent is not None:
            t.root = parent.root
            t.psum = parent.root.psum
        self.tiles.append(t)
        return t

    def _dsem(self, t):
        if t.ds is None:
            if self.dpool:
                t.ds = self.dpool.pop()
            else:
                t.ds = [self.gstack.enter_context(self.nc.semaphore(f"d{self.nd}")), 0]
                self.nd += 1
                self.dall.append(t.ds)
        return t.ds

    def release(self, tiles):
        for t in tiles:
            if t.ds is not None:
                self.dpool.append(t.ds)
                t.ds = None

    def _deps(self, e, R, W):
        d = []
        for t in R:
            d.extend(t.w)
        for t in W:
            d.extend(t.w)
            d.extend(t.r.values())
        if e == "pe" or not SELF_WAITS:
            d = [x for x in d if not (x[0] == "e" and x[1] == e)]
        return d

    def _wait(self, e, deps):
        wd = self.waited[e]
        for dep in deps:
            if dep[0] == "e":
                key = dep[1]
                if wd.get(key, -1) < dep[2]:
                    wd[key] = dep[2]
                    self.rec[e].append(["w", dep])
            else:
                key = id(dep[1])
                if wd.get(key, -1) < dep[2]:
                    wd[key] = dep[2]
                    self.rec[e].append(["w", dep])

    def op(self, e, f, R=(), W=()):
        if PSUM_EXCL:
            W = [t.root for t in W] + [t.root for t in R if t.root.psum]
            R = [t.root for t in R if not t.root.psum]
        else:
            W = [t.root for t in W]
            R = [t.root for t in R]
        self._wait(e, self._deps(e, R, W))
        idx = len(self.rec[e])
        self.rec[e].append(["i", f, False])
        dep = ("e", e, idx)
        for t in R:
            t.r[e] = dep
        for t in W:
            t.w = [dep]
            t.r = {}
        return dep

    def dma_in(self, tile, f, q="sp"):
        ds = self._dsem(tile)
        deps = [x for x in self._deps(q, (), (tile,)) if not (x[0] == "d" and x[1] is ds[0])]
        self._wait(q, deps)
        ds[1] += 16
        sem, v = ds[0], ds[1]
        self.rec[q].append(["d", f, sem])
        tile.w = [("d", sem, v)]
        tile.r = {}

    def dma_out(self, tile, f, q="sp"):
        self._wait(q, self._deps(q, (tile,), ()))
        ds = self._dsem(tile)
        ds[1] += 16
        sem, v = ds[0], ds[1]
        self.rec[q].append(["d", f, sem])
        tile.r["dma"] = ("d", sem, v)

    def flush(self, collective=None):
        nc = self.nc
        last = {}
        for e in self.ENG:
            idxs = [i for i, r in enumerate(self.rec[e]) if r[0] == "i"]
            if not idxs:
                self.rec[e].append(["i", (lambda h: h.nop()) if e != "pe" else (lambda h: h.nop()), False])
                idxs = [len(self.rec[e]) - 1]
            last[e] = idxs[-1]
        for e in self.ENG:
            for e2 in self.ENG:
                if e2 != e:
                    self.rec[e].append(["w", ("e", e2, last[e2])])
            for ds in self.dall:
                if ds[1] > 0:
                    self.rec[e].append(["w", ("d", ds[0], ds[1])])
        for e in self.ENG:
            for r in self.rec[e]:
                if r[0] == "w" and r[1][0] == "e":
                    self.rec[r[1][1]][r[1][2]][2] = True
        cntmap = {}
        plan = {}
        for e in self.ENG:
            c = self.base[e]
            ep = len(self.sems[e]) - 1
            pl = []
            for i, r in enumerate(self.rec[e]):
                if r[0] == "i" and r[2]:
                    if c >= self.EPOCH:
                        self.sems[e].append(self.gstack.enter_context(nc.semaphore(f"s_{e}{len(self.sems[e])}")))
                        ep += 1
                        c = 0
                    c += 1
                    cntmap[(e, i)] = (self.sems[e][ep], c)
            self.base[e] = c
        recs = self.rec
        ccs = self.cc_sem

        def emit(e, h):
            for i, r in enumerate(recs[e]):
                if r[0] == "w":
                    dep = r[1]
                    if dep[0] == "e":
                        sem, v = cntmap[(dep[1], dep[2])]
                        h.wait_ge(sem, v)
                    else:
                        h.wait_ge(dep[1], dep[2])
                elif r[0] == "i":
                    ins = r[1](h)
                    if r[2]:
                        sem, v = cntmap[(e, i)]
                        ins.then_inc(sem, 1)
                    self.ninstr += 1
                else:
                    r[1](h).then_inc(r[2], 16)
                    self.ninstr += 1
            if e == "pool" and collective is not None:
                self.ncc += 1
                collective(h).then_inc(ccs)
                h.wait_ge(ccs, self.ncc)

        with nc.Block() as block:
            @block.tensor
            def _(h):
                emit("pe", h)

            @block.scalar
            def _(h):
                emit("act", h)

            @block.vector
            def _(h):
                emit("dve", h)

            @block.gpsimd
            def _(h):
                emit("pool", h)

            @block.sync
            def _(h):
                emit("sp", h)
        self.rec = {k: [] for k in self.ENG}
        self.waited = {k: {} for k in self.ENG}
        for t in self.tiles:
            t.w = []
            t.r = {}
        if collective is not None:
            self.op("pool", lambda h: h.nop())
            self.flush()

from concourse.bass_utils import run_bass_kernel_spmd

NCORES = 8
NT = 16384
TB = 512
NB = NT // TB
D = 2048
KC = D // 128
EPS = 1e-6
SEQS = [(0, 8192), (8192, 2048), (10240, 2048), (12288, 2048), (14336, 2048)]
NSH = 18
U0 = 1344
G0 = 1344 + 6784
SCALE = 192 ** -0.5


class Ph:
    def __init__(s, nc, S):
        s.nc, s.S, s.st = nc, S, contextlib.ExitStack()

    _uid = [0]

    def sb(s, name, shape, dt=F32):
        Ph._uid[0] += 1
        name = f"{name}_{Ph._uid[0]}"
        return s.S.tile(s.st.enter_context(s.nc.sbuf_tensor(name, shape, dt)), name)

    def ps(s, name, shape, dt=F32):
        Ph._uid[0] += 1
        name = f"{name}_{Ph._uid[0]}"
        return s.S.tile(s.st.enter_context(s.nc.psum_tensor(name, shape, dt)), name, psum=True)

    def end(s, collective=None):
        s.S.flush(collective)
        s.S.release(s.S.tiles)
        s.S.tiles = []
        s.st.close()


def allgather(nc, src, dst):
    return lambda h: h.collective_compute("AllGather", ALU.bypass, replica_groups=[list(range(NCORES))],
                                          ins=[src.ap().opt()], outs=[dst.ap().opt()])


def load_w(S, ph, name, wdram, nk, ncols, scal=None, rows=128):
    wb = ph.sb(name, [128, nk, ncols], BF)
    st = [ph.sb(f"{name}_st{i}", [128, ncols], F32) for i in range(2)]
    for kc in range(nk):
        s_ = st[kc % 2]
        S.dma_in(s_, lambda h, s_=s_, kc=kc: h.dma_start(out=s_[:, :], in_=wdram[kc * 128:(kc + 1) * 128, :]))
        if scal is None:
            S.op("pool", lambda h, s_=s_, kc=kc: h.tensor_copy(out=wb[:, kc, :], in_=s_[:, :]), R=[s_], W=[wb])
        else:
            S.op("pool", lambda h, s_=s_, kc=kc: h.tensor_scalar(out=wb[:, kc, :], in0=s_[:, :], scalar1=scal[:, kc:kc + 1],
                                                                scalar2=None, op0=ALU.mult), R=[s_, scal], W=[wb])
    return wb


def rstd_from(S, ph_tiles, ps, n, dim, rs, rstd, npart=128):
    S.op("act", lambda h: h.activation(out=rs[0:npart, 0:n], in_=ps[0:npart, 0:n], func=AF.Sqrt, bias=EPS, scale=1.0 / dim), R=[ps], W=[rs])
    S.op("dve", lambda h: h.reciprocal(out=rstd[0:npart, 0:n], in_=rs[0:npart, 0:n]), R=[rs], W=[rstd])


def phase_A(nc, S, dr):
    NCC = 13
    ph = Ph(nc, S)
    g0 = ph.sb("A_g0", [128, KC])
    S.dma_in(g0, lambda h: h.dma_start(out=g0[:, :], in_=dr["g0"][:, :]))
    wb = load_w(S, ph, "A_wb", dr["w_inA"], KC, NCC * 128, scal=g0)
    ones = ph.sb("A_ones", [128, 128], BF)
    xs = [ph.sb(f"A_xs{i}", [128, KC, TB]) for i in range(2)]
    xb = [ph.sb(f"A_xb{i}", [128, KC, TB], BF) for i in range(2)]
    sq = ph.sb("A_sq", [128, KC, TB], BF)
    rs = ph.sb("A_rs", [128, TB])
    rstd = ph.sb("A_rstd", [128, TB])
    ot = [ph.sb(f"A_ot{i}", [128, TB], BF) for i in range(4)]
    gt = [ph.sb(f"A_gt{i}", [128, TB]) for i in range(2)]
    pss = ph.ps("A_pss", [128, TB])
    pso = [ph.ps(f"A_pso{i}", [128, TB]) for i in range(4)]
    S.op("pool", lambda h: h.memset(ones[:, :], 1.0), W=[ones])
    xT = dr["xT"].ap().rearrange("(kc p) t -> p kc t", p=128)

    def load_x(tb):
        t = xs[tb % 2]
        S.dma_in(t, lambda h: h.dma_start(out=t[:, :, :], in_=xT[:, :, tb * TB:(tb + 1) * TB]))
    load_x(0)
    oi = 0
    for tb in range(NB):
        x, b = xs[tb % 2], xb[tb % 2]
        if tb + 1 < NB:
            load_x(tb + 1)
        S.op("pool", lambda h, x=x, b=b: h.tensor_copy(out=b[:, :, :], in_=x[:, :, :]), R=[x], W=[b])
        S.op("act", lambda h, x=x: h.activation(out=sq[:, :, :], in_=x[:, :, :], func=AF.Square), R=[x], W=[sq])
        for kc in range(KC):
            S.op("pe", lambda h, kc=kc: h.matmul(pss[:, :], ones[:, :], sq[:, kc, :], start=(kc == 0), stop=(kc == KC - 1)),
                 R=[ones, sq], W=[pss])
        rstd_from(S, None, pss, TB, D, rs, rstd)
        for cc in range(NCC):
            p = pso[cc % 4]
            for kc in range(KC):
                S.op("pe", lambda h, p=p, kc=kc, cc=cc, b=b: h.matmul(p[:, :], wb[:, kc, cc * 128:(cc + 1) * 128], b[:, kc, :],
                                                                  start=(kc == 0), stop=(kc == KC - 1)), R=[wb, b], W=[p])
            o = ot[oi % 4]
            oi += 1
            tsl = slice(tb * TB, (tb + 1) * TB)
            if cc < 9:
                S.op("dve", lambda h, p=p, o=o: h.tensor_tensor(out=o[:, :], in0=p[:, :], in1=rstd[:, :], op=ALU.mult), R=[p, rstd], W=[o])
                dst = dr["sh_b"][cc * 128:(cc + 1) * 128, tsl] if cc < 3 else dr["rkv_s"][(cc - 3) * 128:(cc - 2) * 128, tsl]
            else:
                g = gt[cc % 2]
                S.op("dve", lambda h, p=p, g=g: h.tensor_tensor(out=g[:, :], in0=p[:, :], in1=rstd[:, :], op=ALU.mult), R=[p, rstd], W=[g])
                S.op("act", lambda h, g=g, o=o: h.activation(out=o[:, :], in_=g[:, :], func=AF.Sigmoid), R=[g], W=[o])
                dst = dr["gate_s"][(cc - 9) * 128:(cc - 8) * 128, tsl]
            S.dma_out(o, lambda h, o=o, dst=dst: h.dma_start(out=dst, in_=o[:, :]))
    ph.end(allgather(nc, dr["sh_b"], dr["sh_g"]))


def sh_rows(i):
    return (i % 8) * 384 + (i // 8) * 128


def phase_M(nc, S, dr):
    ph = Ph(nc, S)
    qg = ph.sb("M_qg", [128, 6])
    kg = ph.sb("M_kg", [128, 4])
    S.dma_in(qg, lambda h: h.dma_start(out=qg[:, :], in_=dr["qng"][:, :]))
    S.dma_in(kg, lambda h: h.dma_start(out=kg[:, :], in_=dr["kvg"][:, :]))
    wq = load_w(S, ph, "M_wq", dr["w_uq"], 6, 512, scal=qg)
    wkv = load_w(S, ph, "M_wkv", dr["w_ukv"], 4, 512, scal=kg)
    ones = ph.sb("M_ones", [128, 128], BF)
    ident = ph.sb("M_id", [128, 128], F32)
    S.op("pool", lambda h: h.memset(ones[:, :], 1.0), W=[ones])
    S.dma_in(ident, lambda h: h.dma_start(out=ident[:, :], in_=dr["ident"][:, :]))
    LMAX = 8192
    qn = ph.sb("M_qn", [128, LMAX], BF)
    qr = ph.sb("M_qr", [64, LMAX], BF)
    kn = ph.sb("M_kn", [128, LMAX], BF)
    kr = ph.sb("M_kr", [64, LMAX], BF)
    va = ph.sb("M_va", [128, LMAX // 128, 132], BF)
    S.op("pool", lambda h: h.memset(va[:, :, 128:129], 1.0), W=[va])
    cq = ph.sb("M_cq", [128, 6, TB], BF)
    ckv = ph.sb("M_ckv", [128, 4, TB], BF)
    ckvs = ph.sb("M_ckvs", [128, 4, TB], BF)
    kro = ph.sb("M_kro", [64, 2, TB], BF)
    sqq = ph.sb("M_sqq", [128, 6, TB], BF)
    sqk = ph.sb("M_sqk", [128, 4, TB], BF)
    cs = ph.sb("M_cs", [64, 2, TB])
    rs = ph.sb("M_rs", [128, TB])
    rq = ph.sb("M_rq", [128, TB])
    rk = ph.sb("M_rk", [128, TB])
    rsc = ph.sb("M_rsc", [128, 4])
    rkc = ph.sb("M_rkc", [128, 4])
    t1 = ph.sb("M_t1", [64, TB])
    t2 = ph.sb("M_t2", [64, TB])
    pt = [ph.sb(f"M_pt{i}", [128, TB], BF) for i in range(2)]
    on = ph.sb("M_on", [128, 128], F32)
    rinv = ph.sb("M_rinv", [128, 1])
    yt = [ph.sb(f"M_yt{i}", [128, TB], BF) for i in range(2)]
    psA = [ph.ps(f"M_psA{i}", [128, TB]) for i in range(2)]
    psO = [ph.ps(f"M_psO{i}", [128, 512]) for i in range(4)]
    psB = ph.ps("M_psB", [128, TB])
    psT = ph.ps("M_psT", [128, 512], F32)
    shg = dr["sh_g"]
    yi = 0
    for hl in range(2):
        for (s0, L) in SEQS:
            for tb in range(L // TB):
                tsl = slice(s0 + tb * TB, s0 + (tb + 1) * TB)
                lsl = slice(tb * TB, (tb + 1) * TB)
                for i in range(6):
                    S.dma_in(cq, lambda h, i=i, tsl=tsl: h.dma_start(out=cq[:, i, :], in_=shg[sh_rows(i):sh_rows(i) + 128, tsl]))
                for i in range(4):
                    S.dma_in(ckv, lambda h, i=i, tsl=tsl: h.dma_start(out=ckv[:, i, :], in_=shg[sh_rows(6 + i):sh_rows(6 + i) + 128, tsl]))
                S.dma_in(kro, lambda h, tsl=tsl: h.dma_start(out=kro[:, 0, :], in_=shg[sh_rows(10):sh_rows(10) + 64, tsl]))
                S.dma_in(kro, lambda h, tsl=tsl: h.dma_start(out=kro[:, 1, :], in_=shg[sh_rows(17):sh_rows(17) + 64, tsl]))
                S.dma_in(cs, lambda h, lsl=lsl: h.dma_start(out=cs[:, 0, :], in_=dr["cos2"][:, lsl]))
                S.dma_in(cs, lambda h, lsl=lsl: h.dma_start(out=cs[:, 1, :], in_=dr["sin2"][:, lsl]))
                S.op("act", lambda h: h.activation(out=sqq[:, :, :], in_=cq[:, :, :], func=AF.Square), R=[cq], W=[sqq])
                S.op("act", lambda h: h.activation(out=sqk[:, :, :], in_=ckv[:, :, :], func=AF.Square), R=[ckv], W=[sqk])
                p = psA[0]
                for i in range(6):
                    S.op("pe", lambda h, i=i, p=p: h.matmul(p[:, :], ones[:, :], sqq[:, i, :], start=(i == 0), stop=(i == 5)), R=[ones, sqq], W=[p])
                rstd_from(S, None, p, TB, 768, rs, rq)
                p = psA[1]
                for i in range(4):
                    S.op("pe", lambda h, i=i, p=p: h.matmul(p[:, :], ones[:, :], sqk[:, i, :], start=(i == 0), stop=(i == 3)), R=[ones, sqk], W=[p])
                rstd_from(S, None, p, TB, 512, rs, rk)
                for i in range(4):
                    S.op("pool" if i % 2 else "dve", lambda h, i=i: h.tensor_tensor(out=ckvs[:, i, :], in0=ckv[:, i, :], in1=rk[:, :], op=ALU.mult),
                         R=[ckv, rk], W=[ckvs])
                c0 = hl * 256
                p = psA[0]
                for i in range(6):
                    S.op("pe", lambda h, i=i, p=p, c0=c0: h.matmul(p[:, :], wq[:, i, c0:c0 + 128], cq[:, i, :], start=(i == 0), stop=(i == 5)), R=[wq, cq], W=[p])
                S.op("dve", lambda h, p=p, lsl=lsl: h.tensor_tensor(out=qn[:, lsl], in0=p[:, :], in1=rq[:, :], op=ALU.mult), R=[p, rq], W=[qn])
                p = psA[1]
                for i in range(6 if 'q' not in PHASES else 0):
                    S.op("pe", lambda h, i=i, p=p, c0=c0: h.matmul(p[0:64, :], wq[:, i, c0 + 128:c0 + 192], cq[:, i, :], start=(i == 0), stop=(i == 5)), R=[wq, cq], W=[p])
                S.op("dve", lambda h, p=p: h.tensor_tensor(out=t1[:, :], in0=p[0:64, :], in1=cs[:, 0, :], op=ALU.mult), R=[p, cs], W=[t1])
                p = psA[0]
                for i in range(6):
                    S.op("pe", lambda h, i=i, p=p, c0=c0: h.matmul(p[0:64, :], wq[:, i, c0 + 192:c0 + 256], cq[:, i, :], start=(i == 0), stop=(i == 5)), R=[wq, cq], W=[p])
                S.op("dve", lambda h, p=p: h.tensor_tensor(out=t2[:, :], in0=p[0:64, :], in1=cs[:, 1, :], op=ALU.mult), R=[p, cs], W=[t2])
                S.op("pool", lambda h: h.tensor_tensor(out=t1[:, :], in0=t1[:, :], in1=t2[:, :], op=ALU.add), R=[t2], W=[t1])
                S.op("dve", lambda h, lsl=lsl: h.tensor_tensor(out=qr[:, lsl], in0=t1[:, :], in1=rq[0:64, :], op=ALU.mult), R=[t1, rq], W=[qr])
                k0 = hl * 256
                p = psA[1]
                for i in range(4):
                    S.op("pe", lambda h, i=i, p=p, k0=k0: h.matmul(p[:, :], wkv[:, i, k0:k0 + 128], ckvs[:, i, :], start=(i == 0), stop=(i == 3)), R=[wkv, ckvs], W=[p])
                S.op("dve", lambda h, p=p, lsl=lsl: h.tensor_copy(out=kn[:, lsl], in_=p[:, :]), R=[p], W=[kn])
                S.op("dve", lambda h: h.tensor_tensor(out=t1[:, :], in0=kro[:, 0, :], in1=cs[:, 0, :], op=ALU.mult), R=[kro, cs], W=[t1])
                S.op("pool", lambda h: h.tensor_tensor(out=t2[:, :], in0=kro[:, 1, :], in1=cs[:, 1, :], op=ALU.mult), R=[kro, cs], W=[t2])
                S.op("dve", lambda h, lsl=lsl: h.tensor_tensor(out=kr[:, lsl], in0=t1[:, :], in1=t2[:, :], op=ALU.add), R=[t1, t2], W=[kr])
                for j in range(4 if 'v' not in PHASES else 0):
                    p = psO[j]
                    for i in range(4):
                        S.op("pe", lambda h, i=i, j=j, p=p, k0=k0: h.matmul(p[:, 0:128], ckvs[:, i, j * 128:(j + 1) * 128], wkv[:, i, k0 + 128:k0 + 256],
                                                                  start=(i == 0), stop=(i == 3)), R=[wkv, ckvs], W=[p])
                    S.op("dve", lambda h, j=j, p=p, tb=tb: h.tensor_copy(out=va[:, tb * 4 + j, 0:128], in_=p[:, 0:128]), R=[p], W=[va])
            nkb = L // 128
            for qb in range(L // TB):
                qsl = slice(qb * TB, (qb + 1) * TB)
                for kb in range(nkb):
                    ksl = slice(kb * 128, (kb + 1) * 128)
                    p = psA[kb % 2]
                    S.op("pe", lambda h, p=p, ksl=ksl, qsl=qsl: h.matmul(p[:, :], kn[:, ksl], qn[:, qsl], start=True, stop=False), R=[kn, qn], W=[p])
                    S.op("pe", lambda h, p=p, ksl=ksl, qsl=qsl: h.matmul(p[:, :], kr[:, ksl], qr[:, qsl], start=False, stop=True), R=[kr, qr], W=[p])
                    e = pt[kb % 2]
                    S.op("act", lambda h, p=p, e=e: h.activation(out=e[:, :], in_=p[:, :], func=AF.Exp, scale=SCALE), R=[p], W=[e])
                    for j in range(4):
                        S.op("pe", lambda h, j=j, e=e, kb=kb, nkb=nkb: h.matmul(psO[j][:, 0:129], e[:, j * 128:(j + 1) * 128], va[:, kb, 0:129],
                                                                    start=(kb == 0), stop=(kb == nkb - 1)), R=[e, va], W=[psO[j]])
                y = yt[yi % 2]
                yi += 1
                for j in range(4):
                    S.op("dve", lambda h, j=j: h.reciprocal(out=rinv[:, :], in_=psO[j][:, 128:129]), R=[psO[j]], W=[rinv])
                    S.op("dve", lambda h, j=j: h.tensor_scalar(out=on[:, :], in0=psO[j][:, 0:128], scalar1=rinv[:, 0:1], scalar2=None, op0=ALU.mult),
                         R=[psO[j], rinv], W=[on])
                    S.op("pe", lambda h, j=j: h.transpose(psT[:, j * 128:(j + 1) * 128], on[:, :], ident[:, :]), R=[on, ident], W=[psT])
                S.op("act", lambda h, y=y: h.activation(out=y[:, :], in_=psT[:, 0:512], func=AF.Copy), R=[psT], W=[y])
                dst = dr["y_b"][hl * 128:(hl + 1) * 128, s0 + qb * TB:s0 + (qb + 1) * TB]
                S.dma_out(y, lambda h, y=y, dst=dst: h.dma_start(out=dst, in_=y[:, :]))
    ph.end()


def lin_phase(nc, S, ph, name, src_ap, nk, wb, ncc, epi, tbs=TB, pre=None, ks=None):
    xb = [ph.sb(f"{name}_xb{i}", [128, nk, tbs], BF) for i in range(2)]
    pso = [ph.ps(f"{name}_ps{i}", [128, tbs]) for i in range(3)]
    nb = NT // tbs

    def load(tb):
        t = xb[tb % 2]
        tsl = slice(tb * tbs, (tb + 1) * tbs)
        for f in src_ap(t, tsl):
            S.dma_in(t, f)
    load(0)
    pi = 0
    for tb in range(nb):
        tsl = slice(tb * tbs, (tb + 1) * tbs)
        if tb + 1 < nb:
            load(tb + 1)
        b = xb[tb % 2]
        if pre is not None:
            pre(tb, tsl)
        for cc in range(ncc):
            p = pso[pi % 3]
            pi += 1
            kl = list(range(nk)) if ks is None else ks(cc)
            wc = cc if ks is None else cc % 2
            for n, kc in enumerate(kl):
                S.op("pe", lambda h, p=p, kc=kc, wc=wc, b=b, n=n, kl=kl: h.matmul(p[:, :], wb[:, kc, wc * 128:(wc + 1) * 128], b[:, kc, :],
                                                                              start=(n == 0), stop=(n == len(kl) - 1)), R=[wb, b], W=[p])
            epi(tb, cc, p, tsl)


def ss_partial(S, ph, name):
    ones = ph.sb(f"{name}_ones", [128, 128], BF)
    S.op("pool", lambda h: h.memset(ones[:, :], 1.0), W=[ones])
    sq = [ph.sb(f"{name}_sq{i}", [128, TB], BF) for i in range(2)]
    row = ph.sb(f"{name}_row", [1, TB])
    pss = ph.ps(f"{name}_pss", [128, TB])

    def f(cc, ncc, src, n, tsl, dst):
        s_ = sq[cc % 2]
        S.op("act", lambda h: h.activation(out=s_[:, 0:n], in_=src[:, 0:n], func=AF.Square), R=[src], W=[s_])
        S.op("pe", lambda h: h.matmul(pss[:, 0:n], ones[:, :], s_[:, 0:n], start=(cc == 0), stop=(cc == ncc - 1)), R=[ones, s_], W=[pss])
        if cc == ncc - 1:
            S.op("act", lambda h: h.activation(out=row[0:1, 0:n], in_=pss[0:1, 0:n], func=AF.Copy), R=[pss], W=[row])
            S.dma_out(row, lambda h: h.dma_start(out=dst[0:1, tsl], in_=row[0:1, 0:n]))
    return f


def ss_total(S, ph, name, dim):
    onesf = ph.sb(f"{name}_onesf", [8, 128])
    S.op("pool", lambda h: h.memset(onesf[:, :], 1.0), W=[onesf])
    ssl = ph.sb(f"{name}_ssl", [8, TB])
    rs = ph.sb(f"{name}_rs", [128, TB])
    rstd = ph.sb(f"{name}_rstd", [128, TB])
    pst = ph.ps(f"{name}_pst", [128, TB])

    def f(ssg, tsl, n):
        S.dma_in(ssl, lambda h: h.dma_start(out=ssl[:, 0:n], in_=ssg[:, tsl]))
        S.op("pe", lambda h: h.matmul(pst[:, 0:n], onesf[:, :], ssl[:, 0:n], start=True, stop=True), R=[onesf, ssl], W=[pst])
        rstd_from(S, None, pst, n, dim, rs, rstd)
        return rstd
    return f


def one_dma(ap_fn):
    return lambda t, tsl: [lambda h: h.dma_start(out=t[:, :, :], in_=ap_fn(tsl))]


def phase_C(nc, S, dr):
    ph = Ph(nc, S)
    wb = load_w(S, ph, "C_wb", dr["w_br"], 32, 256)
    gt = [ph.sb(f"C_gt{i}", [128, 4, TB], BF) for i in range(2)]
    tmp = [ph.sb(f"C_tmp{i}", [128, TB]) for i in range(2)]
    t2 = ph.sb("C_t2", [128, TB])
    ot = [ph.sb(f"C_ot{i}", [128, TB], BF) for i in range(2)]
    yg = dr["y_g"].ap().rearrange("(k p) t -> p k t", p=128)
    gs = dr["gate_s"].ap().rearrange("(k p) t -> p k t", p=128)
    st = {}

    def pre(tb, tsl):
        g = gt[tb % 2]
        S.dma_in(g, lambda h: h.dma_start(out=g[:, :, :], in_=gs[:, :, tsl]))
        st["g"] = g

    def ks(cc):
        br = cc // 2
        return [r * 4 + br * 2 + j for r in range(8) for j in range(2)]

    def epi(tb, cc, p, tsl):
        g = st["g"]
        j = cc % 2
        if cc < 2:
            S.op("dve", lambda h: h.tensor_tensor(out=tmp[j][:, :], in0=p[:, :], in1=g[:, j, :], op=ALU.mult), R=[p, g], W=[tmp[j]])
        else:
            o = ot[j]
            S.op("dve", lambda h: h.tensor_tensor(out=t2[:, :], in0=p[:, :], in1=g[:, 2 + j, :], op=ALU.mult), R=[p, g], W=[t2])
            S.op("pool", lambda h: h.tensor_tensor(out=o[:, :], in0=t2[:, :], in1=tmp[j][:, :], op=ALU.add), R=[t2, tmp[j]], W=[o])
            dst = dr["mix_b"][j * 128:(j + 1) * 128, tsl]
            S.dma_out(o, lambda h: h.dma_start(out=dst, in_=o[:, :]))
    lin_phase(nc, S, ph, "C", one_dma(lambda tsl: yg[:, :, tsl]), 32, wb, 4, epi, pre=pre, ks=ks)
    ph.end(allgather(nc, dr["mix_b"], dr["mix_g"]))


def phase_D(nc, S, dr):
    ph = Ph(nc, S)
    wb = load_w(S, ph, "D_wb", dr["w_o"], 16, 256)
    ssp = ss_partial(S, ph, "D")
    mt = [ph.sb(f"D_mt{i}", [128, TB]) for i in range(2)]
    mg = dr["mix_g"].ap().rearrange("(k p) t -> p k t", p=128)

    def epi(tb, cc, p, tsl):
        m = mt[cc % 2]
        S.op("dve", lambda h: h.tensor_copy(out=m[:, :], in_=p[:, :]), R=[p], W=[m])
        dst = dr["mix_s"][cc * 128:(cc + 1) * 128, tsl]
        S.dma_out(m, lambda h: h.dma_start(out=dst, in_=m[:, :]))
        ssp(cc, 2, m, TB, tsl, dr["ss1_b"])
    lin_phase(nc, S, ph, "D", one_dma(lambda tsl: mg[:, :, tsl]), 16, wb, 2, epi)
    ph.end(allgather(nc, dr["ss1_b"], dr["ss1_g"]))


def phase_E(nc, S, dr):
    ph = Ph(nc, S)
    gn = ph.sb("E_gn", [128, 6])
    S.dma_in(gn, lambda h: h.dma_start(out=gn[:, :], in_=dr["gn"][:, :]))
    tot = ss_total(S, ph, "E", D)
    ssp = ss_partial(S, ph, "E2")
    mt = [ph.sb(f"E_mt{i}", [128, 2, TB]) for i in range(2)]
    xt = [ph.sb(f"E_xt{i}", [128, 2, TB]) for i in range(2)]
    x1 = [ph.sb(f"E_x1{i}", [128, TB]) for i in range(2)]
    tt = ph.sb("E_tt", [128, TB])
    hb = [ph.sb(f"E_hb{i}", [128, TB], BF) for i in range(2)]
    ms = dr["mix_s"].ap().rearrange("(k p) t -> p k t", p=128)
    xs = dr["xTs"].ap().rearrange("(k p) t -> p k t", p=128)
    for tb in range(NB):
        tsl = slice(tb * TB, (tb + 1) * TB)
        m, x = mt[tb % 2], xt[tb % 2]
        S.dma_in(m, lambda h, m=m, tsl=tsl: h.dma_start(out=m[:, :, :], in_=ms[:, :, tsl]))
        S.dma_in(x, lambda h, x=x, tsl=tsl: h.dma_start(out=x[:, :, :], in_=xs[:, :, tsl]))
        rstd = tot(dr["ss1_g"], tsl, TB)
        for cc in range(2):
            o = x1[cc]
            S.op("dve", lambda h, m=m, cc=cc: h.tensor_tensor(out=tt[:, :], in0=m[:, cc, :], in1=rstd[:, :], op=ALU.mult), R=[m, rstd], W=[tt])
            S.op("dve", lambda h, o=o, x=x, cc=cc: h.scalar_tensor_tensor(out=o[:, :], in0=tt[:, :], scalar=gn[:, cc:cc + 1], in1=x[:, cc, :],
                                                                       op0=ALU.mult, op1=ALU.add), R=[tt, gn, x], W=[o])
            dst = dr["x1_s"][cc * 128:(cc + 1) * 128, tsl]
            S.dma_out(o, lambda h, o=o, dst=dst: h.dma_start(out=dst, in_=o[:, :]))
            hh = hb[cc]
            S.op("pool", lambda h, hh=hh, o=o, cc=cc: h.tensor_scalar(out=hh[:, :], in0=o[:, :], scalar1=gn[:, 2 + cc:3 + cc], scalar2=None, op0=ALU.mult),
                 R=[o, gn], W=[hh])
            dst2 = dr["h2_b"][cc * 128:(cc + 1) * 128, tsl]
            S.dma_out(hh, lambda h, hh=hh, dst2=dst2: h.dma_start(out=dst2, in_=hh[:, :]))
            ssp(cc, 2, o, TB, tsl, dr["ss2_b"])
    ph.end(allgather(nc, dr["h2_b"], dr["h2_g"]))
    ph = Ph(nc, S)
    nop = ph.sb("E_nop", [128, 8])
    S.op("pool", lambda h: h.memset(nop[:, :], 0.0), W=[nop])
    ph.end(allgather(nc, dr["ss2_b"], dr["ss2_g"]))


def phase_G(nc, S, dr):
    ph = Ph(nc, S)
    wb = load_w(S, ph, "G_wb", dr["w_up"], 16, 1024)
    tot = ss_total(S, ph, "G", D)
    tt = [ph.sb(f"G_tt{i}", [128, TB]) for i in range(2)]
    ot = [ph.sb(f"G_ot{i}", [128, TB], BF) for i in range(3)]
    hg = dr["h2_g"].ap().rearrange("(k p) t -> p k t", p=128)
    st = {"n": 0}

    def pre(tb, tsl):
        st["r"] = tot(dr["ss2_g"], tsl, TB)

    def epi(tb, cc, p, tsl):
        rstd = st["r"]
        t = tt[cc % 2]
        o = ot[st["n"] % 3]
        st["n"] += 1
        S.op("dve", lambda h: h.scalar_tensor_tensor(out=t[:, :], in0=p[:, :], scalar=0.0, in1=rstd[:, :], op0=ALU.max, op1=ALU.mult),
             R=[p, rstd], W=[t])
        S.op("act", lambda h: h.activation(out=o[:, :], in_=t[:, :], func=AF.Square), R=[t], W=[o])
        dst = dr["hid_b"][cc * 128:(cc + 1) * 128, tsl]
        S.dma_out(o, lambda h: h.dma_start(out=dst, in_=o[:, :]))
    lin_phase(nc, S, ph, "G", one_dma(lambda tsl: hg[:, :, tsl]), 16, wb, 8, epi, pre=pre)
    ph.end(allgather(nc, dr["hid_b"], dr["hid_g"]))


def phase_H(nc, S, dr):
    ph = Ph(nc, S)
    wb = load_w(S, ph, "H_wb", dr["w_dn"], 64, 256)
    ssp = ss_partial(S, ph, "H")
    mt = [ph.sb(f"H_mt{i}", [128, 256]) for i in range(2)]
    hg = dr["hid_g"].ap().rearrange("(k p) t -> p k t", p=128)

    def epi(tb, cc, p, tsl):
        m = mt[cc % 2]
        S.op("dve", lambda h: h.tensor_copy(out=m[:, :], in_=p[:, :]), R=[p], W=[m])
        dst = dr["ff_s"][cc * 128:(cc + 1) * 128, tsl]
        S.dma_out(m, lambda h: h.dma_start(out=dst, in_=m[:, :]))
        ssp(cc, 2, m, 256, tsl, dr["ss3_b"])
    lin_phase(nc, S, ph, "H", one_dma(lambda tsl: hg[:, :, tsl]), 64, wb, 2, epi, tbs=256)
    ph.end(allgather(nc, dr["ss3_b"], dr["ss3_g"]))


def phase_I(nc, S, dr):
    ph = Ph(nc, S)
    gn = ph.sb("I_gn", [128, 6])
    S.dma_in(gn, lambda h: h.dma_start(out=gn[:, :], in_=dr["gn"][:, :]))
    tot = ss_total(S, ph, "I", D)
    ft = [ph.sb(f"I_ft{i}", [128, 2, TB]) for i in range(2)]
    xt = [ph.sb(f"I_xt{i}", [128, 2, TB]) for i in range(2)]
    yo = [ph.sb(f"I_yo{i}", [128, TB]) for i in range(2)]
    tt = ph.sb("I_tt", [128, TB])
    fs = dr["ff_s"].ap().rearrange("(k p) t -> p k t", p=128)
    xs = dr["x1_s"].ap().rearrange("(k p) t -> p k t", p=128)
    for tb in range(NB):
        tsl = slice(tb * TB, (tb + 1) * TB)
        f, x = ft[tb % 2], xt[tb % 2]
        S.dma_in(f, lambda h, f=f, tsl=tsl: h.dma_start(out=f[:, :, :], in_=fs[:, :, tsl]))
        S.dma_in(x, lambda h, x=x, tsl=tsl: h.dma_start(out=x[:, :, :], in_=xs[:, :, tsl]))
        rstd = tot(dr["ss3_g"], tsl, TB)
        for cc in range(2):
            o = yo[cc]
            S.op("dve", lambda h, f=f, cc=cc: h.tensor_tensor(out=tt[:, :], in0=f[:, cc, :], in1=rstd[:, :], op=ALU.mult), R=[f, rstd], W=[tt])
            S.op("dve", lambda h, o=o, x=x, cc=cc: h.scalar_tensor_tensor(out=o[:, :], in0=tt[:, :], scalar=gn[:, 4 + cc:5 + cc], in1=x[:, cc, :],
                                                                       op0=ALU.mult, op1=ALU.add), R=[tt, gn, x], W=[o])
            dst = dr["yT"][cc * 128:(cc + 1) * 128, tsl]
            S.dma_out(o, lambda h, o=o, dst=dst: h.dma_start(out=dst, in_=o[:, :]))
    ph.end()


RW_INPUTS = {"rw_vec": [128, 9 * 256], "w_lora": [128, 4 * 256], "w_g2": [256, 256], "mu_rkv": [128, 12], "mu_sh": [128, 12],
             "rw_M": [128, 6 * 128], "identf": [128, 128]}
CDEC = 0.6065306597126334
RSTAGE = 9
RNBLK = 999
GN_EPS = 64e-5
LORA_CH = (11, 12, 13, 14, 15, 16)


def rw_host(inp, c):
    f = lambda k: np.asarray(inp[k], dtype=np.float32)[0]
    sl = slice(256 * c, 256 * (c + 1))
    vec = np.stack([f("rwkv_w0_f")[sl], f("rwkv_w0_b")[sl], f("rwkv_a0_f")[sl], f("rwkv_a0_b")[sl], f("rwkv_k_k")[sl], f("rwkv_k_a")[sl],
                    f("rwkv_r_k").reshape(-1)[sl], f("rwkv_ln_w")[sl], f("rwkv_ln_b")[sl]], 0).reshape(1, -1)
    m = {"rw_vec": np.ascontiguousarray(np.repeat(vec, 128, 0))}
    wl = np.zeros((128, 4, 256), np.float32)
    for j, k in enumerate(("rwkv_w2_f", "rwkv_w2_b", "rwkv_a2_f", "rwkv_a2_b")):
        wl[:96, j] = f(k)[:, sl]
    m["w_lora"] = wl.reshape(128, -1)
    m["w_g2"] = np.ascontiguousarray(f("rwkv_g2")[:, sl])
    mp, mn = f("mu_prev"), f("mu_next")
    mu = np.zeros((128, 6, 2), np.float32)
    for j, base in enumerate((0, 0, 2048, 2048, 4096, 4096)):
        idx = base + 256 * c + 128 * (j % 2) + np.arange(128)
        mu[:, j, 0], mu[:, j, 1] = mp[idx], mn[idx]
    m["mu_rkv"] = mu.reshape(128, -1)
    mu = np.zeros((128, 6, 2), np.float32)
    for j in range(4):
        idx = 6144 + 96 * j + np.arange(96)
        mu[:96, j, 0], mu[:96, j, 1] = mp[idx], mn[idx]
    for j in range(2):
        idx = 6528 + 128 * j + np.arange(128)
        mu[:, 4 + j, 0], mu[:, 4 + j, 1] = mp[idx], mn[idx]
    m["mu_sh"] = mu.reshape(128, -1)
    i = np.arange(128)
    same = (i[:, None] // 64) == (i[None, :] // 64)
    M = np.zeros((128, 6, 128), np.float32)
    lt = same & (i[:, None] < i[None, :])
    gt = same & (i[:, None] > i[None, :])
    eq = np.eye(128, dtype=bool)
    M[:, 0], M[:, 1], M[:, 2] = lt, lt | eq, lt.T
    M[:, 3], M[:, 4], M[:, 5] = gt, gt | eq, gt.T
    m["rw_M"] = M.reshape(128, -1)
    m["identf"] = np.eye(128, dtype=np.float32)
    return m


def phase_R(nc, S, dr):
    for d in (0, 1):
        rwkv_pass(nc, S, dr, d)


def rwkv_pass(nc, S, dr, d):
    ph = Ph(nc, S)
    sb, ps = ph.sb, ph.ps
    op = S.op
    vec = sb("R_vec", [128, 9, 256])
    S.dma_in(vec, lambda h: h.dma_start(out=vec[:, :, :], in_=dr["rw_vec"].ap().rearrange("p (a b) -> p a b", b=256)))
    W0F, W0B, A0F, A0B, KK_, KA_, RK_, LNW, LNB = range(9)
    wl = load_w(S, ph, "R_wl", dr["w_lora"], 1, 1024)
    wg2 = load_w(S, ph, "R_wg2", dr["w_g2"], 2, 256)
    mur = sb("R_mur", [128, 6, 2]); mus = sb("R_mus", [128, 6, 2])
    S.dma_in(mur, lambda h: h.dma_start(out=mur[:, :, :], in_=dr["mu_rkv"].ap().rearrange("p (a b) -> p a b", b=2)))
    S.dma_in(mus, lambda h: h.dma_start(out=mus[:, :, :], in_=dr["mu_sh"].ap().rearrange("p (a b) -> p a b", b=2)))
    c0r = sb("R_c0r", [128, 6]); c0s = sb("R_c0s", [128, 6])
    for (c0t, mut) in ((c0r, mur), (c0s, mus)):
        op("dve", lambda h, c0t=c0t, mut=mut: h.tensor_tensor(out=c0t[:, :], in0=mut[:, :, 0], in1=mut[:, :, 1], op=ALU.add), R=[mut], W=[c0t])
        op("dve", lambda h, c0t=c0t: h.tensor_scalar(out=c0t[:, :], in0=c0t[:, :], scalar1=-1.0, scalar2=1.0, op0=ALU.mult, op1=ALU.add), W=[c0t])
    Mf = sb("R_M", [128, 768])
    S.dma_in(Mf, lambda h: h.dma_start(out=Mf[:, :], in_=dr["rw_M"][:, :]))
    MS, MI, MST = slice(384 * d, 384 * d + 128), slice(384 * d + 128, 384 * d + 256), slice(384 * d + 256, 384 * d + 384)
    MSI = slice(384 * d, 384 * d + 256)
    TL = 63 if d == 0 else 0
    identf = sb("R_idf", [128, 128])
    S.dma_in(identf, lambda h: h.dma_start(out=identf[:, :], in_=dr["identf"][:, :]))
    identb = sb("R_idb", [128, 128], BF)
    op("pool", lambda h: h.tensor_copy(out=identb[:, :], in_=identf[:, :]), R=[identf], W=[identb])
    onesf = sb("R_onesf", [128, 2])
    op("pool", lambda h: h.memset(onesf[:, :], 1.0), W=[onesf])
    uf = [sb(f"R_uf{i}", [128, 6, 514], BF) for i in range(2)]
    lf = [sb(f"R_lf{i}", [128, 6, 514], BF) for i in range(2)]
    usr = sb("R_usr", [128, 6, 512], BF)
    lor = sb("R_lor", [128, 6, 512], BF)
    tA = [sb(f"R_tA{i}", [128, 512]) for i in range(2)]
    rkvT = sb("R_rkvT", [128, 768], BF)
    F = lambda n: sb(n, [128, 256])
    kkr, sqk, kk = F("R_kkr"), F("R_sqk"), F("R_kk")
    ssum = sb("R_ssum", [128, 4]); nrm = sb("R_nrm", [128, 4]); rn = sb("R_rn", [128, 4])
    zz, sg, al, alo = F("R_zz"), F("R_sg"), F("R_al"), F("R_alo")
    Ep, Em, Ea, Ed = F("R_Ep"), F("R_Em"), F("R_Ea"), F("R_Ed")
    t1, kd, bd = F("R_t1"), F("R_kd"), F("R_bd")
    gC = sb("R_gC", [128, 4])
    tm = sb("R_tm", [128, 6, 256], BF)
    cm = sb("R_cm", [128, 2, 512], BF)
    AB = [sb(f"R_AB{i}", [128, 256], BF) for i in range(4)]
    AK = [sb(f"R_AK{i}", [128, 256], BF) for i in range(4)]
    NTt = [sb(f"R_NT{i}", [128, 128], BF) for i in range(4)]
    Xs = [[sb(f"R_X{b}{i}", [128, 128], BF) for i in range(5)] for b in range(4)]
    XTs = [[sb(f"R_XT{b}{i}", [128, 128], BF) for i in range(4)] for b in range(4)]
    Zt = [[sb(f"R_Z{b}{i}", [128, 128], BF) for i in range(2)] for b in range(4)]
    Pm = [[sb(f"R_Pm{b}{i}", [128, 64], BF) for i in range(2)] for b in range(4)]
    Gt = [[sb(f"R_G{b}{i}", [128, 64], BF) for i in range(2)] for b in range(4)]
    st = [sb(f"R_st{h}", [128, 64], BF) for h in range(4)]
    Yt = sb("R_Yt", [128, 256])
    if d == 1:
        yfl, ysum, yn, yo = F("R_yfl"), F("R_ysum"), F("R_yn"), F("R_yo")
        stats = sb("R_stats", [128, 4, 6]); mv = sb("R_mv", [128, 4, 2])
        sd = sb("R_sd", [128, 4]); rsd = sb("R_rsd", [128, 4]); bs = sb("R_bs", [128, 4])
        ybt = sb("R_ybt", [128, 2, 128], BF)
    bT = ps("R_pT", [128, 512])
    bT2 = ps("R_pT2", [128, 512])
    bZ = ps("R_pZ", [128, 512])
    bC = ps("R_pC", [128, 512])
    bD = ps("R_pD", [128, 512])
    bE = ps("R_pE", [128, 512])
    bF_ = ps("R_pF", [128, 512])
    bG = ps("R_pG", [128, 512])
    sub = lambda par, a, b, nm: S.tile(par[:, a:b], nm, parent=par)
    pF = [sub(bF_, 0, 128, "pF0"), sub(bT, 0, 128, "pF1"), sub(bT2, 0, 128, "pF2"), sub(bC, 0, 128, "pF3")]
    pP = [sub(bG, 0, 64, "pP0"), sub(bE, 0, 64, "pP1")]
    pGm = [sub(bG, 64, 128, "pG0"), sub(bE, 64, 128, "pG1")]
    pS = [sub(bZ, 0, 64, "pS0"), sub(bD, 256, 320, "pS1")]
    pYs = [sub(bZ, 64, 128, "pY0"), sub(bD, 320, 384, "pY1")]
    pz = sub(bF_, 128, 256, "pz")
    ev = {"n": 0}

    act_banks = {id(bT), id(bC), id(bD)}
    hbank = [bF_, bT, bT2, bC]
    _subs = {}

    def sub_cache(par, a_, b_):
        k_ = (id(par), a_, b_)
        if k_ not in _subs:
            _subs[k_] = S.tile(par[:, a_:b_], f"sub{len(_subs)}", parent=par)
        return _subs[k_]

    def evac(out_t, out_ap, in_t, in_ap):
        if id(in_t.root) in act_banks:
            op("act", lambda h: h.activation(out=out_ap, in_=in_ap, func=AF.Copy), R=[in_t], W=[out_t])
        else:
            op("dve", lambda h: h.tensor_copy(out=out_ap, in_=in_ap), R=[in_t], W=[out_t])

    def mm(out_t, out_ap, l_t, l_ap, r_t, r_ap, start=True, stop=True):
        op("pe", lambda h: h.matmul(out_ap, l_ap, r_ap, start=start, stop=stop), R=[l_t, r_t], W=[out_t])

    def tt(e, out_t, out_ap, a_t, a_ap, b_t, b_ap, o):
        op(e, lambda h: h.tensor_tensor(out=out_ap, in0=a_ap, in1=b_ap, op=o), R=[a_t, b_t], W=[out_t])

    def stt(out_t, out_ap, a_t, a_ap, scalar, b_t, b_ap, o0, o1, extra=()):
        op("dve", lambda h: h.scalar_tensor_tensor(out=out_ap, in0=a_ap, scalar=scalar, in1=b_ap, op0=o0, op1=o1), R=[a_t, b_t, *extra], W=[out_t])

    def act(out_t, out_ap, in_t, in_ap, func, scale=1.0, bias=0.0):
        op("act", lambda h: h.activation(out=out_ap, in_=in_ap, func=func, scale=scale, bias=bias), R=[in_t], W=[out_t])

    rk = dr["rkv_s"].ap().rearrange("(k p) t -> p k t", p=128)
    shg = dr["sh_g"]
    seq_starts = {s0 for s0, L in SEQS}
    seq_ends = {s0 + L for s0, L in SEQS}

    def load_block(tb):
        t0 = tb * TB
        u, l = uf[tb % 2], lf[tb % 2]
        lo = 0 if t0 in seq_starts else -1
        hi = 0 if (t0 + TB) in seq_ends else 1
        csl = slice(1 + lo, 513 + hi)
        tsl = slice(t0 + lo, t0 + TB + hi)
        if lo == 0:
            op("pool", lambda h: h.memset(u[:, :, 0:1], 0.0), W=[u])
            op("pool", lambda h: h.memset(l[:, :, 0:1], 0.0), W=[l])
        if hi == 0:
            op("pool", lambda h: h.memset(u[:, :, 513:514], 0.0), W=[u])
            op("pool", lambda h: h.memset(l[:, :, 513:514], 0.0), W=[l])
        S.dma_in(u, lambda h: h.dma_start(out=u[:, :, csl], in_=rk[:, :, tsl]))
        for j, ch in enumerate(LORA_CH):
            r0 = sh_rows(ch)
            S.dma_in(l, lambda h, j=j, r0=r0: h.dma_start(out=l[:, j, csl], in_=shg[r0:r0 + 128, tsl]))

    def shift(src, j, c0t, mut, dst_t, dst_ap, func=None):
        a = tA[j % 2]
        op("pool", lambda h: h.tensor_scalar(out=a[:, :], in0=src[:, j, 1:513], scalar1=c0t[:, j:j + 1], scalar2=None, op0=ALU.mult), R=[src, c0t], W=[a])
        stt(a, a[:, :], src, src[:, j, 0:512], mut[:, j, 0:1], a, a[:, :], ALU.mult, ALU.add, extra=(mut,))
        if func is None:
            stt(dst_t, dst_ap, src, src[:, j, 2:514], mut[:, j, 1:2], a, a[:, :], ALU.mult, ALU.add, extra=(mut,))
        else:
            stt(a, a[:, :], src, src[:, j, 2:514], mut[:, j, 1:2], a, a[:, :], ALU.mult, ALU.add, extra=(mut,))
            act(dst_t, dst_ap, a, a[:, :], func)

    blocks = list(range(NB)) if d == 0 else list(range(NB - 1, -1, -1))
    load_block(blocks[0])
    ui = 0
    for bi, tb in enumerate(blocks[:RNBLK]):
        t0 = tb * TB
        if bi + 1 < NB:
            load_block(blocks[bi + 1])
        u, l = uf[tb % 2], lf[tb % 2]
        for j in range(6):
            shift(u, j, c0r, mur, usr, usr[:, j, :])
        for j in range(6):
            shift(l, j, c0s, mus, lor, lor[:, j, :], func=(AF.Tanh if j < 2 else (None if j < 4 else AF.Sigmoid)))
        if (d == 0 and t0 in seq_starts) or (d == 1 and (t0 + TB) in seq_ends):
            for h_ in range(4):
                op("pool", lambda h, h_=h_: h.memset(st[h_][:, :], 0.0), W=[st[h_]])
        tiles = list(range(4)) if d == 0 else [3, 2, 1, 0]
        if RSTAGE < 2:
            tiles = []
        for tj in tiles:
            ksl = slice(tj * 128, (tj + 1) * 128)
            g0_ = t0 + tj * 128
            for j in range(6):
                bt_ = bT if j < 4 else bT2
                mm(bt_, bt_[:, (j % 4) * 128:(j % 4 + 1) * 128], usr, usr[:, j, ksl], identb, identb[:, :])
            evac(rkvT, rkvT[:, 0:512], bT, bT[:, :])
            evac(rkvT, rkvT[:, 512:768], bT2, bT2[:, 0:256])
            r_ap, k_ap, v_ap = rkvT[:, 0:256], rkvT[:, 256:512], rkvT[:, 512:768]
            tt("dve", kkr, kkr[:, :], rkvT, k_ap, vec, vec[:, KK_, :], ALU.mult)
            act(sqk, sqk[:, :], kkr, kkr[:, :], AF.Square)
            op("dve", lambda h: h.tensor_reduce(out=ssum[:, :], in_=sqk[:, :].rearrange("p (a b) -> p a b", b=64), axis=AX.X, op=ALU.add), R=[sqk], W=[ssum])
            act(nrm, nrm[:, :], ssum, ssum[:, :], AF.Sqrt)
            op("dve", lambda h: h.tensor_scalar(out=nrm[:, :], in0=nrm[:, :], scalar1=1e-12, scalar2=None, op0=ALU.max), W=[nrm])
            op("dve", lambda h: h.reciprocal(out=rn[:, :], in_=nrm[:, :]), R=[nrm], W=[rn])
            for h_ in range(4):
                op("dve", lambda h, h_=h_: h.tensor_scalar(out=kk[:, h_ * 64:(h_ + 1) * 64], in0=kkr[:, h_ * 64:(h_ + 1) * 64], scalar1=rn[:, h_:h_ + 1],
                                                         scalar2=None, op0=ALU.mult), R=[kkr, rn], W=[kk])
            if RSTAGE < 3:
                continue
            mm(bZ, bZ[:, 0:256], lor, lor[:, d, ksl], wl, wl[:, 0, d * 256:(d + 1) * 256])
            mm(bZ, bZ[:, 256:512], lor, lor[:, 2 + d, ksl], wl, wl[:, 0, (2 + d) * 256:(3 + d) * 256])
            tt("dve", zz, zz[:, :], bZ, bZ[:, 0:256], vec, vec[:, W0F + d, :], ALU.add)
            act(sg, sg[:, :], zz, zz[:, :], AF.Sigmoid)
            tt("dve", zz, zz[:, :], bZ, bZ[:, 256:512], vec, vec[:, A0F + d, :], ALU.add)
            act(al, al[:, :], zz, zz[:, :], AF.Sigmoid)
            if d == 1:
                mm(bZ, bZ[:, 0:256], lor, lor[:, 2, ksl], wl, wl[:, 0, 512:768])
                tt("dve", zz, zz[:, :], bZ, bZ[:, 0:256], vec, vec[:, A0F, :], ALU.add)
                act(alo, alo[:, :], zz, zz[:, :], AF.Sigmoid)
            mm(bC, bC[:, 0:256], Mf, Mf[:, MI], sg, sg[:, :])
            mm(bC, bC[:, 256:512], Mf, Mf[:, MS], sg, sg[:, :])
            mm(bD, bD[:, 0:256], Mf, Mf[:, MST], sg, sg[:, :])
            act(Ep, Ep[:, :], bC, bC[:, 0:256], AF.Exp, scale=-CDEC)
            for c in range(2):
                pc = 64 * c
                for h_ in range(4):
                    mm(bD, bD[pc:pc + 64, 256 + 64 * h_:256 + 64 * (h_ + 1)], Ep, Ep[:, h_ * 64:(h_ + 1) * 64], identf, identf[:, pc:pc + 64])
            act(Em, Em[:, :], bC, bC[:, 0:256], AF.Exp, scale=CDEC)
            act(Ea, Ea[:, :], bC, bC[:, 256:512], AF.Exp, scale=-CDEC)
            act(Ed, Ed[:, :], bD, bD[:, 0:256], AF.Exp, scale=-CDEC)
            op("act", lambda h: h.activation(out=gC[:, :], in_=bD[:, 256:512].rearrange("p (a b) -> p a b", b=64)[:, :, TL], func=AF.Copy), R=[bD], W=[gC])
            stt(t1, t1[:, :], al, al[:, :], -1.0, vec, vec[:, KA_, :], ALU.add, ALU.mult)
            stt(kd, kd[:, :], t1, t1[:, :], 1.0, rkvT, k_ap, ALU.add, ALU.mult)
            tt("pool", bd, bd[:, :], kk, kk[:, :], al, al[:, :], ALU.mult)
            tt("pool", tm, tm[:, 0, :], rkvT, r_ap, Ep, Ep[:, :], ALU.mult)
            tt("dve", tm, tm[:, 1, :], kd, kd[:, :], Em, Em[:, :], ALU.mult)
            tt("pool", tm, tm[:, 2, :], bd, bd[:, :], Em, Em[:, :], ALU.mult)
            stt(tm, tm[:, 3, :], kk, kk[:, :], -1.0, Ea, Ea[:, :], ALU.mult, ALU.mult)
            tt("dve", tm, tm[:, 4, :], kd, kd[:, :], Ed, Ed[:, :], ALU.mult)
            tt("pool", tm, tm[:, 5, :], bd, bd[:, :], Ed, Ed[:, :], ALU.mult)
            for hp in range(2):
                bt_ = bT if hp == 0 else bT2
                for s_, xi in enumerate((3, 0, 2, 1)):
                    mm(bt_, bt_[:, s_ * 128:(s_ + 1) * 128], tm, tm[:, xi, hp * 128:(hp + 1) * 128], identb, identb[:, :])
            evac(cm, cm[:, 0, :], bT, bT[:, :])
            evac(cm, cm[:, 1, :], bT2, bT2[:, :])
            def head_gen(h_):
                hp, hb = h_ // 2, 64 * (h_ % 2)
                hs = slice(h_ * 64, (h_ + 1) * 64)
                ab, ak, ntt = AB[h_], AK[h_], NTt[h_]
                xs, xts = Xs[h_], XTs[h_]
                bank = hbank[h_]
                sA = sub_cache(bank, 0, 128)
                sB = sub_cache(bank, 128, 256)
                pzh = sub_cache(bank, 256, 384)
                arT = cm[hb:hb + 64, hp, 0:256]
                bTa = cm[hb:hb + 64, hp, 256:384]
                kTa = cm[hb:hb + 64, hp, 384:512]
                aTa = cm[hb:hb + 64, hp, 0:128]
                mm(bE, bE[:, 0:256], cm, bTa, cm, arT)
                tt("dve", ab, ab[:, :], bE, bE[:, 0:256], Mf, Mf[:, MSI], ALU.mult)
                mm(bE, bE[:, 256:512], cm, kTa, cm, arT)
                tt("dve", ak, ak[:, :], bE, bE[:, 256:512], Mf, Mf[:, MSI], ALU.mult)
                yield
                mm(bG, bG[:, 256:384], cm, aTa, cm, bTa)
                tt("dve", ntt, ntt[:, :], bG, bG[:, 256:384], Mf, Mf[:, MST], ALU.mult)
                yield
                X, XT = (ab, ab[:, 0:128]), (ntt, ntt[:, :])
                Xl = [X]
                for i in range(5):
                    mm(sA, sA[:, :], XT[0], XT[1], X[0], X[1])
                    if i < 4:
                        mm(sB, sB[:, :], X[0], X[1], XT[0], XT[1])
                    evac(xs[i], xs[i][:, :], sA, sA[:, :])
                    if i < 4:
                        evac(xts[i], xts[i][:, :], sB, sB[:, :])
                        XT = (xts[i], xts[i][:, :])
                    X = (xs[i], xs[i][:, :])
                    Xl.append(X)
                    yield
                zt = Zt[h_]
                z = zt[0]
                op("pool", lambda h, z=z, hs=hs: h.tensor_copy(out=z[:, 0:64], in_=tm[:, 3, hs]), R=[tm], W=[z])
                vh = rkvT[:, 512 + h_ * 64:512 + (h_ + 1) * 64]
                mm(pzh, pzh[:, 0:64], ak, ak[:, 0:128], rkvT, vh)
                evac(z, z[:, 64:128], pzh, pzh[:, 0:64])
                yield
                for i in range(6):
                    zn = zt[(i + 1) % 2]
                    mm(sA, sA[:, :], identb, identb[:, :], z, z[:, :], start=True, stop=False)
                    mm(sA, sA[:, :], Xl[i][0], Xl[i][1], z, z[:, :], start=False, stop=True)
                    evac(zn, zn[:, :], sA, sA[:, :])
                    z = zn
                    yield
                for c in (((0, 1) if d == 0 else (1, 0)) if RSTAGE >= 5 else ()):
                    pc, po = 64 * c, 64 * (1 - c)
                    cs_ = slice(pc, pc + 64)
                    os_ = slice(po, po + 64)
                    Bh = tm[cs_, 5, hs]
                    Kh = tm[cs_, 4, hs]
                    Rt = tm[cs_, 0, hs]
                    pm, gt_ = Pm[h_][c], Gt[h_][c]
                    mm(pP[c], pP[c][cs_, :], z, z[cs_, 0:64], tm, Bh)
                    stt(pm, pm[cs_, :], identf, identf[cs_, pc:pc + 64], gC[cs_, h_:h_ + 1], pP[c], pP[c][cs_, :], ALU.mult, ALU.add, extra=(gC,))
                    mm(pGm[c], pGm[c][cs_, :], z, z[cs_, 0:64], ab, ab[cs_, 128 + pc:128 + pc + 64], start=True, stop=False)
                    mm(pGm[c], pGm[c][cs_, :], tm, Rt, identb, identb[cs_, pc:pc + 64], start=False, stop=True)
                    evac(gt_, gt_[cs_, :], pGm[c], pGm[c][cs_, :])
                    yield
                    pY = pYs[c]
                    mm(pY, pY[cs_, :], gt_, gt_[cs_, :], st[h_], st[h_][cs_, :], start=True, stop=False)
                    mm(pY, pY[cs_, :], ab, ab[cs_, 128 + pc:128 + pc + 64], z, z[cs_, 64:128], start=False, stop=False)
                    mm(pY, pY[cs_, :], ak, ak[cs_, 128 + pc:128 + pc + 64], rkvT, rkvT[cs_, 512 + h_ * 64:512 + (h_ + 1) * 64], start=False, stop=True)
                    evac(Yt, Yt[cs_, hs], pY, pY[cs_, :])
                    mm(pS[c], pS[c][os_, :], tm, Bh, z, z[cs_, 64:128], start=True, stop=False)
                    mm(pS[c], pS[c][os_, :], tm, Kh, rkvT, rkvT[cs_, 512 + h_ * 64:512 + (h_ + 1) * 64], start=False, stop=False)
                    mm(pS[c], pS[c][os_, :], pm, pm[cs_, :], st[h_], st[h_][cs_, :], start=False, stop=True)
                    evac(st[h_], st[h_][os_, :], pS[c], pS[c][os_, :])
                    yield

            gens = [head_gen(h_) for h_ in range(4 if RSTAGE >= 4 else 0)]
            while gens:
                for g_ in gens[:]:
                    try:
                        next(g_)
                    except StopIteration:
                        gens.remove(g_)
            tok = slice(g0_, g0_ + 128)
            if RSTAGE < 6:
                continue
            if d == 0:
                S.dma_out(Yt, lambda h, tok=tok: h.dma_start(out=dr["yf_s"][tok, :], in_=Yt[:, :]))
            else:
                S.dma_in(yfl, lambda h, tok=tok: h.dma_start(out=yfl[:, :], in_=dr["yf_s"][tok, :]))
                tt("dve", ysum, ysum[:, :], Yt, Yt[:, :], yfl, yfl[:, :], ALU.add)
                for h_ in range(4):
                    hs = slice(h_ * 64, (h_ + 1) * 64)
                    op("dve", lambda h, h_=h_, hs=hs: h.bn_stats(out=stats[:, h_, :], in_=ysum[:, hs]), R=[ysum], W=[stats])
                    op("dve", lambda h, h_=h_: h.bn_aggr(out=mv[:, h_, :], in_=stats[:, h_, :]), R=[stats], W=[mv])
                act(sd, sd[:, :], mv, mv[:, :, 1], AF.Sqrt, bias=GN_EPS)
                op("dve", lambda h: h.reciprocal(out=rsd[:, :], in_=sd[:, :]), R=[sd], W=[rsd])
                for h_ in range(4):
                    hs = slice(h_ * 64, (h_ + 1) * 64)
                    op("dve", lambda h, h_=h_, hs=hs: h.tensor_scalar(out=yn[:, hs], in0=ysum[:, hs], scalar1=mv[:, h_, 0:1], scalar2=rsd[:, h_:h_ + 1],
                                                                    op0=ALU.subtract, op1=ALU.mult), R=[ysum, mv, rsd], W=[yn])
                tt("pool", yn, yn[:, :], yn, yn[:, :], vec, vec[:, LNW, :], ALU.mult)
                tt("pool", yn, yn[:, :], yn, yn[:, :], vec, vec[:, LNB, :], ALU.add)
                tt("dve", t1, t1[:, :], al, al[:, :], alo, alo[:, :], ALU.add)
                stt(t1, t1[:, :], t1, t1[:, :], -2.0, vec, vec[:, KA_, :], ALU.add, ALU.mult)
                stt(kd, kd[:, :], t1, t1[:, :], 2.0, rkvT, k_ap, ALU.add, ALU.mult)
                tt("dve", kd, kd[:, :], kd, kd[:, :], rkvT, r_ap, ALU.mult)
                tt("dve", kd, kd[:, :], kd, kd[:, :], vec, vec[:, RK_, :], ALU.mult)
                op("dve", lambda h: h.tensor_reduce(out=bs[:, :], in_=kd[:, :].rearrange("p (a b) -> p a b", b=64), axis=AX.X, op=ALU.add), R=[kd], W=[bs])
                for h_ in range(4):
                    hs = slice(h_ * 64, (h_ + 1) * 64)
                    stt(yn, yn[:, hs], rkvT, rkvT[:, 512 + h_ * 64:512 + (h_ + 1) * 64], bs[:, h_:h_ + 1], yn, yn[:, hs], ALU.mult, ALU.add, extra=(bs,))
                mm(bZ, bZ[:, 0:256], lor, lor[:, 4, ksl], wg2, wg2[:, 0, :], start=True, stop=False)
                mm(bZ, bZ[:, 0:256], lor, lor[:, 5, ksl], wg2, wg2[:, 1, :], start=False, stop=True)
                tt("dve", yo, yo[:, :], yn, yn[:, :], bZ, bZ[:, 0:256], ALU.mult)
                for cc in range(2):
                    pq = pF[cc]
                    op("pe", lambda h, cc=cc, pq=pq: h.transpose(pq[:, :], yo[:, cc * 128:(cc + 1) * 128], identf[:, :]), R=[yo, identf], W=[pq])
                    evac(ybt, ybt[:, cc, :], pq, pq[:, :])
                S.dma_out(ybt, lambda h, tok=tok: h.dma_start(out=dr["y_b"].ap()[256:512, tok].rearrange("(c p) t -> p c t", p=128), in_=ybt[:, :, :]))
    ph.end()


DBG = []
PHASES = "MRCDEGHI"


def build_nc(with_rwkv=True):
    nc = bass.Bass("TRN2", target_bir_lowering=False)
    dr = {}

    def ein(name, shape, dt=F32):
        dr[name] = nc.dram_tensor(name, shape, dt, kind="ExternalInput")

    def scr(name, shape, dt):
        dr[name] = nc.dram_tensor(name, shape, dt)
    ein("xT", [D, NT]); ein("xTs", [256, NT]); ein("w_inA", [D, 13 * 128]); ein("g0", [128, KC])
    ein("w_uq", [768, 512]); ein("w_ukv", [512, 512]); ein("qng", [128, 6]); ein("kvg", [128, 4])
    ein("ident", [128, 128], F32); ein("cos2", [64, 8192]); ein("sin2", [64, 8192])
    ein("w_br", [4096, 256]); ein("w_o", [D, 256]); ein("w_up", [D, 1024]); ein("w_dn", [8192, 256]); ein("gn", [128, 6])
    for k, shp in RW_INPUTS.items():
        ein(k, shp)
    dr["yT"] = nc.dram_tensor("yT", [256, NT], F32, kind="ExternalOutput")
    scr("sh_b", [384, NT], BF); scr("sh_g", [8 * 384, NT], BF)
    scr("rkv_s", [768, NT], BF); scr("gate_s", [512, NT], BF)
    scr("y_b", [512, NT], BF); scr("y_g", [8 * 512, NT], BF)
    scr("mix_b", [256, NT], BF); scr("mix_g", [D, NT], BF)
    scr("mix_s", [256, NT], F32); scr("x1_s", [256, NT], F32); scr("ff_s", [256, NT], F32)
    scr("h2_b", [256, NT], BF); scr("h2_g", [D, NT], BF)
    scr("hid_b", [1024, NT], BF); scr("hid_g", [8192, NT], BF)
    scr("yf_s", [NT, 256], F32)
    for i in (1, 2, 3):
        scr(f"ss{i}_b", [1, NT], F32); scr(f"ss{i}_g", [8, NT], F32)
    with contextlib.ExitStack() as gs:
        S = Sched(nc, gs)
        phase_A(nc, S, dr)
        if "M" in PHASES:
            phase_M(nc, S, dr)
        if "R" in PHASES:
            phase_R(nc, S, dr)
        ph = Ph(nc, S)
        nop = ph.sb("X_nop", [128, 8])
        S.op("pool", lambda h: h.memset(nop[:, :], 0.0), W=[nop])
        ph.end(allgather(nc, dr["y_b"], dr["y_g"]))
        for nm, fn in (("C", phase_C), ("D", phase_D), ("E", phase_E), ("G", phase_G), ("H", phase_H), ("I", phase_I)):
            if nm in PHASES:
                fn(nc, S, dr)
        if DBG:
            ph = Ph(nc, S)
            dt_ = ph.sb("dbg_t", [128, 8])
            for nm in DBG:
                src = dr[nm]
                dst = nc.dram_tensor("dbg_" + nm, list(src.shape), src.dtype, kind="ExternalOutput")
                for r0 in range(0, src.shape[0], 128):
                    r1 = min(r0 + 128, src.shape[0])
                    for c0 in range(0, src.shape[1], 4096):
                        c1 = min(c0 + 4096, src.shape[1])
                        S.dma_out(dt_, lambda h, src=src, dst=dst, r0=r0, r1=r1, c0=c0, c1=c1: h.dma_start(out=dst[r0:r1, c0:c1], in_=src[r0:r1, c0:c1]))
            ph.end()
    return nc


def _cols(a, n=128):
    return np.ascontiguousarray(a.reshape(-1, 128).T.astype(np.float32))


def kernel(**inp):
    f = lambda k: np.asarray(inp[k], dtype=np.float32)
    x = np.concatenate([f("x_prompt").reshape(-1, D), f("x_sample").reshape(-1, D)], 0)
    xT = np.ascontiguousarray(x.T)
    w_in = f("w_in")[0]
    pad = lambda cols, n=128: list(cols) + [-1] * (n - len(cols))
    sh = [list(range(i * 128, (i + 1) * 128)) for i in range(10)]
    sh.append(pad(range(1280, 1344)))
    sh += [pad(range(U0 + 6144 + 96 * j, U0 + 6144 + 96 * (j + 1))) for j in range(4)]
    sh += [list(range(U0 + 6528, U0 + 6656)), list(range(U0 + 6656, U0 + 6784))]
    sh.append(pad(list(range(1312, 1344)) + list(range(1280, 1312))))
    w_pad = np.concatenate([w_in, np.zeros((D, 1), np.float32)], 1)
    inv = 1.0 / (10000.0 ** (np.arange(0, 64, 2, dtype=np.float32) / 64))
    ang = np.arange(8192, dtype=np.float32)[:, None] * inv[None, :]
    cos, sin = np.cos(ang).T.astype(np.float32), np.sin(ang).T.astype(np.float32)
    cos2 = np.ascontiguousarray(np.concatenate([cos, cos], 0))
    sin2 = np.ascontiguousarray(np.concatenate([-sin, sin], 0))
    import ml_dtypes
    ident = np.eye(128, dtype=np.float32)
    w_uq, w_ukv = f("mla_w_uq")[0], f("mla_w_ukv")[0]
    w_br, w_o, w_up, w_dn = f("w_branch")[0], f("w_out")[0], f("w_mlp_up")[0], f("w_mlp_down")[0]
    in_maps = []
    for c in range(NCORES):
        m = {"xT": xT, "xTs": np.ascontiguousarray(xT[256 * c:256 * (c + 1)])}
        cols = []
        for slot in range(3):
            i = slot * 8 + c
            cols += sh[i] if i < NSH else [-1] * 128
        for base in (0, 2048, 4096):
            cols += list(range(U0 + base + 256 * c, U0 + base + 256 * (c + 1)))
        cols += list(range(G0 + 256 * c, G0 + 256 * (c + 1))) + list(range(G0 + 2048 + 256 * c, G0 + 2048 + 256 * (c + 1)))
        m["w_inA"] = np.ascontiguousarray(w_pad[:, cols])
        m["g0"] = _cols(f("norm_pre_mix")[0])
        qc = []
        for hd in (2 * c, 2 * c + 1):
            b = hd * 192
            qc += list(range(b, b + 192)) + list(range(b + 160, b + 192)) + list(range(b + 128, b + 160))
        m["w_uq"] = np.ascontiguousarray(w_uq[:, qc])
        m["w_ukv"] = np.ascontiguousarray(w_ukv[:, 512 * c:512 * (c + 1)])
        m["qng"] = _cols(f("mla_q_norm")[0]); m["kvg"] = _cols(f("mla_kv_norm")[0])
        m["ident"] = ident; m["cos2"] = cos2; m["sin2"] = sin2
        rows = []
        for r in range(8):
            rows += list(range(256 * r, 256 * (r + 1))) + list(range(2048 + 256 * r, 2048 + 256 * (r + 1)))
        m["w_br"] = np.ascontiguousarray(w_br[rows][:, 256 * c:256 * (c + 1)])
        m["w_o"] = np.ascontiguousarray(w_o[:, 256 * c:256 * (c + 1)])
        m["w_up"] = np.ascontiguousarray(w_up[:, 1024 * c:1024 * (c + 1)])
        m["w_dn"] = np.ascontiguousarray(w_dn[:, 256 * c:256 * (c + 1)])
        sl = slice(256 * c, 256 * (c + 1))
        m["gn"] = np.ascontiguousarray(np.concatenate([_cols(f("norm_post_mix")[0][sl]), _cols(f("norm_pre_mlp")[0][sl]),
                                                       _cols(f("norm_post_mlp")[0][sl])], 1))
        m.update(rw_host(inp, c))
        in_maps.append(m)
    nc = build_nc()
    res = run_bass_kernel_spmd(nc, in_maps, core_ids=list(range(NCORES)))
    global LAST
    LAST = res
    yT = np.concatenate([res.results[c]["yT"] for c in range(NCORES)], 0)
    y = np.ascontiguousarray(yT.T).astype(np.float32)
    return (y[:8192].reshape(1, 8192, D), y[8192:].reshape(4, 2048, D))
```

```python
import numpy as np, contextlib
import concourse.bass as bass
import concourse.mybir as mybir

F32 = mybir.dt.float32
BF = mybir.dt.bfloat16
AF = mybir.ActivationFunctionType
ALU = mybir.AluOpType
AX = mybir.AxisListType


PSUM_EXCL = False
SELF_WAITS = True


class T:
    __slots__ = ("ap", "w", "r", "ds", "name", "root", "psum")

    def __init__(self, ap, name=""):
        self.ap = ap
        self.root = self
        self.psum = False
        self.w = []
        self.r = {}
        self.ds = None
        self.name = name

    def __getitem__(self, k):
        return self.ap[k]


class Sched:
    ENG = ("pe", "act", "dve", "pool", "sp")
    EPOCH = 30000

    def __init__(self, nc, gstack):
        self.nc = nc
        self.gstack = gstack
        self.h = {"pe": nc.tensor, "act": nc.scalar, "dve": nc.vector, "pool": nc.gpsimd, "sp": nc.sync}
        self.rec = {k: [] for k in self.ENG}
        self.base = {k: 0 for k in self.ENG}
        self.sems = {k: [gstack.enter_context(nc.semaphore(f"s_{k}0"))] for k in self.ENG}
        self.waited = {k: {} for k in self.ENG}
        self.clock = {k: {} for k in self.ENG}
        self.iclock = {}
        self.dpool = []
        self.dall = []
        self.nd = 0
        self.cc_sem = gstack.enter_context(nc.semaphore("ccsem"))
        self.ncc = 0
        self.ninstr = 0
        self.tiles = []

    def tile(self, ap, name="", parent=None, psum=False):
        t = T(ap, name)
        t.psum = psum
        if parent is not None:
            t.root = parent.root
            t.psum = parent.root.psum
        self.tiles.append(t)
        return t

    def _dsem(self, t):
        if t.ds is None:
            if self.dpool:
                t.ds = self.dpool.pop()
            else:
                t.ds = [self.gstack.enter_context(self.nc.semaphore(f"d{self.nd}")), 0]
                self.nd += 1
                self.dall.append(t.ds)
        return t.ds

    def release(self, tiles):
        for t in tiles:
            if t.ds is not None:
                self.dpool.append(t.ds)
                t.ds = None

    def _deps(self, e, R, W):
        d = []
        for t in R:
            d.extend(t.w)
        for t in W:
            d.extend(t.w)
            d.extend(t.r.values())
        if e == "pe" or not SELF_WAITS:
            d = [x for x in d if not (x[0] == "e" and x[1] == e)]
        return d

    def _wait(self, e, deps):
        wd = self.waited[e]
        ck = self.clock[e]
        for dep in deps:
            if dep[0] == "e":
                e2, idx = dep[1], dep[2]
                if ck.get(e2, -1) >= idx:
                    continue
                self.rec[e].append(["w", dep])
                for k, v in self.iclock[(e2, idx)].items():
                    if ck.get(k, -1) < v:
                        ck[k] = v
                ck[e2] = idx
            else:
                key = id(dep[1])
                if wd.get(key, -1) < dep[2]:
                    wd[key] = dep[2]
                    self.rec[e].append(["w", dep])

    def op(self, e, f, R=(), W=()):
        if PSUM_EXCL:
            W = [t.root for t in W] + [t.root for t in R if t.root.psum]
            R = [t.root for t in R if not t.root.psum]
        else:
            W = [t.root for t in W]
            R = [t.root for t in R]
        self._wait(e, self._deps(e, R, W))
        idx = len(self.rec[e])
        self.rec[e].append(["i", f, False])
        self.iclock[(e, idx)] = dict(self.clock[e])
        dep = ("e", e, idx)
        for t in R:
            t.r[e] = dep
        for t in W:
            t.w = [dep]
            t.r = {}
        return dep

    def dma_in(self, tile, f, q="sp"):
        ds = self._dsem(tile)
        deps = [x for x in self._deps(q, (), (tile,)) if not (x[0] == "d" and x[1] is ds[0])]
        self._wait(q, deps)
        ds[1] += 16
        sem, v = ds[0], ds[1]
        self.rec[q].append(["d", f, sem])
        tile.w = [("d", sem, v)]
        tile.r = {}

    def dma_out(self, tile, f, q="sp"):
        self._wait(q, self._deps(q, (tile,), ()))
        ds = self._dsem(tile)
        ds[1] += 16
        sem, v = ds[0], ds[1]
        self.rec[q].append(["d", f, sem])
        tile.r["dma"] = ("d", sem, v)

    def flush(self, collective=None):
        nc = self.nc
        last = {}
        for e in self.ENG:
            idxs = [i for i, r in enumerate(self.rec[e]) if r[0] == "i"]
            if not idxs:
                self.rec[e].append(["i", (lambda h: h.nop()) if e != "pe" else (lambda h: h.nop()), False])
                idxs = [len(self.rec[e]) - 1]
            last[e] = idxs[-1]
        for e in self.ENG:
            for e2 in self.ENG:
                if e2 != e:
                    self.rec[e].append(["w", ("e", e2, last[e2])])
            for ds in self.dall:
                if ds[1] > 0:
                    self.rec[e].append(["w", ("d", ds[0], ds[1])])
        for e in self.ENG:
            for r in self.rec[e]:
                if r[0] == "w" and r[1][0] == "e":
                    self.rec[r[1][1]][r[1][2]][2] = True
        cntmap = {}
        plan = {}
        for e in self.ENG:
            c = self.base[e]
            ep = len(self.sems[e]) - 1
            pl = []
            for i, r in enumerate(self.rec[e]):
                if r[0] == "i" and r[2]:
                    if c >= self.EPOCH:
                        self.sems[e].append(self.gstack.enter_context(nc.semaphore(f"s_{e}{len(self.sems[e])}")))
                        ep += 1
                        c = 0
                    c += 1
                    cntmap[(e, i)] = (self.sems[e][ep], c)
            self.base[e] = c
        recs = self.rec
        ccs = self.cc_sem

        def emit(e, h):
            for i, r in enumerate(recs[e]):
                if r[0] == "w":
                    dep = r[1]
                    if dep[0] == "e":
                        sem, v = cntmap[(dep[1], dep[2])]
                        h.wait_ge(sem, v)
                    else:
                        h.wait_ge(dep[1], dep[2])
                elif r[0] == "i":
                    ins = r[1](h)
                    if r[2]:
                        sem, v = cntmap[(e, i)]
                        ins.then_inc(sem, 1)
                    self.ninstr += 1
                else:
                    r[1](h).then_inc(r[2], 16)
                    self.ninstr += 1
            if e == "pool" and collective is not None:
                self.ncc += 1
                collective(h).then_inc(ccs)
                h.wait_ge(ccs, self.ncc)

        with nc.Block() as block:
            @block.tensor
            def _(h):
                emit("pe", h)

            @block.scalar
            def _(h):
                emit("act", h)

            @block.vector
            def _(h):
                emit("dve", h)

            @block.gpsimd
            def _(h):
                emit("pool", h)

            @block.sync
            def _(h):
                emit("sp", h)
        self.rec = {k: [] for k in self.ENG}
        self.waited = {k: {} for k in self.ENG}
        self.clock = {k: {} for k in self.ENG}
        self.iclock = {}
        for t in self.tiles:
            t.w = []
            t.r = {}
        if collective is not None:
            self.op("pool", lambda h: h.nop())
            self.flush()

from concourse.bass_utils import run_bass_kernel_spmd

NCORES = 8
NT = 16384
TB = 512
NB = NT // TB
D = 2048
KC = D // 128
EPS = 1e-6
SEQS = [(0, 8192), (8192, 2048), (10240, 2048), (12288, 2048), (14336, 2048)]
NSH = 18
U0 = 1344
G0 = 1344 + 6784
SCALE = 192 ** -0.5


class Ph:
    def __init__(s, nc, S):
        s.nc, s.S, s.st = nc, S, contextlib.ExitStack()

    _uid = [0]

    def sb(s, name, shape, dt=F32):
        Ph._uid[0] += 1
        name = f"{name}_{Ph._uid[0]}"
        return s.S.tile(s.st.enter_context(s.nc.sbuf_tensor(name, shape, dt)), name)

    def ps(s, name, shape, dt=F32):
        Ph._uid[0] += 1
        name = f"{name}_{Ph._uid[0]}"
        return s.S.tile(s.st.enter_context(s.nc.psum_tensor(name, shape, dt)), name, psum=True)

    def end(s, collective=None):
        s.S.flush(collective)
        s.S.release(s.S.tiles)
        s.S.tiles = []
        s.st.close()


def allgather(nc, src, dst):
    return lambda h: h.collective_compute("AllGather", ALU.bypass, replica_groups=[list(range(NCORES))],
                                          ins=[src.ap().opt()], outs=[dst.ap().opt()])


def load_w(S, ph, name, wdram, nk, ncols, scal=None, rows=128):
    wb = ph.sb(name, [128, nk, ncols], BF)
    st = [ph.sb(f"{name}_st{i}", [128, ncols], F32) for i in range(2)]
    for kc in range(nk):
        s_ = st[kc % 2]
        S.dma_in(s_, lambda h, s_=s_, kc=kc: h.dma_start(out=s_[:, :], in_=wdram[kc * 128:(kc + 1) * 128, :]))
        if scal is None:
            S.op("pool", lambda h, s_=s_, kc=kc: h.tensor_copy(out=wb[:, kc, :], in_=s_[:, :]), R=[s_], W=[wb])
        else:
            S.op("pool", lambda h, s_=s_, kc=kc: h.tensor_scalar(out=wb[:, kc, :], in0=s_[:, :], scalar1=scal[:, kc:kc + 1],
                                                                scalar2=None, op0=ALU.mult), R=[s_, scal], W=[wb])
    return wb


def rstd_from(S, ph_tiles, ps, n, dim, rs, rstd, npart=128):
    S.op("act", lambda h: h.activation(out=rs[0:npart, 0:n], in_=ps[0:npart, 0:n], func=AF.Sqrt, bias=EPS, scale=1.0 / dim), R=[ps], W=[rs])
    S.op("dve", lambda h: h.reciprocal(out=rstd[0:npart, 0:n], in_=rs[0:npart, 0:n]), R=[rs], W=[rstd])


def phase_A(nc, S, dr):
    NCC = 13
    ph = Ph(nc, S)
    g0 = ph.sb("A_g0", [128, KC])
    S.dma_in(g0, lambda h: h.dma_start(out=g0[:, :], in_=dr["g0"][:, :]))
    wb = load_w(S, ph, "A_wb", dr["w_inA"], KC, NCC * 128, scal=g0)
    ones = ph.sb("A_ones", [128, 128], BF)
    xs = [ph.sb(f"A_xs{i}", [128, KC, TB]) for i in range(2)]
    xb = [ph.sb(f"A_xb{i}", [128, KC, TB], BF) for i in range(2)]
    sq = ph.sb("A_sq", [128, KC, TB], BF)
    rs = ph.sb("A_rs", [128, TB])
    rstd = ph.sb("A_rstd", [128, TB])
    ot = [ph.sb(f"A_ot{i}", [128, TB], BF) for i in range(4)]
    gt = [ph.sb(f"A_gt{i}", [128, TB]) for i in range(2)]
    pss = ph.ps("A_pss", [128, TB])
    pso = [ph.ps(f"A_pso{i}", [128, TB]) for i in range(4)]
    S.op("pool", lambda h: h.memset(ones[:, :], 1.0), W=[ones])
    xT = dr["xT"].ap().rearrange("(kc p) t -> p kc t", p=128)

    def load_x(tb):
        t = xs[tb % 2]
        S.dma_in(t, lambda h: h.dma_start(out=t[:, :, :], in_=xT[:, :, tb * TB:(tb + 1) * TB]))
    load_x(0)
    oi = 0
    for tb in range(NB):
        x, b = xs[tb % 2], xb[tb % 2]
        if tb + 1 < NB:
            load_x(tb + 1)
        S.op("pool", lambda h, x=x, b=b: h.tensor_copy(out=b[:, :, :], in_=x[:, :, :]), R=[x], W=[b])
        S.op("act", lambda h, x=x: h.activation(out=sq[:, :, :], in_=x[:, :, :], func=AF.Square), R=[x], W=[sq])
        for kc in range(KC):
            S.op("pe", lambda h, kc=kc: h.matmul(pss[:, :], ones[:, :], sq[:, kc, :], start=(kc == 0), stop=(kc == KC - 1)),
                 R=[ones, sq], W=[pss])
        rstd_from(S, None, pss, TB, D, rs, rstd)
        for cc in range(NCC):
            p = pso[cc % 4]
            for kc in range(KC):
                S.op("pe", lambda h, p=p, kc=kc, cc=cc, b=b: h.matmul(p[:, :], wb[:, kc, cc * 128:(cc + 1) * 128], b[:, kc, :],
                                                                  start=(kc == 0), stop=(kc == KC - 1)), R=[wb, b], W=[p])
            o = ot[oi % 4]
            oi += 1
            tsl = slice(tb * TB, (tb + 1) * TB)
            if cc < 9:
                S.op("dve", lambda h, p=p, o=o: h.tensor_tensor(out=o[:, :], in0=p[:, :], in1=rstd[:, :], op=ALU.mult), R=[p, rstd], W=[o])
                dst = dr["sh_b"][cc * 128:(cc + 1) * 128, tsl] if cc < 3 else dr["rkv_s"][(cc - 3) * 128:(cc - 2) * 128, tsl]
            else:
                g = gt[cc % 2]
                S.op("dve", lambda h, p=p, g=g: h.tensor_tensor(out=g[:, :], in0=p[:, :], in1=rstd[:, :], op=ALU.mult), R=[p, rstd], W=[g])
                S.op("act", lambda h, g=g, o=o: h.activation(out=o[:, :], in_=g[:, :], func=AF.Sigmoid), R=[g], W=[o])
                dst = dr["gate_s"][(cc - 9) * 128:(cc - 8) * 128, tsl]
            S.dma_out(o, lambda h, o=o, dst=dst: h.dma_start(out=dst, in_=o[:, :]))
    ph.end(allgather(nc, dr["sh_b"], dr["sh_g"]))


def sh_rows(i):
    return (i % 8) * 384 + (i // 8) * 128


def phase_M(nc, S, dr):
    ph = Ph(nc, S)
    qg = ph.sb("M_qg", [128, 6])
    kg = ph.sb("M_kg", [128, 4])
    S.dma_in(qg, lambda h: h.dma_start(out=qg[:, :], in_=dr["qng"][:, :]))
    S.dma_in(kg, lambda h: h.dma_start(out=kg[:, :], in_=dr["kvg"][:, :]))
    wq = load_w(S, ph, "M_wq", dr["w_uq"], 6, 512, scal=qg)
    wkv = load_w(S, ph, "M_wkv", dr["w_ukv"], 4, 512, scal=kg)
    ones = ph.sb("M_ones", [128, 128], BF)
    ident = ph.sb("M_id", [128, 128], F32)
    S.op("pool", lambda h: h.memset(ones[:, :], 1.0), W=[ones])
    S.dma_in(ident, lambda h: h.dma_start(out=ident[:, :], in_=dr["ident"][:, :]))
    LMAX = 8192
    qn = ph.sb("M_qn", [128, LMAX], BF)
    qr = ph.sb("M_qr", [64, LMAX], BF)
    kn = ph.sb("M_kn", [128, LMAX], BF)
    kr = ph.sb("M_kr", [64, LMAX], BF)
    va = ph.sb("M_va", [128, LMAX // 128, 132], BF)
    S.op("pool", lambda h: h.memset(va[:, :, 128:129], 1.0), W=[va])
    cq = ph.sb("M_cq", [128, 6, TB], BF)
    ckv = ph.sb("M_ckv", [128, 4, TB], BF)
    ckvs = ph.sb("M_ckvs", [128, 4, TB], BF)
    kro = ph.sb("M_kro", [64, 2, TB], BF)
    sqq = ph.sb("M_sqq", [128, 6, TB], BF)
    sqk = ph.sb("M_sqk", [128, 4, TB], BF)
    cs = ph.sb("M_cs", [64, 2, TB])
    rs = ph.sb("M_rs", [128, TB])
    rq = ph.sb("M_rq", [128, TB])
    rk = ph.sb("M_rk", [128, TB])
    rsc = ph.sb("M_rsc", [128, 4])
    rkc = ph.sb("M_rkc", [128, 4])
    t1 = ph.sb("M_t1", [64, TB])
    t2 = ph.sb("M_t2", [64, TB])
    pt = [ph.sb(f"M_pt{i}", [128, TB], BF) for i in range(2)]
    on = ph.sb("M_on", [128, 128], F32)
    rinv = ph.sb("M_rinv", [128, 1])
    yt = [ph.sb(f"M_yt{i}", [128, TB], BF) for i in range(2)]
    psA = [ph.ps(f"M_psA{i}", [128, TB]) for i in range(2)]
    psO = [ph.ps(f"M_psO{i}", [128, 512]) for i in range(4)]
    psB = ph.ps("M_psB", [128, TB])
    psT = ph.ps("M_psT", [128, 512], F32)
    shg = dr["sh_g"]
    yi = 0
    for hl in range(2):
        for (s0, L) in SEQS:
            for tb in range(L // TB):
                tsl = slice(s0 + tb * TB, s0 + (tb + 1) * TB)
                lsl = slice(tb * TB, (tb + 1) * TB)
                for i in range(6):
                    S.dma_in(cq, lambda h, i=i, tsl=tsl: h.dma_start(out=cq[:, i, :], in_=shg[sh_rows(i):sh_rows(i) + 128, tsl]))
                for i in range(4):
                    S.dma_in(ckv, lambda h, i=i, tsl=tsl: h.dma_start(out=ckv[:, i, :], in_=shg[sh_rows(6 + i):sh_rows(6 + i) + 128, tsl]))
                S.dma_in(kro, lambda h, tsl=tsl: h.dma_start(out=kro[:, 0, :], in_=shg[sh_rows(10):sh_rows(10) + 64, tsl]))
                S.dma_in(kro, lambda h, tsl=tsl: h.dma_start(out=kro[:, 1, :], in_=shg[sh_rows(17):sh_rows(17) + 64, tsl]))
                S.dma_in(cs, lambda h, lsl=lsl: h.dma_start(out=cs[:, 0, :], in_=dr["cos2"][:, lsl]))
                S.dma_in(cs, lambda h, lsl=lsl: h.dma_start(out=cs[:, 1, :], in_=dr["sin2"][:, lsl]))
                S.op("act", lambda h: h.activation(out=sqq[:, :, :], in_=cq[:, :, :], func=AF.Square), R=[cq], W=[sqq])
                S.op("act", lambda h: h.activation(out=sqk[:, :, :], in_=ckv[:, :, :], func=AF.Square), R=[ckv], W=[sqk])
                p = psA[0]
                for i in range(6):
                    S.op("pe", lambda h, i=i, p=p: h.matmul(p[:, :], ones[:, :], sqq[:, i, :], start=(i == 0), stop=(i == 5)), R=[ones, sqq], W=[p])
                rstd_from(S, None, p, TB, 768, rs, rq)
                p = psA[1]
                for i in range(4):
                    S.op("pe", lambda h, i=i, p=p: h.matmul(p[:, :], ones[:, :], sqk[:, i, :], start=(i == 0), stop=(i == 3)), R=[ones, sqk], W=[p])
                rstd_from(S, None, p, TB, 512, rs, rk)
                for i in range(4):
                    S.op("pool" if i % 2 else "dve", lambda h, i=i: h.tensor_tensor(out=ckvs[:, i, :], in0=ckv[:, i, :], in1=rk[:, :], op=ALU.mult),
                         R=[ckv, rk], W=[ckvs])
                c0 = hl * 256
                p = psA[0]
                for i in range(6):
                    S.op("pe", lambda h, i=i, p=p, c0=c0: h.matmul(p[:, :], wq[:, i, c0:c0 + 128], cq[:, i, :], start=(i == 0), stop=(i == 5)), R=[wq, cq], W=[p])
                S.op("dve", lambda h, p=p, lsl=lsl: h.tensor_tensor(out=qn[:, lsl], in0=p[:, :], in1=rq[:, :], op=ALU.mult), R=[p, rq], W=[qn])
                p = psA[1]
                for i in range(6 if 'q' not in PHASES else 0):
                    S.op("pe", lambda h, i=i, p=p, c0=c0: h.matmul(p[0:64, :], wq[:, i, c0 + 128:c0 + 192], cq[:, i, :], start=(i == 0), stop=(i == 5)), R=[wq, cq], W=[p])
                S.op("dve", lambda h, p=p: h.tensor_tensor(out=t1[:, :], in0=p[0:64, :], in1=cs[:, 0, :], op=ALU.mult), R=[p, cs], W=[t1])
                p = psA[0]
                for i in range(6):
                    S.op("pe", lambda h, i=i, p=p, c0=c0: h.matmul(p[0:64, :], wq[:, i, c0 + 192:c0 + 256], cq[:, i, :], start=(i == 0), stop=(i == 5)), R=[wq, cq], W=[p])
                S.op("dve", lambda h, p=p: h.tensor_tensor(out=t2[:, :], in0=p[0:64, :], in1=cs[:, 1, :], op=ALU.mult), R=[p, cs], W=[t2])
                S.op("pool", lambda h: h.tensor_tensor(out=t1[:, :], in0=t1[:, :], in1=t2[:, :], op=ALU.add), R=[t2], W=[t1])
                S.op("dve", lambda h, lsl=lsl: h.tensor_tensor(out=qr[:, lsl], in0=t1[:, :], in1=rq[0:64, :], op=ALU.mult), R=[t1, rq], W=[qr])
                k0 = hl * 256
                p = psA[1]
                for i in range(4):
                    S.op("pe", lambda h, i=i, p=p, k0=k0: h.matmul(p[:, :], wkv[:, i, k0:k0 + 128], ckvs[:, i, :], start=(i == 0), stop=(i == 3)), R=[wkv, ckvs], W=[p])
                S.op("dve", lambda h, p=p, lsl=lsl: h.tensor_copy(out=kn[:, lsl], in_=p[:, :]), R=[p], W=[kn])
                S.op("dve", lambda h: h.tensor_tensor(out=t1[:, :], in0=kro[:, 0, :], in1=cs[:, 0, :], op=ALU.mult), R=[kro, cs], W=[t1])
                S.op("pool", lambda h: h.tensor_tensor(out=t2[:, :], in0=kro[:, 1, :], in1=cs[:, 1, :], op=ALU.mult), R=[kro, cs], W=[t2])
                S.op("dve", lambda h, lsl=lsl: h.tensor_tensor(out=kr[:, lsl], in0=t1[:, :], in1=t2[:, :], op=ALU.add), R=[t1, t2], W=[kr])
                for j in range(4 if 'v' not in PHASES else 0):
                    p = psO[j]
                    for i in range(4):
                        S.op("pe", lambda h, i=i, j=j, p=p, k0=k0: h.matmul(p[:, 0:128], ckvs[:, i, j * 128:(j + 1) * 128], wkv[:, i, k0 + 128:k0 + 256],
                                                                  start=(i == 0), stop=(i == 3)), R=[wkv, ckvs], W=[p])
                    S.op("dve", lambda h, j=j, p=p, tb=tb: h.tensor_copy(out=va[:, tb * 4 + j, 0:128], in_=p[:, 0:128]), R=[p], W=[va])
            nkb = L // 128
            for qb in range(L // TB):
                qsl = slice(qb * TB, (qb + 1) * TB)
                for kb in range(nkb):
                    ksl = slice(kb * 128, (kb + 1) * 128)
                    p = psA[kb % 2]
                    S.op("pe", lambda h, p=p, ksl=ksl, qsl=qsl: h.matmul(p[:, :], kn[:, ksl], qn[:, qsl], start=True, stop=False), R=[kn, qn], W=[p])
                    S.op("pe", lambda h, p=p, ksl=ksl, qsl=qsl: h.matmul(p[:, :], kr[:, ksl], qr[:, qsl], start=False, stop=True), R=[kr, qr], W=[p])
                    e = pt[kb % 2]
                    S.op("act", lambda h, p=p, e=e: h.activation(out=e[:, :], in_=p[:, :], func=AF.Exp, scale=SCALE), R=[p], W=[e])
                    for j in range(4):
                        S.op("pe", lambda h, j=j, e=e, kb=kb, nkb=nkb: h.matmul(psO[j][:, 0:129], e[:, j * 128:(j + 1) * 128], va[:, kb, 0:129],
                                                                    start=(kb == 0), stop=(kb == nkb - 1)), R=[e, va], W=[psO[j]])
                y = yt[yi % 2]
                yi += 1
                for j in range(4):
                    S.op("dve", lambda h, j=j: h.reciprocal(out=rinv[:, :], in_=psO[j][:, 128:129]), R=[psO[j]], W=[rinv])
                    S.op("dve", lambda h, j=j: h.tensor_scalar(out=on[:, :], in0=psO[j][:, 0:128], scalar1=rinv[:, 0:1], scalar2=None, op0=ALU.mult),
                         R=[psO[j], rinv], W=[on])
                    S.op("pe", lambda h, j=j: h.transpose(psT[:, j * 128:(j + 1) * 128], on[:, :], ident[:, :]), R=[on, ident], W=[psT])
                S.op("act", lambda h, y=y: h.activation(out=y[:, :], in_=psT[:, 0:512], func=AF.Copy), R=[psT], W=[y])
                dst = dr["y_b"][hl * 128:(hl + 1) * 128, s0 + qb * TB:s0 + (qb + 1) * TB]
                S.dma_out(y, lambda h, y=y, dst=dst: h.dma_start(out=dst, in_=y[:, :]))
    ph.end()


def lin_phase(nc, S, ph, name, src_ap, nk, wb, ncc, epi, tbs=TB, pre=None, ks=None):
    xb = [ph.sb(f"{name}_xb{i}", [128, nk, tbs], BF) for i in range(2)]
    pso = [ph.ps(f"{name}_ps{i}", [128, tbs]) for i in range(3)]
    nb = NT // tbs

    def load(tb):
        t = xb[tb % 2]
        tsl = slice(tb * tbs, (tb + 1) * tbs)
        for f in src_ap(t, tsl):
            S.dma_in(t, f)
    load(0)
    pi = 0
    for tb in range(nb):
        tsl = slice(tb * tbs, (tb + 1) * tbs)
        if tb + 1 < nb:
            load(tb + 1)
        b = xb[tb % 2]
        if pre is not None:
            pre(tb, tsl)
        for cc in range(ncc):
            p = pso[pi % 3]
            pi += 1
            kl = list(range(nk)) if ks is None else ks(cc)
            wc = cc if ks is None else cc % 2
            for n, kc in enumerate(kl):
                S.op("pe", lambda h, p=p, kc=kc, wc=wc, b=b, n=n, kl=kl: h.matmul(p[:, :], wb[:, kc, wc * 128:(wc + 1) * 128], b[:, kc, :],
                                                                              start=(n == 0), stop=(n == len(kl) - 1)), R=[wb, b], W=[p])
            epi(tb, cc, p, tsl)


def ss_partial(S, ph, name):
    ones = ph.sb(f"{name}_ones", [128, 128], BF)
    S.op("pool", lambda h: h.memset(ones[:, :], 1.0), W=[ones])
    sq = [ph.sb(f"{name}_sq{i}", [128, TB], BF) for i in range(2)]
    row = ph.sb(f"{name}_row", [1, TB])
    pss = ph.ps(f"{name}_pss", [128, TB])

    def f(cc, ncc, src, n, tsl, dst):
        s_ = sq[cc % 2]
        S.op("act", lambda h: h.activation(out=s_[:, 0:n], in_=src[:, 0:n], func=AF.Square), R=[src], W=[s_])
        S.op("pe", lambda h: h.matmul(pss[:, 0:n], ones[:, :], s_[:, 0:n], start=(cc == 0), stop=(cc == ncc - 1)), R=[ones, s_], W=[pss])
        if cc == ncc - 1:
            S.op("act", lambda h: h.activation(out=row[0:1, 0:n], in_=pss[0:1, 0:n], func=AF.Copy), R=[pss], W=[row])
            S.dma_out(row, lambda h: h.dma_start(out=dst[0:1, tsl], in_=row[0:1, 0:n]))
    return f


def ss_total(S, ph, name, dim):
    onesf = ph.sb(f"{name}_onesf", [8, 128])
    S.op("pool", lambda h: h.memset(onesf[:, :], 1.0), W=[onesf])
    ssl = ph.sb(f"{name}_ssl", [8, TB])
    rs = ph.sb(f"{name}_rs", [128, TB])
    rstd = ph.sb(f"{name}_rstd", [128, TB])
    pst = ph.ps(f"{name}_pst", [128, TB])

    def f(ssg, tsl, n):
        S.dma_in(ssl, lambda h: h.dma_start(out=ssl[:, 0:n], in_=ssg[:, tsl]))
        S.op("pe", lambda h: h.matmul(pst[:, 0:n], onesf[:, :], ssl[:, 0:n], start=True, stop=True), R=[onesf, ssl], W=[pst])
        rstd_from(S, None, pst, n, dim, rs, rstd)
        return rstd
    return f


def one_dma(ap_fn):
    return lambda t, tsl: [lambda h: h.dma_start(out=t[:, :, :], in_=ap_fn(tsl))]


def phase_C(nc, S, dr):
    ph = Ph(nc, S)
    wb = load_w(S, ph, "C_wb", dr["w_br"], 32, 256)
    gt = [ph.sb(f"C_gt{i}", [128, 4, TB], BF) for i in range(2)]
    tmp = [ph.sb(f"C_tmp{i}", [128, TB]) for i in range(2)]
    t2 = ph.sb("C_t2", [128, TB])
    ot = [ph.sb(f"C_ot{i}", [128, TB], BF) for i in range(2)]
    yg = dr["y_g"].ap().rearrange("(k p) t -> p k t", p=128)
    gs = dr["gate_s"].ap().rearrange("(k p) t -> p k t", p=128)
    st = {}

    def pre(tb, tsl):
        g = gt[tb % 2]
        S.dma_in(g, lambda h: h.dma_start(out=g[:, :, :], in_=gs[:, :, tsl]))
        st["g"] = g

    def ks(cc):
        br = cc // 2
        return [r * 4 + br * 2 + j for r in range(8) for j in range(2)]

    def epi(tb, cc, p, tsl):
        g = st["g"]
        j = cc % 2
        if cc < 2:
            S.op("dve", lambda h: h.tensor_tensor(out=tmp[j][:, :], in0=p[:, :], in1=g[:, j, :], op=ALU.mult), R=[p, g], W=[tmp[j]])
        else:
            o = ot[j]
            S.op("dve", lambda h: h.tensor_tensor(out=t2[:, :], in0=p[:, :], in1=g[:, 2 + j, :], op=ALU.mult), R=[p, g], W=[t2])
            S.op("pool", lambda h: h.tensor_tensor(out=o[:, :], in0=t2[:, :], in1=tmp[j][:, :], op=ALU.add), R=[t2, tmp[j]], W=[o])
            dst = dr["mix_b"][j * 128:(j + 1) * 128, tsl]
            S.dma_out(o, lambda h: h.dma_start(out=dst, in_=o[:, :]))
    lin_phase(nc, S, ph, "C", one_dma(lambda tsl: yg[:, :, tsl]), 32, wb, 4, epi, pre=pre, ks=ks)
    ph.end(allgather(nc, dr["mix_b"], dr["mix_g"]))


def phase_D(nc, S, dr):
    ph = Ph(nc, S)
    wb = load_w(S, ph, "D_wb", dr["w_o"], 16, 256)
    ssp = ss_partial(S, ph, "D")
    mt = [ph.sb(f"D_mt{i}", [128, TB]) for i in range(2)]
    mg = dr["mix_g"].ap().rearrange("(k p) t -> p k t", p=128)

    def epi(tb, cc, p, tsl):
        m = mt[cc % 2]
        S.op("dve", lambda h: h.tensor_copy(out=m[:, :], in_=p[:, :]), R=[p], W=[m])
        dst = dr["mix_s"][cc * 128:(cc + 1) * 128, tsl]
        S.dma_out(m, lambda h: h.dma_start(out=dst, in_=m[:, :]))
        ssp(cc, 2, m, TB, tsl, dr["ss1_b"])
    lin_phase(nc, S, ph, "D", one_dma(lambda tsl: mg[:, :, tsl]), 16, wb, 2, epi)
    ph.end(allgather(nc, dr["ss1_b"], dr["ss1_g"]))


def phase_E(nc, S, dr):
    ph = Ph(nc, S)
    gn = ph.sb("E_gn", [128, 6])
    S.dma_in(gn, lambda h: h.dma_start(out=gn[:, :], in_=dr["gn"][:, :]))
    tot = ss_total(S, ph, "E", D)
    ssp = ss_partial(S, ph, "E2")
    mt = [ph.sb(f"E_mt{i}", [128, 2, TB]) for i in range(2)]
    xt = [ph.sb(f"E_xt{i}", [128, 2, TB]) for i in range(2)]
    x1 = [ph.sb(f"E_x1{i}", [128, TB]) for i in range(2)]
    tt = ph.sb("E_tt", [128, TB])
    hb = [ph.sb(f"E_hb{i}", [128, TB], BF) for i in range(2)]
    ms = dr["mix_s"].ap().rearrange("(k p) t -> p k t", p=128)
    xs = dr["xTs"].ap().rearrange("(k p) t -> p k t", p=128)
    for tb in range(NB):
        tsl = slice(tb * TB, (tb + 1) * TB)
        m, x = mt[tb % 2], xt[tb % 2]
        S.dma_in(m, lambda h, m=m, tsl=tsl: h.dma_start(out=m[:, :, :], in_=ms[:, :, tsl]))
        S.dma_in(x, lambda h, x=x, tsl=tsl: h.dma_start(out=x[:, :, :], in_=xs[:, :, tsl]))
        rstd = tot(dr["ss1_g"], tsl, TB)
        for cc in range(2):
            o = x1[cc]
            S.op("dve", lambda h, m=m, cc=cc: h.tensor_tensor(out=tt[:, :], in0=m[:, cc, :], in1=rstd[:, :], op=ALU.mult), R=[m, rstd], W=[tt])
            S.op("dve", lambda h, o=o, x=x, cc=cc: h.scalar_tensor_tensor(out=o[:, :], in0=tt[:, :], scalar=gn[:, cc:cc + 1], in1=x[:, cc, :],
                                                                       op0=ALU.mult, op1=ALU.add), R=[tt, gn, x], W=[o])
            dst = dr["x1_s"][cc * 128:(cc + 1) * 128, tsl]
            S.dma_out(o, lambda h, o=o, dst=dst: h.dma_start(out=dst, in_=o[:, :]))
            hh = hb[cc]
            S.op("pool", lambda h, hh=hh, o=o, cc=cc: h.tensor_scalar(out=hh[:, :], in0=o[:, :], scalar1=gn[:, 2 + cc:3 + cc], scalar2=None, op0=ALU.mult),
                 R=[o, gn], W=[hh])
            dst2 = dr["h2_b"][cc * 128:(cc + 1) * 128, tsl]
            S.dma_out(hh, lambda h, hh=hh, dst2=dst2: h.dma_start(out=dst2, in_=hh[:, :]))
            ssp(cc, 2, o, TB, tsl, dr["ss2_b"])
    ph.end(allgather(nc, dr["h2_b"], dr["h2_g"]))
    ph = Ph(nc, S)
    nop = ph.sb("E_nop", [128, 8])
    S.op("pool", lambda h: h.memset(nop[:, :], 0.0), W=[nop])
    ph.end(allgather(nc, dr["ss2_b"], dr["ss2_g"]))


def phase_G(nc, S, dr):
    ph = Ph(nc, S)
    wb = load_w(S, ph, "G_wb", dr["w_up"], 16, 1024)
    tot = ss_total(S, ph, "G", D)
    tt = [ph.sb(f"G_tt{i}", [128, TB]) for i in range(2)]
    ot = [ph.sb(f"G_ot{i}", [128, TB], BF) for i in range(3)]
    hg = dr["h2_g"].ap().rearrange("(k p) t -> p k t", p=128)
    st = {"n": 0}

    def pre(tb, tsl):
        st["r"] = tot(dr["ss2_g"], tsl, TB)

    def epi(tb, cc, p, tsl):
        rstd = st["r"]
        t = tt[cc % 2]
        o = ot[st["n"] % 3]
        st["n"] += 1
        S.op("dve", lambda h: h.scalar_tensor_tensor(out=t[:, :], in0=p[:, :], scalar=0.0, in1=rstd[:, :], op0=ALU.max, op1=ALU.mult),
             R=[p, rstd], W=[t])
        S.op("act", lambda h: h.activation(out=o[:, :], in_=t[:, :], func=AF.Square), R=[t], W=[o])
        dst = dr["hid_b"][cc * 128:(cc + 1) * 128, tsl]
        S.dma_out(o, lambda h: h.dma_start(out=dst, in_=o[:, :]))
    lin_phase(nc, S, ph, "G", one_dma(lambda tsl: hg[:, :, tsl]), 16, wb, 8, epi, pre=pre)
    ph.end(allgather(nc, dr["hid_b"], dr["hid_g"]))


def phase_H(nc, S, dr):
    ph = Ph(nc, S)
    wb = load_w(S, ph, "H_wb", dr["w_dn"], 64, 256)
    ssp = ss_partial(S, ph, "H")
    mt = [ph.sb(f"H_mt{i}", [128, TB]) for i in range(2)]
    hg = dr["hid_g"].ap().rearrange("(k p) t -> p k t", p=128)

    def epi(tb, cc, p, tsl):
        m = mt[cc % 2]
        S.op("dve", lambda h: h.tensor_copy(out=m[:, :], in_=p[:, :]), R=[p], W=[m])
        dst = dr["ff_s"][cc * 128:(cc + 1) * 128, tsl]
        S.dma_out(m, lambda h: h.dma_start(out=dst, in_=m[:, :]))
        ssp(cc, 2, m, TB, tsl, dr["ss3_b"])
    lin_phase(nc, S, ph, "H", one_dma(lambda tsl: hg[:, :, tsl]), 64, wb, 2, epi)
    ph.end(allgather(nc, dr["ss3_b"], dr["ss3_g"]))


def phase_I(nc, S, dr):
    ph = Ph(nc, S)
    gn = ph.sb("I_gn", [128, 6])
    S.dma_in(gn, lambda h: h.dma_start(out=gn[:, :], in_=dr["gn"][:, :]))
    tot = ss_total(S, ph, "I", D)
    ft = [ph.sb(f"I_ft{i}", [128, 2, TB]) for i in range(2)]
    xt = [ph.sb(f"I_xt{i}", [128, 2, TB]) for i in range(2)]
    yo = [ph.sb(f"I_yo{i}", [128, TB]) for i in range(2)]
    tt = ph.sb("I_tt", [128, TB])
    fs = dr["ff_s"].ap().rearrange("(k p) t -> p k t", p=128)
    xs = dr["x1_s"].ap().rearrange("(k p) t -> p k t", p=128)
    for tb in range(NB):
        tsl = slice(tb * TB, (tb + 1) * TB)
        f, x = ft[tb % 2], xt[tb % 2]
        S.dma_in(f, lambda h, f=f, tsl=tsl: h.dma_start(out=f[:, :, :], in_=fs[:, :, tsl]))
        S.dma_in(x, lambda h, x=x, tsl=tsl: h.dma_start(out=x[:, :, :], in_=xs[:, :, tsl]))
        rstd = tot(dr["ss3_g"], tsl, TB)
        for cc in range(2):
            o = yo[cc]
            S.op("dve", lambda h, f=f, cc=cc: h.tensor_tensor(out=tt[:, :], in0=f[:, cc, :], in1=rstd[:, :], op=ALU.mult), R=[f, rstd], W=[tt])
            S.op("dve", lambda h, o=o, x=x, cc=cc: h.scalar_tensor_tensor(out=o[:, :], in0=tt[:, :], scalar=gn[:, 4 + cc:5 + cc], in1=x[:, cc, :],
                                                                       op0=ALU.mult, op1=ALU.add), R=[tt, gn, x], W=[o])
            dst = dr["yT"][cc * 128:(cc + 1) * 128, tsl]
            S.dma_out(o, lambda h, o=o, dst=dst: h.dma_start(out=dst, in_=o[:, :]))
    ph.end()


RW_INPUTS = {"rw_vec": [128, 9 * 256], "w_lora": [128, 4 * 256], "w_g2": [256, 256], "mu_rkv": [128, 12], "mu_sh": [128, 12],
             "rw_M": [128, 6 * 128], "identf": [128, 128]}
CDEC = 0.6065306597126334
RSTAGE = 9
RNBLK = 999
GN_EPS = 64e-5
LORA_CH = (11, 12, 13, 14, 15, 16)


def rw_host(inp, c):
    f = lambda k: np.asarray(inp[k], dtype=np.float32)[0]
    sl = slice(256 * c, 256 * (c + 1))
    vec = np.stack([f("rwkv_w0_f")[sl], f("rwkv_w0_b")[sl], f("rwkv_a0_f")[sl], f("rwkv_a0_b")[sl], f("rwkv_k_k")[sl], f("rwkv_k_a")[sl],
                    f("rwkv_r_k").reshape(-1)[sl], f("rwkv_ln_w")[sl], f("rwkv_ln_b")[sl]], 0).reshape(1, -1)
    m = {"rw_vec": np.ascontiguousarray(np.repeat(vec, 128, 0))}
    wl = np.zeros((128, 4, 256), np.float32)
    for j, k in enumerate(("rwkv_w2_f", "rwkv_w2_b", "rwkv_a2_f", "rwkv_a2_b")):
        wl[:96, j] = f(k)[:, sl]
    m["w_lora"] = wl.reshape(128, -1)
    m["w_g2"] = np.ascontiguousarray(f("rwkv_g2")[:, sl])
    mp, mn = f("mu_prev"), f("mu_next")
    mu = np.zeros((128, 6, 2), np.float32)
    for j, base in enumerate((0, 0, 2048, 2048, 4096, 4096)):
        idx = base + 256 * c + 128 * (j % 2) + np.arange(128)
        mu[:, j, 0], mu[:, j, 1] = mp[idx], mn[idx]
    m["mu_rkv"] = mu.reshape(128, -1)
    mu = np.zeros((128, 6, 2), np.float32)
    for j in range(4):
        idx = 6144 + 96 * j + np.arange(96)
        mu[:96, j, 0], mu[:96, j, 1] = mp[idx], mn[idx]
    for j in range(2):
        idx = 6528 + 128 * j + np.arange(128)
        mu[:, 4 + j, 0], mu[:, 4 + j, 1] = mp[idx], mn[idx]
    m["mu_sh"] = mu.reshape(128, -1)
    i = np.arange(128)
    same = (i[:, None] // 64) == (i[None, :] // 64)
    M = np.zeros((128, 6, 128), np.float32)
    lt = same & (i[:, None] < i[None, :])
    gt = same & (i[:, None] > i[None, :])
    eq = np.eye(128, dtype=bool)
    M[:, 0], M[:, 1], M[:, 2] = lt, lt | eq, lt.T
    M[:, 3], M[:, 4], M[:, 5] = gt, gt | eq, gt.T
    m["rw_M"] = M.reshape(128, -1)
    m["identf"] = np.eye(128, dtype=np.float32)
    return m


def phase_R(nc, S, dr):
    for d in (0, 1):
        rwkv_pass(nc, S, dr, d)


def rwkv_pass(nc, S, dr, d):
    ph = Ph(nc, S)
    sb, ps = ph.sb, ph.ps
    op = S.op
    vec = sb("R_vec", [128, 9, 256])
    S.dma_in(vec, lambda h: h.dma_start(out=vec[:, :, :], in_=dr["rw_vec"].ap().rearrange("p (a b) -> p a b", b=256)))
    W0F, W0B, A0F, A0B, KK_, KA_, RK_, LNW, LNB = range(9)
    wl = load_w(S, ph, "R_wl", dr["w_lora"], 1, 1024)
    wg2 = load_w(S, ph, "R_wg2", dr["w_g2"], 2, 256)
    mur = sb("R_mur", [128, 6, 2]); mus = sb("R_mus", [128, 6, 2])
    S.dma_in(mur, lambda h: h.dma_start(out=mur[:, :, :], in_=dr["mu_rkv"].ap().rearrange("p (a b) -> p a b", b=2)))
    S.dma_in(mus, lambda h: h.dma_start(out=mus[:, :, :], in_=dr["mu_sh"].ap().rearrange("p (a b) -> p a b", b=2)))
    c0r = sb("R_c0r", [128, 6]); c0s = sb("R_c0s", [128, 6])
    for (c0t, mut) in ((c0r, mur), (c0s, mus)):
        op("dve", lambda h, c0t=c0t, mut=mut: h.tensor_tensor(out=c0t[:, :], in0=mut[:, :, 0], in1=mut[:, :, 1], op=ALU.add), R=[mut], W=[c0t])
        op("dve", lambda h, c0t=c0t: h.tensor_scalar(out=c0t[:, :], in0=c0t[:, :], scalar1=-1.0, scalar2=1.0, op0=ALU.mult, op1=ALU.add), W=[c0t])
    Mf = sb("R_M", [128, 768])
    S.dma_in(Mf, lambda h: h.dma_start(out=Mf[:, :], in_=dr["rw_M"][:, :]))
    MS, MI, MST = slice(384 * d, 384 * d + 128), slice(384 * d + 128, 384 * d + 256), slice(384 * d + 256, 384 * d + 384)
    MSI = slice(384 * d, 384 * d + 256)
    TL = 63 if d == 0 else 0
    identf = sb("R_idf", [128, 128])
    S.dma_in(identf, lambda h: h.dma_start(out=identf[:, :], in_=dr["identf"][:, :]))
    identb = sb("R_idb", [128, 128], BF)
    op("pool", lambda h: h.tensor_copy(out=identb[:, :], in_=identf[:, :]), R=[identf], W=[identb])
    onesf = sb("R_onesf", [128, 2])
    op("pool", lambda h: h.memset(onesf[:, :], 1.0), W=[onesf])
    uf = [sb(f"R_uf{i}", [128, 6, 514], BF) for i in range(2)]
    lf = [sb(f"R_lf{i}", [128, 6, 514], BF) for i in range(2)]
    usr = sb("R_usr", [128, 6, 512], BF)
    lor = sb("R_lor", [128, 6, 512], BF)
    tA = [sb(f"R_tA{i}", [128, 512]) for i in range(2)]
    rkvT = sb("R_rkvT", [128, 768], BF)
    F = lambda n: sb(n, [128, 256])
    kkr, sqk, kk = F("R_kkr"), F("R_sqk"), F("R_kk")
    ssum = sb("R_ssum", [128, 4]); nrm = sb("R_nrm", [128, 4]); rn = sb("R_rn", [128, 4])
    zz, sg, al, alo = F("R_zz"), F("R_sg"), F("R_al"), F("R_alo")
    Ep, Em, Ea, Ed = F("R_Ep"), F("R_Em"), F("R_Ea"), F("R_Ed")
    t1, kd, bd = F("R_t1"), F("R_kd"), F("R_bd")
    gC = sb("R_gC", [128, 4])
    tm = sb("R_tm", [128, 6, 256], BF)
    cm = sb("R_cm", [128, 2, 512], BF)
    AB = [sb(f"R_AB{i}", [128, 256], BF) for i in range(4)]
    AK = [sb(f"R_AK{i}", [128, 256], BF) for i in range(4)]
    NTt = [sb(f"R_NT{i}", [128, 128], BF) for i in range(4)]
    XX = [[sb(f"R_XX{b}{i}", [128, 256], BF) for i in range(5)] for b in range(4)]
    Zt = [[sb(f"R_Z{b}{i}", [128, 128], BF) for i in range(2)] for b in range(4)]
    Pm = [[sb(f"R_Pm{b}{i}", [128, 64], BF) for i in range(2)] for b in range(4)]
    Gt = [[sb(f"R_G{b}{i}", [128, 64], BF) for i in range(2)] for b in range(4)]
    st = [sb(f"R_st{h}", [128, 64], BF) for h in range(4)]
    Yt = sb("R_Yt", [128, 256])
    if d == 1:
        yfl, ysum, yn, yo = F("R_yfl"), F("R_ysum"), F("R_yn"), F("R_yo")
        stats = sb("R_stats", [128, 4, 6]); mv = sb("R_mv", [128, 4, 2])
        sd = sb("R_sd", [128, 4]); rsd = sb("R_rsd", [128, 4]); bs = sb("R_bs", [128, 4])
        ybt = sb("R_ybt", [128, 2, 128], BF)
    bT = ps("R_pT", [128, 512])
    bT2 = ps("R_pT2", [128, 512])
    bZ = ps("R_pZ", [128, 512])
    bC = ps("R_pC", [128, 512])
    bD = ps("R_pD", [128, 512])
    bE = ps("R_pE", [128, 512])
    bF_ = ps("R_pF", [128, 512])
    bG = ps("R_pG", [128, 512])
    sub = lambda par, a, b, nm: S.tile(par[:, a:b], nm, parent=par)
    pF = [sub(bF_, 0, 128, "pF0"), sub(bT, 0, 128, "pF1"), sub(bT2, 0, 128, "pF2"), sub(bC, 0, 128, "pF3")]
    pP = [sub(bG, 0, 64, "pP0"), sub(bE, 0, 64, "pP1")]
    pGm = [sub(bG, 64, 128, "pG0"), sub(bE, 64, 128, "pG1")]
    pS = [sub(bZ, 0, 64, "pS0"), sub(bD, 256, 320, "pS1")]
    pYs = [sub(bZ, 64, 128, "pY0"), sub(bD, 320, 384, "pY1")]
    pz = sub(bF_, 128, 256, "pz")
    ev = {"n": 0}

    act_banks = {id(bT), id(bC), id(bD)}
    hbank = [bF_, bT, bT2, bC]
    _subs = {}

    def sub_cache(par, a_, b_):
        k_ = (id(par), a_, b_)
        if k_ not in _subs:
            _subs[k_] = S.tile(par[:, a_:b_], f"sub{len(_subs)}", parent=par)
        return _subs[k_]

    def evac(out_t, out_ap, in_t, in_ap):
        if id(in_t.root) in act_banks:
            op("act", lambda h: h.activation(out=out_ap, in_=in_ap, func=AF.Copy), R=[in_t], W=[out_t])
        else:
            op("dve", lambda h: h.tensor_copy(out=out_ap, in_=in_ap), R=[in_t], W=[out_t])

    def mm(out_t, out_ap, l_t, l_ap, r_t, r_ap, start=True, stop=True):
        op("pe", lambda h: h.matmul(out_ap, l_ap, r_ap, start=start, stop=stop), R=[l_t, r_t], W=[out_t])

    def tt(e, out_t, out_ap, a_t, a_ap, b_t, b_ap, o):
        op(e, lambda h: h.tensor_tensor(out=out_ap, in0=a_ap, in1=b_ap, op=o), R=[a_t, b_t], W=[out_t])

    def stt(out_t, out_ap, a_t, a_ap, scalar, b_t, b_ap, o0, o1, extra=()):
        op("dve", lambda h: h.scalar_tensor_tensor(out=out_ap, in0=a_ap, scalar=scalar, in1=b_ap, op0=o0, op1=o1), R=[a_t, b_t, *extra], W=[out_t])

    def act(out_t, out_ap, in_t, in_ap, func, scale=1.0, bias=0.0):
        op("act", lambda h: h.activation(out=out_ap, in_=in_ap, func=func, scale=scale, bias=bias), R=[in_t], W=[out_t])

    rk = dr["rkv_s"].ap().rearrange("(k p) t -> p k t", p=128)
    shg = dr["sh_g"]
    seq_starts = {s0 for s0, L in SEQS}
    seq_ends = {s0 + L for s0, L in SEQS}

    def load_block(tb):
        t0 = tb * TB
        u, l = uf[tb % 2], lf[tb % 2]
        lo = 0 if t0 in seq_starts else -1
        hi = 0 if (t0 + TB) in seq_ends else 1
        csl = slice(1 + lo, 513 + hi)
        tsl = slice(t0 + lo, t0 + TB + hi)
        if lo == 0:
            op("pool", lambda h: h.memset(u[:, :, 0:1], 0.0), W=[u])
            op("pool", lambda h: h.memset(l[:, :, 0:1], 0.0), W=[l])
        if hi == 0:
            op("pool", lambda h: h.memset(u[:, :, 513:514], 0.0), W=[u])
            op("pool", lambda h: h.memset(l[:, :, 513:514], 0.0), W=[l])
        S.dma_in(u, lambda h: h.dma_start(out=u[:, :, csl], in_=rk[:, :, tsl]))
        for j, ch in enumerate(LORA_CH):
            r0 = sh_rows(ch)
            S.dma_in(l, lambda h, j=j, r0=r0: h.dma_start(out=l[:, j, csl], in_=shg[r0:r0 + 128, tsl]))

    def shift(src, j, c0t, mut, dst_t, dst_ap, func=None):
        a = tA[j % 2]
        op("pool", lambda h: h.tensor_scalar(out=a[:, :], in0=src[:, j, 1:513], scalar1=c0t[:, j:j + 1], scalar2=None, op0=ALU.mult), R=[src, c0t], W=[a])
        stt(a, a[:, :], src, src[:, j, 0:512], mut[:, j, 0:1], a, a[:, :], ALU.mult, ALU.add, extra=(mut,))
        if func is None:
            stt(dst_t, dst_ap, src, src[:, j, 2:514], mut[:, j, 1:2], a, a[:, :], ALU.mult, ALU.add, extra=(mut,))
        else:
            stt(a, a[:, :], src, src[:, j, 2:514], mut[:, j, 1:2], a, a[:, :], ALU.mult, ALU.add, extra=(mut,))
            act(dst_t, dst_ap, a, a[:, :], func)

    blocks = list(range(NB)) if d == 0 else list(range(NB - 1, -1, -1))
    load_block(blocks[0])
    ui = 0
    for bi, tb in enumerate(blocks[:RNBLK]):
        t0 = tb * TB
        if bi + 1 < NB:
            load_block(blocks[bi + 1])
        u, l = uf[tb % 2], lf[tb % 2]
        for j in range(6):
            shift(u, j, c0r, mur, usr, usr[:, j, :])
        for j in range(6):
            shift(l, j, c0s, mus, lor, lor[:, j, :], func=(AF.Tanh if j < 2 else (None if j < 4 else AF.Sigmoid)))
        if (d == 0 and t0 in seq_starts) or (d == 1 and (t0 + TB) in seq_ends):
            for h_ in range(4):
                op("pool", lambda h, h_=h_: h.memset(st[h_][:, :], 0.0), W=[st[h_]])
        tiles = list(range(4)) if d == 0 else [3, 2, 1, 0]
        if RSTAGE < 2:
            tiles = []
        for tj in tiles:
            ksl = slice(tj * 128, (tj + 1) * 128)
            g0_ = t0 + tj * 128
            for j in range(6):
                bt_ = bT if j < 4 else bT2
                mm(bt_, bt_[:, (j % 4) * 128:(j % 4 + 1) * 128], usr, usr[:, j, ksl], identb, identb[:, :])
            evac(rkvT, rkvT[:, 0:512], bT, bT[:, :])
            evac(rkvT, rkvT[:, 512:768], bT2, bT2[:, 0:256])
            r_ap, k_ap, v_ap = rkvT[:, 0:256], rkvT[:, 256:512], rkvT[:, 512:768]
            tt("dve", kkr, kkr[:, :], rkvT, k_ap, vec, vec[:, KK_, :], ALU.mult)
            act(sqk, sqk[:, :], kkr, kkr[:, :], AF.Square)
            op("dve", lambda h: h.tensor_reduce(out=ssum[:, :], in_=sqk[:, :].rearrange("p (a b) -> p a b", b=64), axis=AX.X, op=ALU.add), R=[sqk], W=[ssum])
            act(nrm, nrm[:, :], ssum, ssum[:, :], AF.Sqrt)
            op("dve", lambda h: h.tensor_scalar(out=nrm[:, :], in0=nrm[:, :], scalar1=1e-12, scalar2=None, op0=ALU.max), W=[nrm])
            op("dve", lambda h: h.reciprocal(out=rn[:, :], in_=nrm[:, :]), R=[nrm], W=[rn])
            for h_ in range(4):
                op("dve", lambda h, h_=h_: h.tensor_scalar(out=kk[:, h_ * 64:(h_ + 1) * 64], in0=kkr[:, h_ * 64:(h_ + 1) * 64], scalar1=rn[:, h_:h_ + 1],
                                                         scalar2=None, op0=ALU.mult), R=[kkr, rn], W=[kk])
            if RSTAGE < 3:
                continue
            mm(bZ, bZ[:, 0:256], lor, lor[:, d, ksl], wl, wl[:, 0, d * 256:(d + 1) * 256])
            mm(bZ, bZ[:, 256:512], lor, lor[:, 2 + d, ksl], wl, wl[:, 0, (2 + d) * 256:(3 + d) * 256])
            tt("dve", zz, zz[:, :], bZ, bZ[:, 0:256], vec, vec[:, W0F + d, :], ALU.add)
            act(sg, sg[:, :], zz, zz[:, :], AF.Sigmoid)
            tt("dve", zz, zz[:, :], bZ, bZ[:, 256:512], vec, vec[:, A0F + d, :], ALU.add)
            act(al, al[:, :], zz, zz[:, :], AF.Sigmoid)
            if d == 1:
                mm(bZ, bZ[:, 0:256], lor, lor[:, 2, ksl], wl, wl[:, 0, 512:768])
                tt("dve", zz, zz[:, :], bZ, bZ[:, 0:256], vec, vec[:, A0F, :], ALU.add)
                act(alo, alo[:, :], zz, zz[:, :], AF.Sigmoid)
            mm(bC, bC[:, 0:256], Mf, Mf[:, MI], sg, sg[:, :])
            mm(bC, bC[:, 256:512], Mf, Mf[:, MS], sg, sg[:, :])
            mm(bD, bD[:, 0:256], Mf, Mf[:, MST], sg, sg[:, :])
            act(Ep, Ep[:, :], bC, bC[:, 0:256], AF.Exp, scale=-CDEC)
            for c in range(2):
                pc = 64 * c
                for h_ in range(4):
                    mm(bD, bD[pc:pc + 64, 256 + 64 * h_:256 + 64 * (h_ + 1)], Ep, Ep[:, h_ * 64:(h_ + 1) * 64], identf, identf[:, pc:pc + 64])
            act(Em, Em[:, :], bC, bC[:, 0:256], AF.Exp, scale=CDEC)
            act(Ea, Ea[:, :], bC, bC[:, 256:512], AF.Exp, scale=-CDEC)
            act(Ed, Ed[:, :], bD, bD[:, 0:256], AF.Exp, scale=-CDEC)
            op("act", lambda h: h.activation(out=gC[:, :], in_=bD[:, 256:512].rearrange("p (a b) -> p a b", b=64)[:, :, TL], func=AF.Copy), R=[bD], W=[gC])
            stt(t1, t1[:, :], al, al[:, :], -1.0, vec, vec[:, KA_, :], ALU.add, ALU.mult)
            stt(kd, kd[:, :], t1, t1[:, :], 1.0, rkvT, k_ap, ALU.add, ALU.mult)
            tt("pool", bd, bd[:, :], kk, kk[:, :], al, al[:, :], ALU.mult)
            tt("pool", tm, tm[:, 0, :], rkvT, r_ap, Ep, Ep[:, :], ALU.mult)
            tt("dve", tm, tm[:, 1, :], kd, kd[:, :], Em, Em[:, :], ALU.mult)
            tt("pool", tm, tm[:, 2, :], bd, bd[:, :], Em, Em[:, :], ALU.mult)
            stt(tm, tm[:, 3, :], kk, kk[:, :], -1.0, Ea, Ea[:, :], ALU.mult, ALU.mult)
            tt("dve", tm, tm[:, 4, :], kd, kd[:, :], Ed, Ed[:, :], ALU.mult)
            tt("pool", tm, tm[:, 5, :], bd, bd[:, :], Ed, Ed[:, :], ALU.mult)
            for hp in range(2):
                bt_ = bT if hp == 0 else bT2
                for s_, xi in enumerate((3, 0, 2, 1)):
                    mm(bt_, bt_[:, s_ * 128:(s_ + 1) * 128], tm, tm[:, xi, hp * 128:(hp + 1) * 128], identb, identb[:, :])
            evac(cm, cm[:, 0, :], bT, bT[:, :])
            evac(cm, cm[:, 1, :], bT2, bT2[:, :])
            def head_gen(h_):
                hp, hb = h_ // 2, 64 * (h_ % 2)
                hs = slice(h_ * 64, (h_ + 1) * 64)
                ab, ak, ntt = AB[h_], AK[h_], NTt[h_]
                bank = hbank[h_]
                sA = sub_cache(bank, 0, 128)
                sB = sub_cache(bank, 128, 256)
                pzh = sub_cache(bank, 256, 384)
                arT = cm[hb:hb + 64, hp, 0:256]
                bTa = cm[hb:hb + 64, hp, 256:384]
                kTa = cm[hb:hb + 64, hp, 384:512]
                aTa = cm[hb:hb + 64, hp, 0:128]
                mm(bE, bE[:, 0:256], cm, bTa, cm, arT)
                tt("dve", ab, ab[:, :], bE, bE[:, 0:256], Mf, Mf[:, MSI], ALU.mult)
                mm(bE, bE[:, 256:512], cm, kTa, cm, arT)
                tt("dve", ak, ak[:, :], bE, bE[:, 256:512], Mf, Mf[:, MSI], ALU.mult)
                yield
                mm(bG, bG[:, 256:384], cm, aTa, cm, bTa)
                tt("dve", ntt, ntt[:, :], bG, bG[:, 256:384], Mf, Mf[:, MST], ALU.mult)
                yield
                sAB = sub_cache(bank, 0, 256)
                xx = XX[h_]
                X, XT = (ab, ab[:, 0:128]), (ntt, ntt[:, :])
                Xl = [X]
                for i in range(5):
                    mm(sA, sA[:, :], XT[0], XT[1], X[0], X[1])
                    if i < 4:
                        mm(sB, sB[:, :], X[0], X[1], XT[0], XT[1])
                        evac(xx[i], xx[i][:, :], sAB, sAB[:, :])
                        XT = (xx[i], xx[i][:, 128:256])
                    else:
                        evac(xx[i], xx[i][:, 0:128], sA, sA[:, :])
                    X = (xx[i], xx[i][:, 0:128])
                    Xl.append(X)
                    yield
                zt = Zt[h_]
                z = zt[0]
                op("pool", lambda h, z=z, hs=hs: h.tensor_copy(out=z[:, 0:64], in_=tm[:, 3, hs]), R=[tm], W=[z])
                vh = rkvT[:, 512 + h_ * 64:512 + (h_ + 1) * 64]
                mm(pzh, pzh[:, 0:64], ak, ak[:, 0:128], rkvT, vh)
                evac(z, z[:, 64:128], pzh, pzh[:, 0:64])
                yield
                for i in range(6):
                    zn = zt[(i + 1) % 2]
                    mm(sA, sA[:, :], Xl[i][0], Xl[i][1], z, z[:, :])
                    tt("dve", zn, zn[:, :], sA, sA[:, :], z, z[:, :], ALU.add)
                    z = zn
                    yield
                for c in (((0, 1) if d == 0 else (1, 0)) if RSTAGE >= 5 else ()):
                    pc, po = 64 * c, 64 * (1 - c)
                    cs_ = slice(pc, pc + 64)
                    os_ = slice(po, po + 64)
                    Bh = tm[cs_, 5, hs]
                    Kh = tm[cs_, 4, hs]
                    Rt = tm[cs_, 0, hs]
                    pm, gt_ = Pm[h_][c], Gt[h_][c]
                    mm(pP[c], pP[c][cs_, :], z, z[cs_, 0:64], tm, Bh)
                    stt(pm, pm[cs_, :], identf, identf[cs_, pc:pc + 64], gC[cs_, h_:h_ + 1], pP[c], pP[c][cs_, :], ALU.mult, ALU.add, extra=(gC,))
                    mm(pGm[c], pGm[c][cs_, :], z, z[cs_, 0:64], ab, ab[cs_, 128 + pc:128 + pc + 64], start=True, stop=False)
                    mm(pGm[c], pGm[c][cs_, :], tm, Rt, identb, identb[cs_, pc:pc + 64], start=False, stop=True)
                    evac(gt_, gt_[cs_, :], pGm[c], pGm[c][cs_, :])
                    yield
                    pY = pYs[c]
                    mm(pY, pY[cs_, :], gt_, gt_[cs_, :], st[h_], st[h_][cs_, :], start=True, stop=False)
                    mm(pY, pY[cs_, :], ab, ab[cs_, 128 + pc:128 + pc + 64], z, z[cs_, 64:128], start=False, stop=False)
                    mm(pY, pY[cs_, :], ak, ak[cs_, 128 + pc:128 + pc + 64], rkvT, rkvT[cs_, 512 + h_ * 64:512 + (h_ + 1) * 64], start=False, stop=True)
                    evac(Yt, Yt[cs_, hs], pY, pY[cs_, :])
                    mm(pS[c], pS[c][os_, :], tm, Bh, z, z[cs_, 64:128], start=True, stop=False)
                    mm(pS[c], pS[c][os_, :], tm, Kh, rkvT, rkvT[cs_, 512 + h_ * 64:512 + (h_ + 1) * 64], start=False, stop=False)
                    mm(pS[c], pS[c][os_, :], pm, pm[cs_, :], st[h_], st[h_][cs_, :], start=False, stop=True)
                    evac(st[h_], st[h_][os_, :], pS[c], pS[c][os_, :])
                    yield

            gens = [head_gen(h_) for h_ in range(4 if RSTAGE >= 4 else 0)]
            while gens:
                for g_ in gens[:]:
                    try:
                        next(g_)
                    except StopIteration:
                        gens.remove(g_)
            tok = slice(g0_, g0_ + 128)
            if RSTAGE < 6:
                continue
            if d == 0:
                S.dma_out(Yt, lambda h, tok=tok: h.dma_start(out=dr["yf_s"][tok, :], in_=Yt[:, :]))
            else:
                S.dma_in(yfl, lambda h, tok=tok: h.dma_start(out=yfl[:, :], in_=dr["yf_s"][tok, :]))
                tt("dve", ysum, ysum[:, :], Yt, Yt[:, :], yfl, yfl[:, :], ALU.add)
                for h_ in range(4):
                    hs = slice(h_ * 64, (h_ + 1) * 64)
                    op("dve", lambda h, h_=h_, hs=hs: h.bn_stats(out=stats[:, h_, :], in_=ysum[:, hs]), R=[ysum], W=[stats])
                    op("dve", lambda h, h_=h_: h.bn_aggr(out=mv[:, h_, :], in_=stats[:, h_, :]), R=[stats], W=[mv])
                act(sd, sd[:, :], mv, mv[:, :, 1], AF.Sqrt, bias=GN_EPS)
                op("dve", lambda h: h.reciprocal(out=rsd[:, :], in_=sd[:, :]), R=[sd], W=[rsd])
                for h_ in range(4):
                    hs = slice(h_ * 64, (h_ + 1) * 64)
                    op("dve", lambda h, h_=h_, hs=hs: h.tensor_scalar(out=yn[:, hs], in0=ysum[:, hs], scalar1=mv[:, h_, 0:1], scalar2=rsd[:, h_:h_ + 1],
                                                                    op0=ALU.subtract, op1=ALU.mult), R=[ysum, mv, rsd], W=[yn])
                tt("pool", yn, yn[:, :], yn, yn[:, :], vec, vec[:, LNW, :], ALU.mult)
                tt("pool", yn, yn[:, :], yn, yn[:, :], vec, vec[:, LNB, :], ALU.add)
                tt("dve", t1, t1[:, :], al, al[:, :], alo, alo[:, :], ALU.add)
                stt(t1, t1[:, :], t1, t1[:, :], -2.0, vec, vec[:, KA_, :], ALU.add, ALU.mult)
                stt(kd, kd[:, :], t1, t1[:, :], 2.0, rkvT, k_ap, ALU.add, ALU.mult)
                tt("dve", kd, kd[:, :], kd, kd[:, :], rkvT, r_ap, ALU.mult)
                tt("dve", kd, kd[:, :], kd, kd[:, :], vec, vec[:, RK_, :], ALU.mult)
                op("dve", lambda h: h.tensor_reduce(out=bs[:, :], in_=kd[:, :].rearrange("p (a b) -> p a b", b=64), axis=AX.X, op=ALU.add), R=[kd], W=[bs])
                for h_ in range(4):
                    hs = slice(h_ * 64, (h_ + 1) * 64)
                    stt(yn, yn[:, hs], rkvT, rkvT[:, 512 + h_ * 64:512 + (h_ + 1) * 64], bs[:, h_:h_ + 1], yn, yn[:, hs], ALU.mult, ALU.add, extra=(bs,))
                mm(bZ, bZ[:, 0:256], lor, lor[:, 4, ksl], wg2, wg2[:, 0, :], start=True, stop=False)
                mm(bZ, bZ[:, 0:256], lor, lor[:, 5, ksl], wg2, wg2[:, 1, :], start=False, stop=True)
                tt("dve", yo, yo[:, :], yn, yn[:, :], bZ, bZ[:, 0:256], ALU.mult)
                for cc in range(2):
                    pq = pF[cc]
                    op("pe", lambda h, cc=cc, pq=pq: h.transpose(pq[:, :], yo[:, cc * 128:(cc + 1) * 128], identf[:, :]), R=[yo, identf], W=[pq])
                    evac(ybt, ybt[:, cc, :], pq, pq[:, :])
                S.dma_out(ybt, lambda h, tok=tok: h.dma_start(out=dr["y_b"].ap()[256:512, tok].rearrange("(c p) t -> p c t", p=128), in_=ybt[:, :, :]))
    ph.end()


DBG = []
PHASES = "MRCDEGHI"


def build_nc(with_rwkv=True):
    nc = bass.Bass("TRN2", target_bir_lowering=False)
    dr = {}

    def ein(name, shape, dt=F32):
        dr[name] = nc.dram_tensor(name, shape, dt, kind="ExternalInput")

    def scr(name, shape, dt):
        dr[name] = nc.dram_tensor(name, shape, dt)
    ein("xT", [D, NT]); ein("xTs", [256, NT]); ein("w_inA", [D, 13 * 128]); ein("g0", [128, KC])
    ein("w_uq", [768, 512]); ein("w_ukv", [512, 512]); ein("qng", [128, 6]); ein("kvg", [128, 4])
    ein("ident", [128, 128], F32); ein("cos2", [64, 8192]); ein("sin2", [64, 8192])
    ein("w_br", [4096, 256]); ein("w_o", [D, 256]); ein("w_up", [D, 1024]); ein("w_dn", [8192, 256]); ein("gn", [128, 6])
    for k, shp in RW_INPUTS.items():
        ein(k, shp)
    dr["yT"] = nc.dram_tensor("yT", [256, NT], F32, kind="ExternalOutput")
    scr("sh_b", [384, NT], BF); scr("sh_g", [8 * 384, NT], BF)
    scr("rkv_s", [768, NT], BF); scr("gate_s", [512, NT], BF)
    scr("y_b", [512, NT], BF); scr("y_g", [8 * 512, NT], BF)
    scr("mix_b", [256, NT], BF); scr("mix_g", [D, NT], BF)
    scr("mix_s", [256, NT], F32); scr("x1_s", [256, NT], F32); scr("ff_s", [256, NT], F32)
    scr("h2_b", [256, NT], BF); scr("h2_g", [D, NT], BF)
    scr("hid_b", [1024, NT], BF); scr("hid_g", [8192, NT], BF)
    scr("yf_s", [NT, 256], F32)
    for i in (1, 2, 3):
        scr(f"ss{i}_b", [1, NT], F32); scr(f"ss{i}_g", [8, NT], F32)
    with contextlib.ExitStack() as gs:
        S = Sched(nc, gs)
        phase_A(nc, S, dr)
        if "M" in PHASES:
            phase_M(nc, S, dr)
        if "R" in PHASES:
            phase_R(nc, S, dr)
        ph = Ph(nc, S)
        nop = ph.sb("X_nop", [128, 8])
        S.op("pool", lambda h: h.memset(nop[:, :], 0.0), W=[nop])
        ph.end(allgather(nc, dr["y_b"], dr["y_g"]))
        for nm, fn in (("C", phase_C), ("D", phase_D), ("E", phase_E), ("G", phase_G), ("H", phase_H), ("I", phase_I)):
            if nm in PHASES:
                fn(nc, S, dr)
        if DBG:
            ph = Ph(nc, S)
            dt_ = ph.sb("dbg_t", [128, 8])
            for nm in DBG:
                src = dr[nm]
                dst = nc.dram_tensor("dbg_" + nm, list(src.shape), src.dtype, kind="ExternalOutput")
                for r0 in range(0, src.shape[0], 128):
                    r1 = min(r0 + 128, src.shape[0])
                    for c0 in range(0, src.shape[1], 4096):
                        c1 = min(c0 + 4096, src.shape[1])
                        S.dma_out(dt_, lambda h, src=src, dst=dst, r0=r0, r1=r1, c0=c0, c1=c1: h.dma_start(out=dst[r0:r1, c0:c1], in_=src[r0:r1, c0:c1]))
            ph.end()
    return nc


def _cols(a, n=128):
    return np.ascontiguousarray(a.reshape(-1, 128).T.astype(np.float32))


def kernel(**inp):
    f = lambda k: np.asarray(inp[k], dtype=np.float32)
    x = np.concatenate([f("x_prompt").reshape(-1, D), f("x_sample").reshape(-1, D)], 0)
    xT = np.ascontiguousarray(x.T)
    w_in = f("w_in")[0]
    pad = lambda cols, n=128: list(cols) + [-1] * (n - len(cols))
    sh = [list(range(i * 128, (i + 1) * 128)) for i in range(10)]
    sh.append(pad(range(1280, 1344)))
    sh += [pad(range(U0 + 6144 + 96 * j, U0 + 6144 + 96 * (j + 1))) for j in range(4)]
    sh += [list(range(U0 + 6528, U0 + 6656)), list(range(U0 + 6656, U0 + 6784))]
    sh.append(pad(list(range(1312, 1344)) + list(range(1280, 1312))))
    w_pad = np.concatenate([w_in, np.zeros((D, 1), np.float32)], 1)
    inv = 1.0 / (10000.0 ** (np.arange(0, 64, 2, dtype=np.float32) / 64))
    ang = np.arange(8192, dtype=np.float32)[:, None] * inv[None, :]
    cos, sin = np.cos(ang).T.astype(np.float32), np.sin(ang).T.astype(np.float32)
    cos2 = np.ascontiguousarray(np.concatenate([cos, cos], 0))
    sin2 = np.ascontiguousarray(np.concatenate([-sin, sin], 0))
    import ml_dtypes
    ident = np.eye(128, dtype=np.float32)
    w_uq, w_ukv = f("mla_w_uq")[0], f("mla_w_ukv")[0]
    w_br, w_o, w_up, w_dn = f("w_branch")[0], f("w_out")[0], f("w_mlp_up")[0], f("w_mlp_down")[0]
    in_maps = []
    for c in range(NCORES):
        m = {"xT": xT, "xTs": np.ascontiguousarray(xT[256 * c:256 * (c + 1)])}
        cols = []
        for slot in range(3):
            i = slot * 8 + c
            cols += sh[i] if i < NSH else [-1] * 128
        for base in (0, 2048, 4096):
            cols += list(range(U0 + base + 256 * c, U0 + base + 256 * (c + 1)))
        cols += list(range(G0 + 256 * c, G0 + 256 * (c + 1))) + list(range(G0 + 2048 + 256 * c, G0 + 2048 + 256 * (c + 1)))
        m["w_inA"] = np.ascontiguousarray(w_pad[:, cols])
        m["g0"] = _cols(f("norm_pre_mix")[0])
        qc = []
        for hd in (2 * c, 2 * c + 1):
            b = hd * 192
            qc += list(range(b, b + 192)) + list(range(b + 160, b + 192)) + list(range(b + 128, b + 160))
        m["w_uq"] = np.ascontiguousarray(w_uq[:, qc])
        m["w_ukv"] = np.ascontiguousarray(w_ukv[:, 512 * c:512 * (c + 1)])
        m["qng"] = _cols(f("mla_q_norm")[0]); m["kvg"] = _cols(f("mla_kv_norm")[0])
        m["ident"] = ident; m["cos2"] = cos2; m["sin2"] = sin2
        rows = []
        for r in range(8):
            rows += list(range(256 * r, 256 * (r + 1))) + list(range(2048 + 256 * r, 2048 + 256 * (r + 1)))
        m["w_br"] = np.ascontiguousarray(w_br[rows][:, 256 * c:256 * (c + 1)])
        m["w_o"] = np.ascontiguousarray(w_o[:, 256 * c:256 * (c + 1)])
        m["w_up"] = np.ascontiguousarray(w_up[:, 1024 * c:1024 * (c + 1)])
        m["w_dn"] = np.ascontiguousarray(w_dn[:, 256 * c:256 * (c + 1)])
        sl = slice(256 * c, 256 * (c + 1))
        m["gn"] = np.ascontiguousarray(np.concatenate([_cols(f("norm_post_mix")[0][sl]), _cols(f("norm_pre_mlp")[0][sl]),
                                                       _cols(f("norm_post_mlp")[0][sl])], 1))
        m.update(rw_host(inp, c))
        in_maps.append(m)
    nc = build_nc()
    res = run_bass_kernel_spmd(nc, in_maps, core_ids=list(range(NCORES)))
    global LAST
    LAST = res
    yT = np.concatenate([res.results[c]["yT"] for c in range(NCORES)], 0)
    y = np.ascontiguousarray(yT.T).astype(np.float32)
    return (y[:8192].reshape(1, 8192, D), y[8192:].reshape(4, 2048, D))
```
